# Optimizing a Trainium2 kernel written in Bass

```python
import math
import jax, jax.numpy as jnp
from jax import lax
import numpy as np

D_MODEL = 1024
BATCH = 2
SEQ = 16384
DEPTH = 2

GRID_W = 64
CTX_LEN = 256
HEAD_DIM = 64
BLOCK = 128
WINDOW = 128
ROPE_BASE = 10000.0
EPS = 1e-6
NEG = -1e30
GROUP_W = D_MODEL // 4
A_HEADS = GROUP_W // HEAD_DIM
A_QK_DIM = HEAD_DIM // 2
A_V_DIM = HEAD_DIM
A_W = A_HEADS * A_V_DIM
B_CH = GROUP_W
CONV_W = 3
W_Q_HEADS = GROUP_W // HEAD_DIM
W_KV_HEADS = W_Q_HEADS // 2
W_GROUP = W_Q_HEADS // W_KV_HEADS
W_W = W_Q_HEADS * HEAD_DIM
G_Q_HEADS = GROUP_W // HEAD_DIM
G_KV_HEADS = G_Q_HEADS // 2
G_GROUP = G_Q_HEADS // G_KV_HEADS
G_W = G_Q_HEADS * HEAD_DIM
MIX_W = A_W + B_CH + W_W + G_W
SPLIT_SIZES = (A_HEADS * 2 * A_QK_DIM, A_HEADS * 2 * A_QK_DIM, A_HEADS * A_V_DIM,
               B_CH, B_CH, B_CH,
               W_Q_HEADS * HEAD_DIM, W_KV_HEADS * HEAD_DIM, W_KV_HEADS * HEAD_DIM,
               G_Q_HEADS * HEAD_DIM, G_KV_HEADS * HEAD_DIM, G_KV_HEADS * HEAD_DIM)
IN_W = sum(SPLIT_SIZES)
D_FF = 256 * ((8 * D_MODEL // 3 + 255) // 256)
N_MOD = 9

kernel_name = "hybrid_parallel_group_dit_block"


def _layer_norm(x, g, b):
    xf = x.astype(jnp.float32)
    mu = jnp.mean(xf, axis=-1, keepdims=True)
    var = jnp.mean(jnp.square(xf - mu), axis=-1, keepdims=True)
    return ((xf - mu) * lax.rsqrt(var + EPS) * g + b).astype(x.dtype)


def _rms_norm(x, g):
    xf = x.astype(jnp.float32)
    return (xf * lax.rsqrt(jnp.mean(xf * xf, axis=-1, keepdims=True) + EPS) * g).astype(x.dtype)


def _modulate(s, shift, scale):
    return s * (1.0 + scale) + shift


def _swiglu(h, w_gu, w_dn):
    g, u = jnp.split(h @ w_gu, 2, axis=-1)
    return (jax.nn.silu(g) * u) @ w_dn


def _ffn_step(s, shift, scale, gate, w_gu, w_dn, g, b, alpha):
    h = _modulate(s, shift, scale)
    return _layer_norm(alpha * s + 0.5 * gate * _swiglu(h, w_gu, w_dn), g, b)


def _rope_tables(rows, dim):
    n_freq = dim // 4
    inv_freq = ROPE_BASE ** (-jnp.arange(n_freq, dtype=jnp.float32) / n_freq)
    ang_r = jnp.arange(rows, dtype=jnp.float32)[:, None] * inv_freq
    ang_c = jnp.arange(GRID_W, dtype=jnp.float32)[:, None] * inv_freq
    ang_r = jnp.broadcast_to(ang_r[:, None, :], (rows, GRID_W, n_freq))
    ang_c = jnp.broadcast_to(ang_c[None, :, :], (rows, GRID_W, n_freq))
    ang = jnp.concatenate([ang_r, ang_r, ang_c, ang_c], axis=-1).reshape(rows * GRID_W, dim)
    return jnp.cos(ang), jnp.sin(ang)


def _apply_rope(x, cos, sin):
    shape = (cos.shape[0],) + (1,) * (x.ndim - 3) + (cos.shape[1],)
    a, b, c, d = jnp.split(x, 4, axis=-1)
    rot = jnp.concatenate([-b, a, -d, c], axis=-1)
    return (x * cos.reshape(shape) + rot * sin.reshape(shape)).astype(x.dtype)


def _split_cols(z):
    idx = np.cumsum(SPLIT_SIZES)[:-1].tolist()
    return jnp.split(z, idx, axis=-1)


def _gqa_softmax(q, k, v, sink=None):
    logits = jnp.einsum('bqhgd,bkhd->bhgqk', q, k).astype(jnp.float32) * (HEAD_DIM ** -0.5)
    if sink is None:
        p = jax.nn.softmax(logits, axis=-1)
    else:
        s = jnp.broadcast_to(sink.astype(jnp.float32).reshape((1,) + q.shape[2:4] + (1, 1)),
                             logits.shape[:-1] + (1,))
        p = jax.nn.softmax(jnp.concatenate([logits, s], axis=-1), axis=-1)[..., :-1]
    return jnp.einsum('bhgqk,bkhd->bqhgd', p.astype(v.dtype), v)


def _diff_softmax(q, k, v, lam):
    logits = jnp.einsum('bqhmd,bkhmd->bhmqk', q, k).astype(jnp.float32) * (A_QK_DIM ** -0.5)
    p = jax.nn.softmax(logits, axis=-1)
    w = p[:, :, 0] - lam * p[:, :, 1]
    return jnp.einsum('bhqk,bkhd->bqhd', w.astype(v.dtype), v)


def _diff_attention(zx, zc, lam, lambda_init, subln_w, rope, need_ctx):
    aq, ak, av = zx
    cq, ck, cv = zc
    B, S = aq.shape[:2]
    L = cq.shape[1]
    nb = S // BLOCK
    q = _apply_rope(aq.reshape(B, S, A_HEADS, 2, A_QK_DIM), *rope)
    k = _apply_rope(ak.reshape(B, S, A_HEADS, 2, A_QK_DIM), *rope)
    v = av.reshape(B, S, A_HEADS, A_V_DIM)
    qc = cq.reshape(B, L, A_HEADS, 2, A_QK_DIM)
    kc = ck.reshape(B, L, A_HEADS, 2, A_QK_DIM)
    vc = cv.reshape(B, L, A_HEADS, A_V_DIM)
    k_all = jnp.concatenate([kc, k], axis=1)
    v_all = jnp.concatenate([vc, v], axis=1)
    qb = jnp.moveaxis(q.reshape(B, nb, BLOCK, A_HEADS, 2, A_QK_DIM), 1, 0)
    o = lax.map(lambda qi: _diff_softmax(qi, k_all, v_all, lam), qb)
    o = jnp.moveaxis(o, 0, 1).reshape(B, S, A_HEADS, A_V_DIM)
    y_lat = (_rms_norm(o, subln_w) * (1.0 - lambda_init)).reshape(B, S, A_W)
    y_ctx = None
    if need_ctx:
        oc = _diff_softmax(qc, kc, vc, lam)
        y_ctx = (_rms_norm(oc, subln_w) * (1.0 - lambda_init)).reshape(B, L, A_W)
    return y_lat, y_ctx


def _short_conv(gb, gc, u, conv_w):
    h = gc * u
    y = lax.conv_general_dilated(h, conv_w[:, None, :].astype(h.dtype), window_strides=(1,),
                                 padding=[(CONV_W // 2, CONV_W // 2)],
                                 dimension_numbers=('NWC', 'WIO', 'NWC'),
                                 feature_group_count=B_CH)
    return gb * y


def _banded_sink_attention(q, k, v, kc, vc, sink):
    B, S = q.shape[:2]
    L = kc.shape[1]
    nb = S // BLOCK
    qb = q.reshape(B, nb, BLOCK, W_KV_HEADS, W_GROUP, HEAD_DIM)
    pad = ((0, 0), (BLOCK, BLOCK), (0, 0), (0, 0))
    kp = jnp.pad(k, pad).reshape(B, nb + 2, BLOCK, W_KV_HEADS, HEAD_DIM)
    vp = jnp.pad(v, pad).reshape(B, nb + 2, BLOCK, W_KV_HEADS, HEAD_DIM)
    kw = jnp.concatenate([kp[:, :-2], kp[:, 1:-1], kp[:, 2:]], axis=2)
    vw = jnp.concatenate([vp[:, :-2], vp[:, 1:-1], vp[:, 2:]], axis=2)
    kpos = jnp.arange(3 * BLOCK) - BLOCK
    qpos = jnp.arange(BLOCK)
    rel = kpos[None, :] - qpos[:, None]
    absk = jnp.arange(nb)[:, None, None] * BLOCK + kpos[None, None, :]
    valid = (jnp.abs(rel) <= WINDOW)[None] & (absk >= 0) & (absk < S)
    scale = HEAD_DIM ** -0.5
    lw = jnp.einsum('bnqhgd,bnkhd->bhgnqk', qb, kw).astype(jnp.float32) * scale
    lw = jnp.where(valid, lw, NEG)
    lc = jnp.einsum('bnqhgd,bchd->bhgnqc', qb, kc).astype(jnp.float32) * scale
    ls = jnp.broadcast_to(sink.astype(jnp.float32).reshape(1, W_KV_HEADS, W_GROUP, 1, 1, 1),
                          lw.shape[:-1] + (1,))
    p = jax.nn.softmax(jnp.concatenate([lw, lc, ls], axis=-1), axis=-1)
    pw = p[..., :3 * BLOCK].astype(v.dtype)
    pc = p[..., 3 * BLOCK:3 * BLOCK + L].astype(v.dtype)
    o = (jnp.einsum('bhgnqk,bnkhd->bnqhgd', pw, vw)
         + jnp.einsum('bhgnqc,bchd->bnqhgd', pc, vc))
    return o.reshape(B, S, W_W)


def _window_attention(zx, zc, sink, rope, need_ctx):
    q, k, v = zx
    qc, kc, vc = zc
    B, S = q.shape[:2]
    L = qc.shape[1]
    q = _apply_rope(q.reshape(B, S, W_KV_HEADS, W_GROUP, HEAD_DIM), *rope)
    k = _apply_rope(k.reshape(B, S, W_KV_HEADS, HEAD_DIM), *rope)
    v = v.reshape(B, S, W_KV_HEADS, HEAD_DIM)
    qc = qc.reshape(B, L, W_KV_HEADS, W_GROUP, HEAD_DIM)
    kc = kc.reshape(B, L, W_KV_HEADS, HEAD_DIM)
    vc = vc.reshape(B, L, W_KV_HEADS, HEAD_DIM)
    y_lat = _banded_sink_attention(q, k, v, kc, vc, sink)
    y_ctx = _gqa_softmax(qc, kc, vc, sink).reshape(B, L, W_W) if need_ctx else None
    return y_lat, y_ctx


def _global_attention(zx, zc, qn_w, kn_w, rope, need_ctx):
    q, k, v = zx
    qc, kc, vc = zc
    B, S = q.shape[:2]
    L = qc.shape[1]
    nb = S // BLOCK
    q = _apply_rope(_rms_norm(q.reshape(B, S, G_KV_HEADS, G_GROUP, HEAD_DIM), qn_w), *rope)
    k = _apply_rope(_rms_norm(k.reshape(B, S, G_KV_HEADS, HEAD_DIM), kn_w), *rope)
    v = v.reshape(B, S, G_KV_HEADS, HEAD_DIM)
    qc = _rms_norm(qc.reshape(B, L, G_KV_HEADS, G_GROUP, HEAD_DIM), qn_w)
    kc = _rms_norm(kc.reshape(B, L, G_KV_HEADS, HEAD_DIM), kn_w)
    vc = vc.reshape(B, L, G_KV_HEADS, HEAD_DIM)
    k_all = jnp.concatenate([kc, k], axis=1)
    v_all = jnp.concatenate([vc, v], axis=1)
    qb = jnp.moveaxis(q.reshape(B, nb, BLOCK, G_KV_HEADS, G_GROUP, HEAD_DIM), 1, 0)
    o = lax.map(lambda qi: _gqa_softmax(qi, k_all, v_all), qb)
    y_lat = jnp.moveaxis(o, 0, 1).reshape(B, S, G_W)
    y_ctx = _gqa_softmax(qc, kc, vc).reshape(B, L, G_W) if need_ctx else None
    return y_lat, y_ctx


def setup_inputs(seed: int = 0) -> dict:
    key = jax.random.key(seed)
    ks = jax.random.split(key, 24)
    beta = (8.0 * DEPTH) ** -0.25

    def nrm(k, shape, s):
        return jax.random.normal(k, shape, jnp.float32) * s

    return {
        "x": nrm(ks[0], (BATCH, SEQ, D_MODEL), 1.0),
        "c": nrm(ks[1], (BATCH, D_MODEL), 1.0),
        "ctx": nrm(ks[2], (BATCH, CTX_LEN, D_MODEL), 1.0),
        "c_ctx": nrm(ks[3], (D_MODEL,), 1.0),
        "w_mod": nrm(ks[4], (DEPTH, D_MODEL, N_MOD * D_MODEL), 0.5 * D_MODEL ** -0.5),
        "b_mod": nrm(ks[5], (DEPTH, N_MOD * D_MODEL), 0.02),
        "ln_g": 1.0 + nrm(ks[6], (DEPTH, 3, D_MODEL), 0.02),
        "ln_b": nrm(ks[7], (DEPTH, 3, D_MODEL), 0.02),
        "w_gu1": nrm(ks[8], (DEPTH, D_MODEL, 2 * D_FF), D_MODEL ** -0.5),
        "w_dn1": nrm(ks[9], (DEPTH, D_FF, D_MODEL), beta * D_FF ** -0.5),
        "w_in": nrm(ks[10], (DEPTH, D_MODEL, IN_W), D_MODEL ** -0.5),
        "w_out": nrm(ks[11], (DEPTH, MIX_W, D_MODEL), beta * MIX_W ** -0.5),
        "conv_w": nrm(ks[12], (DEPTH, CONV_W, B_CH), CONV_W ** -0.5),
        "lam_q1": nrm(ks[13], (DEPTH, A_QK_DIM), 0.1),
        "lam_k1": nrm(ks[14], (DEPTH, A_QK_DIM), 0.1),
        "lam_q2": nrm(ks[15], (DEPTH, A_QK_DIM), 0.1),
        "lam_k2": nrm(ks[16], (DEPTH, A_QK_DIM), 0.1),
        "subln_w": 1.0 + nrm(ks[17], (DEPTH, A_V_DIM), 0.02),
        "sink": nrm(ks[18], (DEPTH, W_Q_HEADS), 0.5),
        "qn_w": 1.0 + nrm(ks[19], (DEPTH, HEAD_DIM), 0.02),
        "kn_w": 1.0 + nrm(ks[20], (DEPTH, HEAD_DIM), 0.02),
        "w_gu2": nrm(ks[21], (DEPTH, D_MODEL, 2 * D_FF), D_MODEL ** -0.5),
        "w_dn2": nrm(ks[22], (DEPTH, D_FF, D_MODEL), beta * D_FF ** -0.5),
    }


def reference(x, c, ctx, c_ctx, w_mod, b_mod, ln_g, ln_b, w_gu1, w_dn1, w_in, w_out, conv_w,
              lam_q1, lam_k1, lam_q2, lam_k2, subln_w, sink, qn_w, kn_w, w_gu2, w_dn2):
    B, S, _ = x.shape
    rows = S // GRID_W
    rope_a = _rope_tables(rows, A_QK_DIM)
    rope_h = _rope_tables(rows, HEAD_DIM)
    alpha = (2.0 * DEPTH) ** 0.25
    xc = ctx
    for l in range(DEPTH):
        need_ctx = l < DEPTH - 1
        lambda_init = 0.8 - 0.6 * math.exp(-0.3 * l)
        mx = jnp.split((jax.nn.silu(c) @ w_mod[l] + b_mod[l])[:, None, :], N_MOD, axis=-1)
        mc = jnp.split((jax.nn.silu(c_ctx) @ w_mod[l] + b_mod[l])[None, None, :], N_MOD, axis=-1)

        x = _ffn_step(x, mx[0], mx[1], mx[2], w_gu1[l], w_dn1[l], ln_g[l, 0], ln_b[l, 0], alpha)
        xc = _ffn_step(xc, mc[0], mc[1], mc[2], w_gu1[l], w_dn1[l], ln_g[l, 0], ln_b[l, 0], alpha)

        zx = _split_cols(_modulate(x, mx[3], mx[4]) @ w_in[l])
        zc = _split_cols(_modulate(xc, mc[3], mc[4]) @ w_in[l])
        lam = (jnp.exp(jnp.sum(lam_q1[l] * lam_k1[l])) - jnp.exp(jnp.sum(lam_q2[l] * lam_k2[l]))
               + lambda_init).astype(jnp.float32)
        ya_x, ya_c = _diff_attention(zx[0:3], zc[0:3], lam, lambda_init, subln_w[l], rope_a, need_ctx)
        yb_x = _short_conv(zx[3], zx[4], zx[5], conv_w[l])
        yw_x, yw_c = _window_attention(zx[6:9], zc[6:9], sink[l], rope_h, need_ctx)
        yg_x, yg_c = _global_attention(zx[9:12], zc[9:12], qn_w[l], kn_w[l], rope_h, need_ctx)
        y_lat = jnp.concatenate([ya_x, yb_x, yw_x, yg_x], axis=-1) @ w_out[l]
        x = _layer_norm(alpha * x + mx[5] * y_lat, ln_g[l, 1], ln_b[l, 1])

        x = _ffn_step(x, mx[6], mx[7], mx[8], w_gu2[l], w_dn2[l], ln_g[l, 2], ln_b[l, 2], alpha)

        if need_ctx:
            yb_c = _short_conv(zc[3], zc[4], zc[5], conv_w[l])
            y_ctx = jnp.concatenate([ya_c, yb_c, yw_c, yg_c], axis=-1) @ w_out[l]
            xc = _layer_norm(alpha * xc + mc[5] * y_ctx, ln_g[l, 1], ln_b[l, 1])
            xc = _ffn_step(xc, mc[6], mc[7], mc[8], w_gu2[l], w_dn2[l], ln_g[l, 2], ln_b[l, 2], alpha)
    return x
```

```python
import contextlib
import math
import numpy as np
import ml_dtypes
import concourse.bass as bass
import concourse.mybir as mybir
from concourse.bass_utils import run_bass_kernel_spmd

F32 = mybir.dt.float32
BF16 = mybir.dt.bfloat16
AF = mybir.ActivationFunctionType
ALU = mybir.AluOpType

D = 1024
DFF = 2816
NFF = 22
LCTX = 256
EPS = 1e-6
GRID_W = 64
ROPE_BASE = 10000.0
NWC = 26
LP = 263
LC = 152
SAME_ENGINE_SYNC = True
DEBUG_WHERE = None


class Op:
    __slots__ = ("eng", "fn", "deps", "sig", "isdma", "sem", "val", "id", "where")


def _region(x):
    if isinstance(x, tuple):
        return ("K", x), (0, 1, 0, 1)
    t = x.tensor
    pairs = list(x.ap)
    tn = type(t).__name__
    if tn.startswith("DRam"):
        lo = x.offset
        hi = lo + sum((c - 1) * abs(s) for s, c in pairs) + 1
        return ("D", t.name), (0, 1, lo, hi)
    shape = list(t.shape)
    row = 1
    for s in shape[1:]:
        row *= s
    p0 = x.offset // row
    f0 = x.offset % row
    npart = pairs[0][1]
    f1 = f0 + sum((c - 1) * abs(s) for s, c in pairs[1:]) + 1
    return ("S", t.name), (p0, p0 + npart, f0, f1)


class Prog:
    NDMASEM = 10

    def __init__(self):
        self.ops = []
        self.hist = {}

    def add(self, eng, fn, reads=(), writes=(), isdma=False):
        op = Op()
        op.eng = eng
        op.fn = fn
        op.isdma = isdma
        op.sig = False
        op.id = len(self.ops)
        if DEBUG_WHERE:
            import sys as _s
            f = _s._getframe(2)
            op.where = (f.f_code.co_name, f.f_lineno)
        deps = set()
        for x in reads:
            self._access(op, x, False, deps)
        for x in writes:
            self._access(op, x, True, deps)
        op.deps = deps
        self.ops.append(op)
        return op

    def _access(self, op, x, isw, deps):
        name, reg = _region(x)
        lst = self.hist.get(name)
        if lst is None:
            lst = []
            self.hist[name] = lst
        p0, p1, f0, f1 = reg
        new = []
        for e in lst:
            (q0, q1, g0, g1), eid, ew, eeng, edma = e
            ov = q0 < p1 and p0 < q1 and g0 < f1 and f0 < g1
            if ov and (isw or ew) and eid != op.id:
                deps.add(eid)
            if ov and isw and p0 <= q0 and q1 <= p1 and f0 <= g0 and g1 <= f1:
                continue
            if (not isw) and (not ew) and (edma == op.isdma) and eeng == op.eng \
                    and (q0, q1, g0, g1) == reg and (not edma or name[0] == "D"):
                continue
            new.append(e)
        new.append((reg, op.id, isw, op.eng, op.isdma))
        self.hist[name] = new

    def emit(self, nc, es):
        ops = self.ops
        engs = ["pe", "act", "dve", "pool", "sp"]
        for op in ops:
            for d in op.deps:
                p = ops[d]
                if p.eng == "pe" and op.eng == "pe" and not op.isdma and not p.isdma:
                    continue
                if (not SAME_ENGINE_SYNC) and p.eng == op.eng and not p.isdma and not op.isdma:
                    continue
                p.sig = True
        csem = {e: es.enter_context(nc.semaphore("c_" + e)) for e in ["pe", "act", "dve", "pool"]}
        dsem = {q: [es.enter_context(nc.semaphore(f"d_{q}{i}")) for i in range(self.NDMASEM)]
                for q in ["sp", "pool", "act"]}
        cnt = {e: 0 for e in csem}
        dcnt = {q: 0 for q in dsem}
        for op in ops:
            if op.isdma:
                n = dcnt[op.eng]
                dcnt[op.eng] += 1
                k = n % self.NDMASEM
                op.sem = dsem[op.eng][k]
                op.val = 16 * (n // self.NDMASEM + 1)
                op.sig = True
                if n >= self.NDMASEM:
                    op.deps.add(("prev", op.sem, op.val - 16))
            elif op.sig:
                cnt[op.eng] += 1
                op.sem = csem[op.eng]
                op.val = cnt[op.eng]
        per = {e: [] for e in engs}
        for op in ops:
            per[op.eng].append(op)
        finals = {q: [] for q in dsem}
        for q in dsem:
            n = dcnt[q]
            for k in range(min(n, self.NDMASEM)):
                last = ((n - 1 - k) // self.NDMASEM) * self.NDMASEM + k
                finals[q].append((dsem[q][k], 16 * (last // self.NDMASEM + 1)))

        def run(engname, e):
            known = {}
            for op in per[engname]:
                waits = {}
                for d in op.deps:
                    if isinstance(d, tuple):
                        sem, val = d[1], d[2]
                    else:
                        p = ops[d]
                        if not p.isdma and not op.isdma and p.eng == op.eng:
                            if p.eng == "pe" or not SAME_ENGINE_SYNC:
                                continue
                        sem, val = p.sem, p.val
                    key = id(sem)
                    if waits.get(key, (None, 0))[1] < val:
                        waits[key] = (sem, val)
                for key, (sem, val) in waits.items():
                    if known.get(key, 0) >= val:
                        continue
                    known[key] = val
                    e.wait_ge(sem, val)
                ins = op.fn(e)
                if DEBUG_WHERE and ins.ins.name in DEBUG_WHERE:
                    print("DBG", ins.ins.name, op.where)
                if op.sig:
                    ins.then_inc(op.sem, 16 if op.isdma else 1)
            for q in dsem:
                if q == engname or (engname == "sp" and q == "act"):
                    for sem, val in finals[q]:
                        e.wait_ge(sem, val)

        block = es.enter_context(nc.Block())
        block.sync(lambda e: run("sp", e))
        block.tensor(lambda e: run("pe", e))
        block.scalar(lambda e: run("act", e))
        block.vector(lambda e: run("dve", e))
        block.gpsimd(lambda e: run("pool", e))

    def matmul(self, out, lhsT, rhs, start=True, stop=True, tp=None):
        kw = dict(start=start, stop=stop)
        if tp is not None:
            kw["tile_position"] = tp
        return self.add("pe", lambda e: e.matmul(out, lhsT=lhsT, rhs=rhs, **kw), [lhsT, rhs], [out])

    def activation(self, out, in_, func, scale=1.0, bias=None, eng="act"):
        rd = [in_]
        kw = {}
        if not isinstance(scale, (int, float)):
            rd.append(scale)
        if bias is not None:
            kw["bias"] = bias
            if not isinstance(bias, (int, float)):
                rd.append(bias)
        return self.add(eng, lambda e: e.activation(out=out, in_=in_, func=func, scale=scale, **kw), rd, [out])

    def tt(self, out, a, b, op, eng="dve"):
        return self.add(eng, lambda e: e.tensor_tensor(out=out, in0=a, in1=b, op=op), [a, b], [out])

    def ts(self, out, a, s1, s2, op0, op1=None, eng="dve"):
        rd = [a] + [s for s in (s1, s2) if s is not None and not isinstance(s, (int, float))]
        if op1 is None:
            return self.add(eng, lambda e: e.tensor_scalar(out=out, in0=a, scalar1=s1, scalar2=None, op0=op0), rd, [out])
        return self.add(eng, lambda e: e.tensor_scalar(out=out, in0=a, scalar1=s1, scalar2=s2, op0=op0, op1=op1), rd, [out])

    def stt(self, out, in0, scalar, in1, op0, op1, eng="dve"):
        rd = [in0, in1] + ([] if isinstance(scalar, (int, float)) else [scalar])
        return self.add(eng, lambda e: e.scalar_tensor_tensor(out=out, in0=in0, scalar=scalar, in1=in1, op0=op0, op1=op1), rd, [out])

    def copy(self, out, in_, eng="dve"):
        if eng == "act":
            return self.add("act", lambda e: e.copy(out=out, in_=in_), [in_], [out])
        return self.add(eng, lambda e: e.tensor_copy(out=out, in_=in_), [in_], [out])

    def memset(self, ap, val, eng="dve"):
        return self.add(eng, lambda e: e.memset(ap, val), [], [ap])

    def reduce_sum(self, out, in_, eng="dve"):
        return self.add(eng, lambda e: e.reduce_sum(out=out, in_=in_, axis=mybir.AxisListType.X), [in_], [out])

    def reciprocal(self, out, in_, eng="dve"):
        return self.add(eng, lambda e: e.reciprocal(out=out, in_=in_), [in_], [out])

    def dma(self, out, in_, q="sp", rd=None, wr=None, slow=False):
        rd = [in_] if rd is None else rd
        wr = [out] if wr is None else wr
        if slow:
            return self.add(q, lambda e: e.dma_start(out=out, in_=in_, allow_slow_non_contiguous=True), rd, wr, isdma=True)
        return self.add(q, lambda e: e.dma_start(out=out, in_=in_), rd, wr, isdma=True)


class Cfg:
    def __init__(self, S_LOC=4096, CPB=4, NB=2, DEPTH=2):
        self.S = S_LOC
        self.CPB = CPB
        self.NB = NB
        self.DEPTH = DEPTH
        self.NT = S_LOC // 512
        self.NCORES = CPB * NB
        self.SEQ = S_LOC * CPB
        self.NPAR = DEPTH * LP + 16 + 2 * CPB
        self.NCST = DEPTH * LC


def pay_off(S):
    nkb = S // 128
    o = {}
    o["kA"] = 0
    o["kC2"] = 2 * S
    o["kD2"] = 4 * S
    o["hB"] = 6 * S
    o["gB"] = 8 * S
    o["vA"] = 10 * S
    o["vC"] = o["vA"] + nkb * 260
    o["vD"] = o["vC"] + nkb * 130
    o["PW"] = o["vD"] + nkb * 130
    return o


def _rot_perm(n, hd):
    q = hd // 4
    idx = np.arange(n)
    h = idx // hd
    r = idx % hd
    quarter = r // q
    partner = np.where(quarter % 2 == 0, r + q, r - q)
    return h * hd + partner


ROPED = [("aq0", 0, 32), ("aq1", 128, 32), ("ak0", 256, 32), ("ak1", 384, 32), ("cq0", 1536, 64), ("cq1", 1664, 64),
         ("ck", 1792, 64), ("dq0", 2048, 64), ("dq1", 2176, 64), ("dk", 2304, 64)]


def win_columns():
    cols = []
    for _, a, hd in ROPED:
        cols.append(np.arange(a, a + 128))
        cols.append(a + _rot_perm(128, hd))
    cols.append(np.arange(768, 1536))
    cols = np.concatenate(cols)
    assert cols.size == NWC * 128
    vcols = np.concatenate([np.arange(512, 768), np.arange(1920, 2048), np.arange(2432, 2560)])
    return cols, vcols


def host_weights(inp, l):
    out = {}
    for i, (gu, dn) in enumerate([("w_gu1", "w_dn1"), ("w_gu2", "w_dn2")]):
        w = np.asarray(inp[gu][l])
        g = w[:, :DFF].reshape(8, 128, NFF, 128)
        u = w[:, DFF:].reshape(8, 128, NFF, 128)
        gu_ = np.stack([g, u], axis=3)
        out[f"wgu{l}{i}"] = np.ascontiguousarray(gu_.transpose(2, 1, 0, 3, 4)).reshape(NFF, 128, 8 * 256)
        w = np.asarray(inp[dn][l])
        w = w.reshape(NFF, 128, 8, 128)
        out[f"wdn{l}{i}"] = np.ascontiguousarray(w.transpose(2, 1, 0, 3)).reshape(8, 128, NFF * 128)
    cols, vcols = win_columns()
    w = np.asarray(inp["w_in"][l])
    wf = w[:, cols].reshape(8, 128, NWC, 128)
    out[f"win{l}"] = np.ascontiguousarray(wf.transpose(2, 1, 0, 3)).reshape(NWC, 128, 8 * 128)
    wv = w[:, vcols].reshape(8, 128, 512)
    out[f"winv{l}"] = np.ascontiguousarray(wv.transpose(1, 0, 2)).reshape(1, 128, 8 * 512)
    w = np.asarray(inp["w_out"][l])
    wo = np.zeros((14, 128, 8, 128), np.float32)
    def put(ch, rows):
        wo[ch, :rows.shape[0]] = rows.reshape(rows.shape[0], 8, 128)
    for h in range(4):
        put(h, w[h * 64:(h + 1) * 64])
        put(6 + h, w[512 + h * 64:512 + (h + 1) * 64])
        put(10 + h, w[768 + h * 64:768 + (h + 1) * 64])
    put(4, w[256:384])
    put(5, w[384:512])
    out[f"wout{l}"] = np.ascontiguousarray(wo.transpose(2, 1, 0, 3)).reshape(8, 128, 14 * 128)
    w = np.asarray(inp["w_mod"][l])
    wm = w.reshape(8, 128, 72, 128)
    out[f"wmod{l}"] = np.ascontiguousarray(wm.transpose(2, 1, 0, 3)).reshape(72, 128, 8 * 128)
    return out


WSHAPES = {"wgu": (NFF, 128, 2048), "wdn": (8, 128, NFF * 128), "win": (NWC, 128, 1024),
           "winv": (1, 128, 4096), "wout": (8, 128, 14 * 128)}


def host_params(inp, cfg, core):
    b = core // cfg.CPB
    j = core % cfg.CPB
    P = np.zeros((128, cfg.NPAR), np.float32)
    p = np.arange(128)
    for l in range(cfg.DEPTH):
        o = l * LP
        P[:, o:o + 72] = np.asarray(inp["b_mod"][l]).reshape(72, 128).T
        P[:, o + 72:o + 96] = np.asarray(inp["ln_g"][l]).reshape(24, 128).T
        P[:, o + 96:o + 120] = np.asarray(inp["ln_b"][l]).reshape(24, 128).T
        cw = np.asarray(inp["conv_w"][l])
        for cc in range(2):
            for w in range(3):
                P[:, o + 120 + cc * 3 + w] = cw[w, cc * 128:(cc + 1) * 128]
        P[:, o + 126] = np.asarray(inp["subln_w"][l])[p % 64]
        qn = np.asarray(inp["qn_w"][l])
        kn = np.asarray(inp["kn_w"][l])
        pp = _rot_perm(128, 64) % 64
        P[:, o + 127] = qn[p % 64]
        P[:, o + 128] = kn[p % 64]
        P[:, o + 129] = qn[pp]
        P[:, o + 130] = kn[pp]
        P[:, o + 131:o + 135] = np.asarray(inp["sink"][l])[None, :]
        for i, nm in enumerate(["lam_q1", "lam_k1", "lam_q2", "lam_k2"]):
            P[:, o + 135 + 32 * i:o + 135 + 32 * (i + 1)] = np.asarray(inp[nm][l])[None, :]
    o = cfg.DEPTH * LP
    cb = np.asarray(inp["c"][b]).reshape(8, 128).T
    cc = np.asarray(inp["c_ctx"]).reshape(8, 128).T
    P[:, o:o + 16:2] = cb
    P[:, o + 1:o + 16:2] = cc
    o += 16
    if j > 0:
        P[:, o + j - 1] = 1.0
    if j < cfg.CPB - 1:
        P[:, o + cfg.CPB + j + 1] = 1.0
    return P


def host_consts(cfg, core):
    j = core % cfg.CPB
    out = {}
    pos = j * cfg.S + np.arange(cfg.S)
    row = (pos // GRID_W).astype(np.float32)
    col = (pos % GRID_W).astype(np.float32)

    def tab(dim):
        nf = dim // 4
        inv = (ROPE_BASE ** (-np.arange(nf, dtype=np.float32) / nf)).astype(np.float32)
        ar = row[:, None] * inv
        ac = col[:, None] * inv
        ang = np.concatenate([ar, ar, ac, ac], axis=-1)
        sign = np.concatenate([-np.ones(nf), np.ones(nf), -np.ones(nf), np.ones(nf)]).astype(np.float32)
        c = np.cos(ang).astype(np.float32).T
        s = (np.sin(ang).astype(np.float32) * sign).T
        rep = 128 // dim
        return np.tile(c, (rep, 1)), np.tile(s, (rep, 1))
    ca, sa = tab(32)
    ch, sh = tab(64)
    t = np.stack([ca, sa, ch, sh], axis=1)
    t = t.reshape(128, 4, cfg.NT, 512).transpose(2, 0, 1, 3)
    out["rope"] = np.ascontiguousarray(t).astype(np.float32)
    m = np.zeros((128, 6, 4, 128), np.float32)
    jj = np.arange(128)[:, None]
    ii = np.arange(128)[None, :]
    for mm in range(6):
        for qb in range(4):
            d = (mm - 1) - qb
            if d == -1:
                m[:, mm, qb] = (jj >= ii)
            elif d == 0:
                m[:, mm, qb] = 1.0
            elif d == 1:
                m[:, mm, qb] = (jj <= ii)
    out["wmask"] = m.reshape(128, 6 * 512).astype(ml_dtypes.bfloat16)
    return out


N16 = 54 * 1024
N32 = 18 * 1024


class Arena:
    def __init__(self, a16, a32):
        self.a16, self.a32 = a16, a32
        self.o16 = self.o32 = 0

    def bf(self, n):
        ap = self.a16[:, self.o16:self.o16 + n]
        self.o16 += n
        assert self.o16 <= N16, ("bf16 arena overflow", self.o16)
        return ap

    def f32(self, n):
        ap = self.a32[:, self.o32:self.o32 + n]
        self.o32 += n
        assert self.o32 <= N32, ("fp32 arena overflow", self.o32)
        return ap

    def reset(self):
        self.o16 = self.o32 = 0


def v3(ap, inner):
    return ap.rearrange("p (a b) -> p a b", b=inner)


def bank(ps, i, w=512):
    return ps[:, i * 512:i * 512 + w]


def emit_p0(P, cfg, T, al, ps):
    al.reset()
    inb = [al.f32(4096) for _ in range(2)]
    outb = [al.bf(4096) for _ in range(2)]
    par = al.f32(cfg.NPAR)
    cst = al.f32(cfg.NCST)
    sc = al.f32(16)
    tmp = al.f32(32)
    s12 = al.f32(4)
    P.dma(par, T["params"])
    k = 0
    for l in range(cfg.DEPTH):
        names = [f"wgu{l}0", f"wdn{l}0", f"win{l}", f"winv{l}", f"wout{l}", f"wgu{l}1", f"wdn{l}1"]
        for name in names:
            src, dst = T[name], T[name + "b"]
            N0, _, Fd = src.shape
            G = max(1, 4096 // Fd)
            for g0 in range(0, N0, G):
                g1 = min(N0, g0 + G)
                n = (g1 - g0) * Fd
                ib, ob = inb[k % 2][:, :n], outb[k % 2][:, :n]
                P.dma(v3(ib, Fd), src[g0:g1].rearrange("g p f -> p g f"))
                P.copy(ob, ib, eng=("dve" if k % 2 == 0 else "act"))
                P.dma(dst[g0:g1].rearrange("g p f -> p g f"), v3(ob, Fd), q="pool")
                k += 1
    cb = cfg.DEPTH * LP
    P.activation(sc, par[:, cb:cb + 16], AF.Silu)
    for l in range(cfg.DEPTH):
        o = l * LP
        oc = l * LC
        psm = bank(ps, l % 2, 144).rearrange("p (n w) -> p n w", w=2)
        for g0 in range(0, 72, 4):
            ib = inb[k % 2]
            k += 1
            P.dma(v3(ib, 1024), T[f"wmod{l}"][g0:g0 + 4].rearrange("g p f -> p g f"))
            for g in range(4):
                for kc in range(8):
                    P.matmul(psm[:, g0 + g, :], ib[:, g * 1024 + kc * 128:g * 1024 + (kc + 1) * 128],
                             sc[:, 2 * kc:2 * kc + 2], start=(kc == 0), stop=(kc == 7))
        for w in range(2):
            mv = cst[:, oc + w * 72:oc + (w + 1) * 72]
            P.tt(mv, psm[:, :, w], par[:, o:o + 72], ALU.add)
            for n in (1, 4, 7):
                P.ts(mv[:, n * 8:n * 8 + 8], mv[:, n * 8:n * 8 + 8], 1.0, None, ALU.add)
            for n in (2, 8):
                P.ts(mv[:, n * 8:n * 8 + 8], mv[:, n * 8:n * 8 + 8], 0.5, None, ALU.mult)
        lam_init = 0.8 - 0.6 * math.exp(-0.3 * l)
        for i in range(2):
            P.tt(tmp, par[:, o + 135 + 64 * i:o + 167 + 64 * i], par[:, o + 167 + 64 * i:o + 199 + 64 * i], ALU.mult)
            P.reduce_sum(s12[:, i:i + 1], tmp)
        P.activation(s12[:, 2:4], s12[:, 0:2], AF.Exp)
        P.tt(cst[:, oc + 144:oc + 145], s12[:, 2:3], s12[:, 3:4], ALU.subtract)
        P.ts(cst[:, oc + 144:oc + 145], cst[:, oc + 144:oc + 145], lam_init, None, ALU.add)
        P.ts(cst[:, oc + 145:oc + 146], cst[:, oc + 144:oc + 145], -1.0, None, ALU.mult)
        P.activation(cst[:, oc + 146:oc + 150], par[:, o + 131:o + 135], AF.Exp)
        P.ts(cst[:, oc + 150:oc + 151], par[:, o + 126:o + 127], 1.0 - lam_init, None, ALU.mult)
        P.memset(cst[:, oc + 151:oc + 152], 0.0)
    P.dma(T["cst"], cst, q="pool")


class FFNBufs:
    def __init__(self, al, TW=512, hT=None, aT=None, wgu=None):
        self.hT = hT if hT is not None else al.bf(8 * TW)
        self.aT = aT if aT is not None else al.bf(NFF * TW)
        wg = wgu if wgu is not None else al.bf(2 * 4096)
        self.wgu = [wg[:, 0:4096], wg[:, 4096:8192]]
        self.wdn_all = al.bf(2 * NFF * 128)
        self.wdn = [self.wdn_all[:, 0:NFF * 128], self.wdn_all[:, NFF * 128:2 * NFF * 128]]
        self.sg = [al.bf(TW) for _ in range(2)]
        self.t = [al.f32(TW) for _ in range(2)]
        self.z = al.f32(8 * TW)
        self.lt = [al.f32(TW) for _ in range(8)]
        self.u = [al.f32(TW) for _ in range(2)]


def emit_rsqrt(P, out, in_):
    P.ts(out, in_, EPS, None, ALU.add)
    P.activation(out, out, AF.Ln)
    P.activation(out, out, AF.Exp, scale=-0.5)


def emit_ln(P, ps, fb, z, TW, lng, lnb, out, ones, bk=(6, 7)):
    acc1, acc2, mean, msq, var, rstd, nmr, sq = [x[:, :TW] for x in fb.lt]
    zc = lambda c: z[:, c * TW:(c + 1) * TW]
    P.tt(acc1, zc(0), zc(1), ALU.add)
    for c in range(2, 8):
        P.tt(acc1, acc1, zc(c), ALU.add)
    for c in range(8):
        if c == 0:
            P.activation(acc2, zc(0), AF.Square)
        else:
            P.activation(sq, zc(c), AF.Square)
            P.tt(acc2, acc2, sq, ALU.add)
    b1, b2 = bank(ps, bk[0], TW), bank(ps, bk[1], TW)
    P.matmul(b1, ones, acc1)
    P.matmul(b2, ones, acc2)
    P.ts(mean, b1, 1.0 / D, None, ALU.mult)
    P.tt(msq, mean, mean, ALU.mult)
    P.stt(var, b2, 1.0 / D, msq, ALU.mult, ALU.subtract)
    emit_rsqrt(P, rstd, var)
    P.stt(nmr, mean, -1.0, rstd, ALU.mult, ALU.mult)
    for c in range(8):
        u = fb.u[c % 2][:, :TW]
        P.tt(u, zc(c), rstd, ALU.mult)
        P.tt(u, u, nmr, ALU.add)
        P.activation(out[:, c * TW:(c + 1) * TW], u, AF.Identity, scale=lng[:, c:c + 1], bias=lnb[:, c:c + 1])


def emit_ffn(P, ps, fb, s, TW, mods, wgu, wdn, lng, lnb, out, ones):
    alpha = (2.0 * 2) ** 0.25
    sc_ = lambda c: s[:, c * TW:(c + 1) * TW]
    hT = lambda c: fb.hT[:, c * TW:(c + 1) * TW]
    for c in range(8):
        P.ts(hT(c), sc_(c), mods[:, 8 + c:9 + c], mods[:, c:c + 1], ALU.mult, ALU.add)
    for jp in range(NFF // 2):
        wb = fb.wgu[jp % 2]
        P.dma(v3(wb, 2048), wgu[2 * jp:2 * jp + 2].rearrange("g p f -> p g f"))
        for jj in range(2):
            j = 2 * jp + jj
            bg, bu = bank(ps, 2 * (j % 2), TW), bank(ps, 2 * (j % 2) + 1, TW)
            for kc in range(8):
                P.matmul(bg, wb[:, jj * 2048 + kc * 256:jj * 2048 + kc * 256 + 128], hT(kc), start=(kc == 0), stop=(kc == 7))
            for kc in range(8):
                P.matmul(bu, wb[:, jj * 2048 + kc * 256 + 128:jj * 2048 + kc * 256 + 256], hT(kc), start=(kc == 0), stop=(kc == 7))
            sg = fb.sg[j % 2][:, :TW]
            P.activation(sg, bg, AF.Silu)
            P.tt(fb.aT[:, j * TW:(j + 1) * TW], sg, bu, ALU.mult)
    for dc in range(8):
        wb = fb.wdn[dc % 2]
        P.dma(wb, wdn[dc])
        by = bank(ps, 4 + dc % 2, TW)
        for j in range(NFF):
            P.matmul(by, wb[:, j * 128:(j + 1) * 128], fb.aT[:, j * TW:(j + 1) * TW], start=(j == 0), stop=(j == NFF - 1))
        t = fb.t[dc % 2][:, :TW]
        P.activation(t, by, AF.Identity, scale=mods[:, 16 + dc:17 + dc])
        P.stt(fb.z[:, dc * TW:(dc + 1) * TW], sc_(dc), alpha, t, ALU.mult, ALU.add)
    emit_ln(P, ps, fb, fb.z, TW, lng, lnb, out, ones)


def emit_p1(P, cfg, T, l, al, ps):
    al.reset()
    S = cfg.S
    par = al.f32(cfg.NPAR)
    cst = al.f32(cfg.NCST)
    ones = al.f32(128)
    blk = al.f32(128)
    fb = FFNBufs(al)
    sbuf = [al.f32(8 * 512)]
    s1 = fb.z
    rope = al.f32(4 * 512)
    tmp = fb.lt[0:4]
    win = fb.wgu
    winv = fb.wdn_all[:, 0:4096]
    qt = al.bf(6 * 512)
    kt = al.bf(4 * 512)
    bt = al.bf(4 * 512)
    vt = [al.bf(4 * 520) for _ in range(2)]
    P.dma(par, T["params"])
    P.dma(cst, T["cst"])
    P.memset(ones, 1.0)
    P.memset(blk, 0.0)
    P.memset(blk[0:64, 0:64], 1.0)
    P.memset(blk[64:128, 64:128], 1.0)
    for b in vt:
        P.memset(b, 1.0)
    o = l * LP
    oc = l * LC
    src_s = T["xT"] if l == 0 else T[f"sout{l - 1}"]
    src_c = T["ctxT"] if l == 0 else T[f"cout{l - 1}"]
    tiles = ["ctx"] + list(range(cfg.NT))
    for it, ti in enumerate(tiles):
        isctx = ti == "ctx"
        TW = 256 if isctx else 512
        w = 1 if isctx else 0
        mods = cst[:, oc + w * 72:oc + (w + 1) * 72]
        s = sbuf[0][:, :8 * TW]
        P.dma(s, src_c if isctx else src_s[ti])
        s1t = s1[:, :8 * TW]
        emit_ffn(P, ps, fb, s, TW, mods[:, 0:24], T[f"wgu{l}0b"], T[f"wdn{l}0b"],
                 par[:, o + 72:o + 80], par[:, o + 96:o + 104], s1t, ones)
        P.dma(T[f"c1_{l}"] if isctx else T[f"s1_{l}"][ti], s1t, q="pool")
        hT = lambda c: fb.hT[:, c * TW:(c + 1) * TW]
        for c in range(8):
            P.ts(hT(c), s1t[:, c * TW:(c + 1) * TW], mods[:, 32 + c:33 + c], mods[:, 24 + c:25 + c], ALU.mult, ALU.add)
        if not isctx:
            P.dma(rope, T["rope"][ti])
        P.dma(winv, T[f"winv{l}b"][0])
        wq = []
        def wchunk(ch):
            g = ch // 4
            if ch % 4 == 0:
                n = min(4, NWC - ch)
                P.dma(v3(win[g % 2][:, :n * 1024], 1024), T[f"win{l}b"][ch:ch + n].rearrange("g p f -> p g f"))
            return win[g % 2][:, (ch % 4) * 1024:(ch % 4 + 1) * 1024]
        bkc = [0]
        def proj(ch):
            wb = wchunk(ch)
            b = bank(ps, bkc[0] % 4, TW)
            bkc[0] += 1
            for kc in range(8):
                P.matmul(b, wb[:, kc * 128:(kc + 1) * 128], hT(kc), start=(kc == 0), stop=(kc == 7))
            return b
        dsts = {"aq0": qt[:, 0:TW], "aq1": qt[:, TW:2 * TW], "cq0": qt[:, 2 * TW:3 * TW], "cq1": qt[:, 3 * TW:4 * TW],
                "dq0": qt[:, 4 * TW:5 * TW], "dq1": qt[:, 5 * TW:6 * TW],
                "ak0": kt[:, 0:TW], "ak1": kt[:, TW:2 * TW], "ck": kt[:, 2 * TW:3 * TW], "dk": kt[:, 3 * TW:4 * TW]}
        for ri, (nm, _, hd) in enumerate(ROPED):
            bz = proj(2 * ri)
            dst = dsts[nm]
            isd = nm.startswith("d")
            t0, t1, t2, t3 = [x[:, :TW] for x in tmp]
            if isd:
                gcol = par[:, o + 127:o + 128] if nm[1] == "q" else par[:, o + 128:o + 129]
                gpcol = par[:, o + 129:o + 130] if nm[1] == "q" else par[:, o + 130:o + 131]
                P.activation(t3, bz, AF.Square)
                bs = bank(ps, 6, TW)
                P.matmul(bs, blk, t3)
                P.ts(t3, bs, 1.0 / 64, None, ALU.mult)
                emit_rsqrt(P, t3, t3)
            if isctx:
                if isd:
                    P.stt(dst, bz, gcol, t3, ALU.mult, ALU.mult)
                else:
                    P.copy(dst, bz, eng="act")
                continue
            br = proj(2 * ri + 1)
            cosT = rope[:, (0 if hd == 32 else 2) * 512:(0 if hd == 32 else 2) * 512 + 512]
            sinT = rope[:, (1 if hd == 32 else 3) * 512:(1 if hd == 32 else 3) * 512 + 512]
            if isd:
                P.stt(t0, bz, gcol, cosT, ALU.mult, ALU.mult)
                P.stt(t1, br, gpcol, sinT, ALU.mult, ALU.mult)
                P.tt(t0, t0, t1, ALU.add)
                P.tt(dst, t0, t3, ALU.mult)
            else:
                P.tt(t0, bz, cosT, ALU.mult)
                P.tt(t1, br, sinT, ALU.mult)
                P.tt(dst, t0, t1, ALU.add)
        for cc in range(2):
            b = proj(20 + cc)
            P.copy(bt[:, (2 + cc) * TW:(3 + cc) * TW], b, eng="act")
        for cc in range(2):
            b = proj(22 + cc)
            P.copy(tmp[cc][:, :TW], b, eng="act")
        for cc in range(2):
            b = proj(24 + cc)
            P.tt(bt[:, cc * TW:(cc + 1) * TW], tmp[cc][:, :TW], b, ALU.mult)
        nsub = TW // 128
        vb = vt[it % 2]
        for sub in range(nsub):
            b = bank(ps, bkc[0] % 4, 512)
            bkc[0] += 1
            for kc in range(8):
                P.matmul(b, fb.hT[:, kc * TW + sub * 128:kc * TW + (sub + 1) * 128], winv[:, kc * 512:(kc + 1) * 512],
                         start=(kc == 0), stop=(kc == 7))
            P.copy(v3(vb[:, sub * 520:(sub + 1) * 520], 65)[:, :, 0:64], v3(b, 64), eng="act")
        if isctx:
            pay, po, SS, c0, kb0 = T[f"payc{l}"], pay_off(LCTX), LCTX, 0, 0
            P.dma(T[f"qc{l}"], qt[:, :6 * TW], q="pool")
        else:
            pay, po, SS, c0, kb0 = T[f"pay{l}"], pay_off(S), S, ti * 512, ti * 4
            P.dma(T[f"q{l}"][ti], qt, q="pool")
        P.dma(v3(pay[:, po["kA"]:po["kA"] + 2 * SS], SS)[:, :, c0:c0 + TW], v3(kt[:, 0:2 * TW], TW), q="pool")
        for nm, ki in (("kC2", 2), ("kD2", 3)):
            for kvs in range(2):
                for half in range(2):
                    P.dma(pay[64 * half:64 * half + 64, po[nm] + kvs * SS + c0:po[nm] + kvs * SS + c0 + TW],
                          kt[64 * kvs:64 * kvs + 64, ki * TW:(ki + 1) * TW], q="pool")
        P.dma(v3(pay[:, po["hB"]:po["hB"] + 4 * SS], SS)[:, :, c0:c0 + TW], v3(bt[:, 0:4 * TW], TW), q="pool")
        vb3 = v3(vb[:, :nsub * 520], 520)
        P.dma(v3(pay[:, po["vA"] + kb0 * 260:po["vA"] + (kb0 + nsub) * 260], 260), vb3[:, :, 0:260], q="pool")
        P.dma(v3(pay[:, po["vC"] + kb0 * 130:po["vC"] + (kb0 + nsub) * 130], 130), vb3[:, :, 260:390], q="pool")
        P.dma(v3(pay[:, po["vD"] + kb0 * 130:po["vD"] + (kb0 + nsub) * 130], 130), vb3[:, :, 390:520], q="pool")


def tensor_specs(cfg):
    sp = {}
    NT, S = cfg.NT, cfg.S
    sp["xT"] = ([NT, 128, 4096], F32)
    sp["ctxT"] = ([128, 2048], F32)
    sp["params"] = ([128, cfg.NPAR], F32)
    sp["rope"] = ([NT, 128, 2048], F32)
    sp["wmask"] = ([128, 3072], BF16)
    sp["cst"] = ([128, cfg.NCST], F32)
    for l in range(cfg.DEPTH):
        for i in range(2):
            sp[f"wgu{l}{i}"] = (list(WSHAPES["wgu"]), F32)
            sp[f"wdn{l}{i}"] = (list(WSHAPES["wdn"]), F32)
        sp[f"win{l}"] = (list(WSHAPES["win"]), F32)
        sp[f"winv{l}"] = (list(WSHAPES["winv"]), F32)
        sp[f"wout{l}"] = (list(WSHAPES["wout"]), F32)
        sp[f"wmod{l}"] = ([72, 128, 1024], F32)
        for nm in [f"wgu{l}0", f"wgu{l}1", f"wdn{l}0", f"wdn{l}1", f"win{l}", f"winv{l}", f"wout{l}"]:
            sp[nm + "b"] = (sp[nm][0], BF16)
        sp[f"s1_{l}"] = ([NT, 128, 4096], F32)
        sp[f"c1_{l}"] = ([128, 2048], F32)
        sp[f"q{l}"] = ([NT, 128, 3072], BF16)
        sp[f"qc{l}"] = ([128, 1536], BF16)
        sp[f"pay{l}"] = ([128, pay_off(S)["PW"]], BF16)
        sp[f"payc{l}"] = ([128, pay_off(LCTX)["PW"]], BF16)
        sp[f"gath{l}"] = ([cfg.CPB * 128, pay_off(S)["PW"]], BF16)
        sp[f"sout{l}"] = ([NT, 128, 4096], F32)
        sp[f"cout{l}"] = ([128, 2048], F32)
    return sp


def build_launch(cfg, phases, ext_in, ext_out, internal=()):
    nc = bass.Bass("TRN2", target_bir_lowering=False)
    sp = tensor_specs(cfg)
    T = {}
    for nm in ext_in:
        T[nm] = nc.dram_tensor(nm, sp[nm][0], sp[nm][1], kind="ExternalInput").ap()
    for nm in ext_out:
        T[nm] = nc.dram_tensor(nm, sp[nm][0], sp[nm][1], kind="ExternalOutput").ap()
    for nm in internal:
        T[nm] = nc.dram_tensor(nm, sp[nm][0], sp[nm][1]).ap()
    with contextlib.ExitStack() as es:
        a16 = es.enter_context(nc.sbuf_tensor("a16", [128, N16], BF16))
        a32 = es.enter_context(nc.sbuf_tensor("a32", [128, N32], F32))
        ps = es.enter_context(nc.psum_tensor("ps", [128, 4096], F32))
        al = Arena(a16, a32)
        P = Prog()
        for ph in phases:
            ph(P, cfg, T, al, ps)
        P.emit(nc, es)
    return nc


RAWW = lambda l: [f"wgu{l}0", f"wdn{l}0", f"win{l}", f"winv{l}", f"wout{l}", f"wgu{l}1", f"wdn{l}1"]


def emit_attn(P, ps, N, qs, tps, blocks, scale, PT):
    blocks = list(blocks) if not hasattr(blocks, "__next__") else blocks
    prev = None
    i = 0

    def pv(pt, vs, first, last):
        for t in range(4):
            P.matmul(ps[0:65, (4 + t) * 512:(4 + t) * 512 + N], vs[t], pt[:, t * N:(t + 1) * N], start=first, stop=last)

    for ks, vs, mask in blocks:
        pt = PT[i % 2][:, :4 * N]
        for t in range(4):
            P.matmul(ps[:, t * 512:t * 512 + N], ks[t], qs[t], tp=tps[t])
        for half in range(2):
            if N == 512:
                P.activation(pt[:, half * 1024:(half + 1) * 1024], ps[:, half * 1024:(half + 1) * 1024], AF.Exp, scale=scale)
            else:
                P.activation(v3(pt[:, half * 2 * N:(half + 1) * 2 * N], N), v3(ps[:, half * 1024:(half + 1) * 1024], 512)[:, :, 0:N],
                             AF.Exp, scale=scale)
        if mask is not None:
            for t in range(4):
                P.tt(pt[:, t * N:(t + 1) * N], pt[:, t * N:(t + 1) * N], mask, ALU.mult)
        if prev is not None:
            pv(prev[0], prev[1], prev[2] == 0, False)
        prev = (pt, vs, i)
        i += 1
    pv(prev[0], prev[1], prev[2] == 0, True)


def emit_p2(P, cfg, T, l, al, ps):
    al.reset()
    S, CPB, NKB = cfg.S, cfg.CPB, cfg.S // 128
    need_ctx = l < cfg.DEPTH - 1
    po, pc = pay_off(S), pay_off(LCTX)
    o, oc = l * LP, l * LC
    alpha = (2.0 * 2) ** 0.25
    par = al.f32(cfg.NPAR)
    cst = al.f32(cfg.NCST)
    ones = al.f32(128)
    sel = al.f32(64)
    s1t = al.f32(8 * 512)
    R1 = al.bf(11312)
    R2 = al.bf(8192)
    R3 = al.bf(4096)
    fb = FFNBufs(al, hT=R3, aT=R1[:, :NFF * 512], wgu=R2)
    kvb = [R1[:, i * 1544:(i + 1) * 1544] for i in range(3)]
    PT = [R1[:, 4632 + i * 2048:4632 + (i + 1) * 2048] for i in range(2)]
    yT = R2[:, :14 * 512]
    qt = R3[:, :6 * 512]
    wmask = al.bf(3072)
    kCx = [al.bf(S + 256) for _ in range(2)]
    vCx = al.bf((NKB + 2) * 130)
    kCc = al.bf(512)
    vCc = al.bf(260)
    wout = [al.bf(14 * 128) for _ in range(2)]
    hx = al.bf(2 * 514)
    gbt = al.bf(2 * 512)
    cand = al.bf(CPB * 130)
    candh = al.bf(4 * CPB)
    hal = al.bf(4)
    lt = fb.lt
    pay, payc, gath = T[f"pay{l}"], T[f"payc{l}"], T[f"gath{l}"]
    gath3 = gath.rearrange("(r p) c -> p r c", p=128)
    wsel = lambda side, r: par[:, cfg.DEPTH * LP + 16 + side * CPB + r:cfg.DEPTH * LP + 17 + side * CPB + r]

    P.dma(par, T["params"])
    P.dma(cst, T["cst"])
    P.dma(wmask, T["wmask"])
    P.memset(ones, 1.0)
    P.memset(sel, 0.0)
    P.memset(sel[64:65, :], 1.0)

    def combine(dst, srcs, side):
        P.ts(dst, srcs[0], wsel(side, 0), None, ALU.mult)
        for r in range(1, CPB):
            P.stt(dst, srcs[r], wsel(side, r), dst, ALU.mult, ALU.add)

    for kvs in range(2):
        P.dma(kCx[kvs][:, 128:128 + S], pay[:, po["kC2"] + kvs * S:po["kC2"] + (kvs + 1) * S])
        for side in range(2):
            c_src = po["kC2"] + kvs * S + (S - 128 if side == 0 else 0)
            c_dst = 0 if side == 0 else 128 + S
            P.dma(v3(cand[:, :CPB * 128], 128), gath3[:, :, c_src:c_src + 128])
            combine(kCx[kvs][:, c_dst:c_dst + 128], [cand[:, r * 128:(r + 1) * 128] for r in range(CPB)], side)
    P.dma(vCx[:, 130:130 * (NKB + 1)], pay[:, po["vC"]:po["vC"] + NKB * 130])
    for side in range(2):
        c_src = po["vC"] + ((NKB - 1) * 130 if side == 0 else 0)
        c_dst = 0 if side == 0 else (NKB + 1) * 130
        P.dma(v3(cand[:, :CPB * 130], 130), gath3[:, :, c_src:c_src + 130])
        combine(vCx[:, c_dst:c_dst + 130], [cand[:, r * 130:(r + 1) * 130] for r in range(CPB)], side)
    for r in range(CPB):
        for cc in range(2):
            for side in range(2):
                c_src = po["hB"] + cc * S + (S - 1 if side == 0 else 0)
                k = r * 4 + cc * 2 + side
                P.dma(candh[:, k:k + 1], gath[r * 128:(r + 1) * 128, c_src:c_src + 1], slow=True)
    for cc in range(2):
        for side in range(2):
            combine(hal[:, cc * 2 + side:cc * 2 + side + 1],
                    [candh[:, r * 4 + cc * 2 + side:r * 4 + cc * 2 + side + 1] for r in range(CPB)], side)
    P.dma(v3(kCc, 256), v3(payc[:, pc["kC2"]:pc["kC2"] + 512], 256))
    P.dma(vCc, payc[:, pc["vC"]:pc["vC"] + 260])

    nlam = cst[:, oc + 145:oc + 146]
    ysub = cst[:, oc + 150:oc + 151]

    def fin_std(N, t, ydst, esink=None):
        o_s = lt[2 * (t % 2)][0:65, :N]
        r = lt[2 * (t % 2) + 1][0:64, :N]
        P.copy(o_s, ps[0:65, (4 + t) * 512:(4 + t) * 512 + N], eng="act")
        zb = ps[0:64, t * 512:t * 512 + N]
        P.matmul(zb, sel[0:65, 0:64], o_s)
        if esink is not None:
            P.ts(r, zb, esink, None, ALU.add)
            P.reciprocal(r, r)
        else:
            P.reciprocal(r, zb)
        P.tt(ydst, o_s[0:64, :], r, ALU.mult)

    def fin_A(N, hl, ydst):
        o1, o2 = lt[0][0:65, :N], lt[1][0:65, :N]
        r1, r2 = lt[2][0:64, :N], lt[3][0:64, :N]
        t1, t2, rs = lt[4][0:64, :N], lt[5][0:64, :N], lt[6][0:64, :N]
        P.copy(o1, ps[0:65, (4 + 2 * hl) * 512:(4 + 2 * hl) * 512 + N], eng="act")
        P.copy(o2, ps[0:65, (5 + 2 * hl) * 512:(5 + 2 * hl) * 512 + N], eng="act")
        z1, z2, ssb = ps[0:64, 0:N], ps[0:64, 512:512 + N], ps[0:64, 1024:1024 + N]
        P.matmul(z1, sel[0:65, 0:64], o1)
        P.matmul(z2, sel[0:65, 0:64], o2)
        P.reciprocal(r1, z1)
        P.reciprocal(r2, z2)
        P.tt(t1, o1[0:64, :], r1, ALU.mult)
        P.tt(t2, o2[0:64, :], r2, ALU.mult)
        P.stt(t1, t2, nlam[0:64, :], t1, ALU.mult, ALU.add)
        P.tt(t2, t1, t1, ALU.mult)
        P.matmul(ssb, ones[0:64, 0:64], t2)
        P.ts(rs, ssb, 1.0 / 64, None, ALU.mult)
        emit_rsqrt(P, rs, rs)
        P.stt(ydst, t1, ysub[0:64, :], rs, ALU.mult, ALU.mult)

    tiles = (["ctx"] if need_ctx else []) + list(range(cfg.NT))
    for it, ti in enumerate(tiles):
        isctx = ti == "ctx"
        N = 256 if isctx else 512
        w = 1 if isctx else 0
        mods = cst[:, oc + w * 72:oc + (w + 1) * 72]
        P.dma(qt[:, :6 * N], T[f"qc{l}"] if isctx else T[f"q{l}"][ti])
        P.dma(s1t[:, :8 * N], T[f"c1_{l}"] if isctx else T[f"s1_{l}"][ti])
        yc = lambda c, rows=128: yT[0:rows, c * N:(c + 1) * N]
        scs = [(payc, pc, LCTX, 0, 256)]
        if not isctx:
            for r in range(CPB):
                for sc in range(S // 512):
                    scs.append((gath[r * 128:(r + 1) * 128, :], po, S, sc * 512, 512))
        cnt = [0]

        def gen(kind, hp):
            def load(j):
                src, od, SS, k0, nk = scs[j]
                buf = kvb[cnt[0] % 3]
                cnt[0] += 1
                nsub = nk // 128
                kb0 = k0 // 128
                if kind == "A":
                    P.dma(buf[:, 0:nk], src[:, od["kA"] + hp * SS + k0:od["kA"] + hp * SS + k0 + nk])
                    P.dma(v3(buf[:, 1024:1024 + nsub * 130], 130),
                          v3(src[:, od["vA"] + kb0 * 260:od["vA"] + (kb0 + nsub) * 260], 260)[:, :, hp * 130:(hp + 1) * 130])
                else:
                    P.dma(v3(buf[:, 0:1024], 512)[:, :, 0:nk], v3(src[:, od["kD2"]:od["kD2"] + 2 * SS], SS)[:, :, k0:k0 + nk])
                    P.dma(buf[:, 1024:1024 + nsub * 130], src[:, od["vD"] + kb0 * 130:od["vD"] + (kb0 + nsub) * 130])
                return buf
            nxt = load(0)
            for j in range(len(scs)):
                buf = nxt
                if j + 1 < len(scs):
                    nxt = load(j + 1)
                for sub in range(scs[j][4] // 128):
                    if kind == "A":
                        ks = [buf[32 * t:32 * t + 32, sub * 128:(sub + 1) * 128] for t in range(4)]
                        vs = [buf[:, 1024 + sub * 130 + (t // 2) * 65:1024 + sub * 130 + (t // 2) * 65 + 65] for t in range(4)]
                    else:
                        ks = [buf[64 * (t % 2):64 * (t % 2) + 64, (t // 2) * 512 + sub * 128:(t // 2) * 512 + (sub + 1) * 128] for t in range(4)]
                        vs = [buf[:, 1024 + sub * 130 + (t // 2) * 65:1024 + sub * 130 + (t // 2) * 65 + 65] for t in range(4)]
                    yield ks, vs, None

        for hp in range(2):
            qs = [qt[32 * t:32 * t + 32, hp * N:(hp + 1) * N] for t in range(4)]
            emit_attn(P, ps, N, qs, [(32 * t, 0) for t in range(4)], gen("A", hp), 32 ** -0.5, PT)
            for hl in range(2):
                fin_A(N, hl, yc(2 * hp + hl, 64))
        qs = [qt[64 * (t % 2):64 * (t % 2) + 64, (4 + t // 2) * N:(5 + t // 2) * N] for t in range(4)]
        tpd = [(64 * (t % 2), 0) for t in range(4)]
        emit_attn(P, ps, N, qs, tpd, gen("D", 0), 64 ** -0.5, PT)
        for t in range(4):
            fin_std(N, t, yc(10 + t, 64))
        qs = [qt[64 * (t % 2):64 * (t % 2) + 64, (2 + t // 2) * N:(3 + t // 2) * N] for t in range(4)]
        blocks = []
        for j in range(2):
            ks = [kCc[64 * (t % 2):64 * (t % 2) + 64, (t // 2) * 256 + j * 128:(t // 2) * 256 + (j + 1) * 128] for t in range(4)]
            vs = [vCc[:, j * 130 + (t // 2) * 65:j * 130 + (t // 2) * 65 + 65] for t in range(4)]
            blocks.append((ks, vs, None))
        if not isctx:
            for m in range(6):
                e = 4 * ti + m
                ks = [kCx[t // 2][64 * (t % 2):64 * (t % 2) + 64, e * 128:(e + 1) * 128] for t in range(4)]
                vs = [vCx[:, e * 130 + (t // 2) * 65:e * 130 + (t // 2) * 65 + 65] for t in range(4)]
                blocks.append((ks, vs, wmask[:, m * 512:(m + 1) * 512]))
        emit_attn(P, ps, N, qs, tpd, blocks, 64 ** -0.5, PT)
        for t in range(4):
            fin_std(N, t, yc(6 + t, 64), esink=cst[0:64, oc + 146 + t:oc + 147 + t])
        if isctx:
            P.memset(hx, 0.0)
            P.dma(v3(hx, 514)[:, :, 1:N + 1], v3(payc[:, pc["hB"]:pc["hB"] + 2 * LCTX], LCTX))
            P.dma(v3(gbt[:, :2 * N], N), v3(payc[:, pc["gB"]:pc["gB"] + 2 * LCTX], LCTX))
        else:
            c0 = ti * 512
            lo = max(c0 - 1, 0)
            hi = min(c0 + N + 1, S)
            P.dma(v3(hx, 514)[:, :, lo - (c0 - 1):hi - (c0 - 1)], v3(pay[:, po["hB"]:po["hB"] + 2 * S], S)[:, :, lo:hi])
            for cc in range(2):
                if c0 == 0:
                    P.copy(hx[:, cc * 514:cc * 514 + 1], hal[:, cc * 2:cc * 2 + 1])
                if c0 + N == S:
                    P.copy(hx[:, cc * 514 + N + 1:cc * 514 + N + 2], hal[:, cc * 2 + 1:cc * 2 + 2])
            P.dma(v3(gbt, 512), v3(pay[:, po["gB"]:po["gB"] + 2 * S], S)[:, :, c0:c0 + N])
        for cc in range(2):
            acc = lt[cc][:, :N]
            cw = lambda wi: par[:, o + 120 + cc * 3 + wi:o + 121 + cc * 3 + wi]
            P.ts(acc, hx[:, cc * 514:cc * 514 + N], cw(0), None, ALU.mult)
            P.stt(acc, hx[:, cc * 514 + 1:cc * 514 + N + 1], cw(1), acc, ALU.mult, ALU.add)
            P.stt(acc, hx[:, cc * 514 + 2:cc * 514 + N + 2], cw(2), acc, ALU.mult, ALU.add)
            P.tt(yc(4 + cc), acc, gbt[:, cc * N:(cc + 1) * N], ALU.mult)
        for dc in range(8):
            wb = wout[dc % 2]
            P.dma(wb, T[f"wout{l}b"][dc])
            by = bank(ps, 4 + dc % 2, N)
            for c in range(14):
                rows = 128 if c in (4, 5) else 64
                P.matmul(by, wb[0:rows, c * 128:(c + 1) * 128], yc(c, rows), start=(c == 0), stop=(c == 13))
            t = fb.t[dc % 2][:, :N]
            P.activation(t, by, AF.Identity, scale=mods[:, 40 + dc:41 + dc])
            P.stt(fb.z[:, dc * N:(dc + 1) * N], s1t[:, dc * N:(dc + 1) * N], alpha, t, ALU.mult, ALU.add)
        z = fb.z[:, :8 * N]
        emit_ln(P, ps, fb, z, N, par[:, o + 80:o + 88], par[:, o + 104:o + 112], z, ones)
        emit_ffn(P, ps, fb, z, N, mods[:, 48:72], T[f"wgu{l}1b"], T[f"wdn{l}1b"],
                 par[:, o + 88:o + 96], par[:, o + 112:o + 120], z, ones)
        P.dma(T[f"cout{l}"] if isctx else T[f"sout{l}"][ti], z, q="pool")


def host_inputs(cfg, inp):
    W = {}
    for l in range(cfg.DEPTH):
        W.update(host_weights(inp, l))
    maps = []
    x = np.asarray(inp["x"], np.float32)
    ctx = np.asarray(inp["ctx"], np.float32)
    for core in range(cfg.NCORES):
        b, j = core // cfg.CPB, core % cfg.CPB
        m = dict(W)
        m["params"] = host_params(inp, cfg, core)
        m.update(host_consts(cfg, core))
        xs = x[b, j * cfg.S:(j + 1) * cfg.S]
        m["xT"] = np.ascontiguousarray(xs.reshape(cfg.NT, 512, 8, 128).transpose(0, 3, 2, 1)).reshape(cfg.NT, 128, 4096)
        m["ctxT"] = np.ascontiguousarray(ctx[b].reshape(256, 8, 128).transpose(2, 1, 0)).reshape(128, 2048)
        maps.append(m)
    return maps


def assemble(cfg, outs):
    y = np.zeros((cfg.NB, cfg.SEQ, D), np.float32)
    for core in range(cfg.NCORES):
        b, j = core // cfg.CPB, core % cfg.CPB
        t = outs[core].reshape(cfg.NT, 128, 8, 512).transpose(0, 3, 2, 1).reshape(cfg.S, D)
        y[b, j * cfg.S:(j + 1) * cfg.S] = t
    return y


def p1_phase(l):
    return lambda P, c, T, al, ps: emit_p1(P, c, T, l, al, ps)


def p2_phase(l):
    return lambda P, c, T, al, ps: emit_p2(P, c, T, l, al, ps)


def run_unfused(cfg, inp, keep=None):
    maps = host_inputs(cfg, inp)
    cores = list(range(cfg.NCORES))
    store = [dict(m) for m in maps]
    LD = cfg.DEPTH

    def launch(phases, ext_in, ext_out):
        nc = build_launch(cfg, phases, ext_in, ext_out)
        res = run_bass_kernel_spmd(nc, [{k: store[c][k] for k in ext_in} for c in cores], core_ids=cores)
        for c in cores:
            for k in ext_out:
                store[c][k] = res.results[c][k]

    raw = sum([RAWW(l) + [f"wmod{l}"] for l in range(LD)], [])
    wb = [n + "b" for l in range(LD) for n in RAWW(l)]
    launch([emit_p0], ["params"] + raw, wb + ["cst"])
    for c in cores:
        for k in raw:
            store[c].pop(k, None)
    for l in range(LD):
        s_in = "xT" if l == 0 else f"sout{l - 1}"
        c_in = "ctxT" if l == 0 else f"cout{l - 1}"
        launch([p1_phase(l)], [s_in, c_in, "params", "cst", "rope", f"wgu{l}0b", f"wdn{l}0b", f"win{l}b", f"winv{l}b"],
               [f"s1_{l}", f"c1_{l}", f"q{l}", f"qc{l}", f"pay{l}", f"payc{l}"])
        for b in range(cfg.NB):
            g = np.concatenate([store[b * cfg.CPB + j][f"pay{l}"] for j in range(cfg.CPB)], axis=0)
            for j in range(cfg.CPB):
                store[b * cfg.CPB + j][f"gath{l}"] = g
        outs = [f"sout{l}"] + ([f"cout{l}"] if l < LD - 1 else [])
        launch([p2_phase(l)], [f"s1_{l}", f"c1_{l}", f"q{l}", f"qc{l}", f"pay{l}", f"payc{l}", f"gath{l}", "params", "cst",
                               "wmask", f"wout{l}b", f"wgu{l}1b", f"wdn{l}1b"], outs)
    if keep is not None:
        keep.extend(store)
    return assemble(cfg, [store[c][f"sout{LD - 1}"] for c in cores])


def kernel(**inputs):
    cfg = Cfg()
    return run_unfused(cfg, inputs)
```

```python
import contextlib
import math
import numpy as np
import ml_dtypes
import concourse.bass as bass
import concourse.mybir as mybir
from concourse.bass_utils import run_bass_kernel_spmd

F32 = mybir.dt.float32
BF16 = mybir.dt.bfloat16
AF = mybir.ActivationFunctionType
ALU = mybir.AluOpType

D = 1024
DFF = 2816
NFF = 22
LCTX = 256
EPS = 1e-6
GRID_W = 64
ROPE_BASE = 10000.0
NWC = 26
LP = 263
LC = 152
SAME_ENGINE_SYNC = True
DEBUG_WHERE = None


class Op:
    __slots__ = ("eng", "fn", "deps", "sig", "isdma", "sem", "val", "id", "where")


def _region(x):
    if isinstance(x, tuple):
        return ("K", x), (0, 1, 0, 1)
    t = x.tensor
    pairs = list(x.ap)
    tn = type(t).__name__
    if tn.startswith("DRam"):
        lo = x.offset
        hi = lo + sum((c - 1) * abs(s) for s, c in pairs) + 1
        return ("D", t.name), (0, 1, lo, hi)
    shape = list(t.shape)
    row = 1
    for s in shape[1:]:
        row *= s
    p0 = x.offset // row
    f0 = x.offset % row
    npart = pairs[0][1]
    f1 = f0 + sum((c - 1) * abs(s) for s, c in pairs[1:]) + 1
    return ("S", t.name), (p0, p0 + npart, f0, f1)


class Prog:
    NDMASEM = 10

    def __init__(self):
        self.ops = []
        self.hist = {}

    def add(self, eng, fn, reads=(), writes=(), isdma=False):
        op = Op()
        op.eng = eng
        op.fn = fn
        op.isdma = isdma
        op.sig = False
        op.id = len(self.ops)
        if DEBUG_WHERE:
            import sys as _s
            f = _s._getframe(2)
            op.where = (f.f_code.co_name, f.f_lineno)
        deps = set()
        for x in reads:
            self._access(op, x, False, deps)
        for x in writes:
            self._access(op, x, True, deps)
        op.deps = deps
        self.ops.append(op)
        return op

    def _access(self, op, x, isw, deps):
        name, reg = _region(x)
        lst = self.hist.get(name)
        if lst is None:
            lst = []
            self.hist[name] = lst
        p0, p1, f0, f1 = reg
        new = []
        for e in lst:
            (q0, q1, g0, g1), eid, ew, eeng, edma = e
            ov = q0 < p1 and p0 < q1 and g0 < f1 and f0 < g1
            if ov and (isw or ew) and eid != op.id:
                deps.add(eid)
            if ov and isw and p0 <= q0 and q1 <= p1 and f0 <= g0 and g1 <= f1:
                continue
            if (not isw) and (not ew) and (edma == op.isdma) and eeng == op.eng \
                    and (q0, q1, g0, g1) == reg and (not edma or name[0] == "D"):
                continue
            new.append(e)
        new.append((reg, op.id, isw, op.eng, op.isdma))
        self.hist[name] = new

    def emit(self, nc, es):
        ops = self.ops
        engs = ["pe", "act", "dve", "pool", "sp"]
        for op in ops:
            for d in op.deps:
                p = ops[d]
                if p.eng == "pe" and op.eng == "pe" and not op.isdma and not p.isdma:
                    continue
                if (not SAME_ENGINE_SYNC) and p.eng == op.eng and not p.isdma and not op.isdma:
                    continue
                p.sig = True
        csem = {e: es.enter_context(nc.semaphore("c_" + e)) for e in ["pe", "act", "dve", "pool"]}
        dsem = {q: [es.enter_context(nc.semaphore(f"d_{q}{i}")) for i in range(self.NDMASEM)]
                for q in ["sp", "pool", "act"]}
        cnt = {e: 0 for e in csem}
        dcnt = {q: 0 for q in dsem}
        for op in ops:
            if op.isdma:
                n = dcnt[op.eng]
                dcnt[op.eng] += 1
                k = n % self.NDMASEM
                op.sem = dsem[op.eng][k]
                op.val = 16 * (n // self.NDMASEM + 1)
                op.sig = True
                if n >= self.NDMASEM:
                    op.deps.add(("prev", op.sem, op.val - 16))
            elif op.sig:
                cnt[op.eng] += 1
                op.sem = csem[op.eng]
                op.val = cnt[op.eng]
        per = {e: [] for e in engs}
        for op in ops:
            per[op.eng].append(op)
        finals = {q: [] for q in dsem}
        for q in dsem:
            n = dcnt[q]
            for k in range(min(n, self.NDMASEM)):
                last = ((n - 1 - k) // self.NDMASEM) * self.NDMASEM + k
                finals[q].append((dsem[q][k], 16 * (last // self.NDMASEM + 1)))

        def run(engname, e):
            known = {}
            for op in per[engname]:
                waits = {}
                for d in op.deps:
                    if isinstance(d, tuple):
                        sem, val = d[1], d[2]
                    else:
                        p = ops[d]
                        if not p.isdma and not op.isdma and p.eng == op.eng:
                            if p.eng == "pe" or not SAME_ENGINE_SYNC:
                                continue
                        sem, val = p.sem, p.val
                    key = id(sem)
                    if waits.get(key, (None, 0))[1] < val:
                        waits[key] = (sem, val)
                for key, (sem, val) in waits.items():
                    if known.get(key, 0) >= val:
                        continue
                    known[key] = val
                    e.wait_ge(sem, val)
                ins = op.fn(e)
                if DEBUG_WHERE and ins.ins.name in DEBUG_WHERE:
                    print("DBG", ins.ins.name, op.where)
                if op.sig:
                    ins.then_inc(op.sem, 16 if op.isdma else 1)
            for q in dsem:
                if q == engname or (engname == "sp" and q == "act"):
                    for sem, val in finals[q]:
                        e.wait_ge(sem, val)

        block = es.enter_context(nc.Block())
        block.sync(lambda e: run("sp", e))
        block.tensor(lambda e: run("pe", e))
        block.scalar(lambda e: run("act", e))
        block.vector(lambda e: run("dve", e))
        block.gpsimd(lambda e: run("pool", e))

    def matmul(self, out, lhsT, rhs, start=True, stop=True, tp=None):
        kw = dict(start=start, stop=stop)
        if tp is not None:
            kw["tile_position"] = tp
        return self.add("pe", lambda e: e.matmul(out, lhsT=lhsT, rhs=rhs, **kw), [lhsT, rhs], [out])

    def activation(self, out, in_, func, scale=1.0, bias=None, eng="act"):
        rd = [in_]
        kw = {}
        if not isinstance(scale, (int, float)):
            rd.append(scale)
        if bias is not None:
            kw["bias"] = bias
            if not isinstance(bias, (int, float)):
                rd.append(bias)
        return self.add(eng, lambda e: e.activation(out=out, in_=in_, func=func, scale=scale, **kw), rd, [out])

    def tt(self, out, a, b, op, eng="dve"):
        return self.add(eng, lambda e: e.tensor_tensor(out=out, in0=a, in1=b, op=op), [a, b], [out])

    def ts(self, out, a, s1, s2, op0, op1=None, eng="dve"):
        rd = [a] + [s for s in (s1, s2) if s is not None and not isinstance(s, (int, float))]
        if op1 is None:
            return self.add(eng, lambda e: e.tensor_scalar(out=out, in0=a, scalar1=s1, scalar2=None, op0=op0), rd, [out])
        return self.add(eng, lambda e: e.tensor_scalar(out=out, in0=a, scalar1=s1, scalar2=s2, op0=op0, op1=op1), rd, [out])

    def stt(self, out, in0, scalar, in1, op0, op1, eng="dve"):
        rd = [in0, in1] + ([] if isinstance(scalar, (int, float)) else [scalar])
        return self.add(eng, lambda e: e.scalar_tensor_tensor(out=out, in0=in0, scalar=scalar, in1=in1, op0=op0, op1=op1), rd, [out])

    def copy(self, out, in_, eng="dve"):
        if eng == "act":
            return self.add("act", lambda e: e.copy(out=out, in_=in_), [in_], [out])
        return self.add(eng, lambda e: e.tensor_copy(out=out, in_=in_), [in_], [out])

    def memset(self, ap, val, eng="dve"):
        return self.add(eng, lambda e: e.memset(ap, val), [], [ap])

    def reduce_sum(self, out, in_, eng="dve"):
        return self.add(eng, lambda e: e.reduce_sum(out=out, in_=in_, axis=mybir.AxisListType.X), [in_], [out])

    def reciprocal(self, out, in_, eng="dve"):
        return self.add(eng, lambda e: e.reciprocal(out=out, in_=in_), [in_], [out])

    def dma(self, out, in_, q="sp", rd=None, wr=None, slow=False):
        rd = [in_] if rd is None else rd
        wr = [out] if wr is None else wr
        if slow:
            return self.add(q, lambda e: e.dma_start(out=out, in_=in_, allow_slow_non_contiguous=True), rd, wr, isdma=True)
        return self.add(q, lambda e: e.dma_start(out=out, in_=in_), rd, wr, isdma=True)


class Cfg:
    def __init__(self, S_LOC=4096, CPB=4, NB=2, DEPTH=2):
        self.S = S_LOC
        self.CPB = CPB
        self.NB = NB
        self.DEPTH = DEPTH
        self.NT = S_LOC // 512
        self.NCORES = CPB * NB
        self.SEQ = S_LOC * CPB
        self.NPAR = DEPTH * LP + 16 + 2 * CPB
        self.NCST = DEPTH * LC
        self.MPC = 72 // self.NCORES if 72 % self.NCORES == 0 else None


def pay_off(S):
    nkb = S // 128
    o = {}
    o["kA"] = 0
    o["kC2"] = 2 * S
    o["kD2"] = 4 * S
    o["hB"] = 6 * S
    o["gB"] = 8 * S
    o["vA"] = 10 * S
    o["vC"] = o["vA"] + nkb * 260
    o["vD"] = o["vC"] + nkb * 130
    o["PW"] = o["vD"] + nkb * 130
    return o


def _rot_perm(n, hd):
    q = hd // 4
    idx = np.arange(n)
    h = idx // hd
    r = idx % hd
    quarter = r // q
    partner = np.where(quarter % 2 == 0, r + q, r - q)
    return h * hd + partner


ROPED = [("aq0", 0, 32), ("aq1", 128, 32), ("ak0", 256, 32), ("ak1", 384, 32), ("cq0", 1536, 64), ("cq1", 1664, 64),
         ("ck", 1792, 64), ("dq0", 2048, 64), ("dq1", 2176, 64), ("dk", 2304, 64)]


def win_columns():
    cols = []
    for _, a, hd in ROPED:
        cols.append(np.arange(a, a + 128))
        cols.append(a + _rot_perm(128, hd))
    cols.append(np.arange(768, 1536))
    cols = np.concatenate(cols)
    assert cols.size == NWC * 128
    vcols = np.concatenate([np.arange(512, 768), np.arange(1920, 2048), np.arange(2432, 2560)])
    return cols, vcols


def host_weights(inp, l):
    out = {}
    for i, (gu, dn) in enumerate([("w_gu1", "w_dn1"), ("w_gu2", "w_dn2")]):
        w = np.asarray(inp[gu][l])
        g = w[:, :DFF].reshape(8, 128, NFF, 128)
        u = w[:, DFF:].reshape(8, 128, NFF, 128)
        gu_ = np.stack([g, u], axis=3)
        out[f"wgu{l}{i}"] = np.ascontiguousarray(gu_.transpose(2, 1, 0, 3, 4)).reshape(NFF, 128, 8 * 256)
        w = np.asarray(inp[dn][l])
        w = w.reshape(NFF, 128, 8, 128)
        out[f"wdn{l}{i}"] = np.ascontiguousarray(w.transpose(2, 1, 0, 3)).reshape(8, 128, NFF * 128)
    cols, vcols = win_columns()
    w = np.asarray(inp["w_in"][l])
    wf = w[:, cols].reshape(8, 128, NWC, 128)
    out[f"win{l}"] = np.ascontiguousarray(wf.transpose(2, 1, 0, 3)).reshape(NWC, 128, 8 * 128)
    wv = w[:, vcols].reshape(8, 128, 512)
    out[f"winv{l}"] = np.ascontiguousarray(wv.transpose(1, 0, 2)).reshape(1, 128, 8 * 512)
    w = np.asarray(inp["w_out"][l])
    wo = np.zeros((14, 128, 8, 128), np.float32)
    def put(ch, rows):
        wo[ch, :rows.shape[0]] = rows.reshape(rows.shape[0], 8, 128)
    for h in range(4):
        put(h, w[h * 64:(h + 1) * 64])
        put(6 + h, w[512 + h * 64:512 + (h + 1) * 64])
        put(10 + h, w[768 + h * 64:768 + (h + 1) * 64])
    put(4, w[256:384])
    put(5, w[384:512])
    out[f"wout{l}"] = np.ascontiguousarray(wo.transpose(2, 1, 0, 3)).reshape(8, 128, 14 * 128)
    w = np.asarray(inp["w_mod"][l])
    wm = w.reshape(8, 128, 72, 128)
    out[f"wmod{l}"] = np.ascontiguousarray(wm.transpose(2, 1, 0, 3)).reshape(72, 128, 8 * 128)
    return out


WSHAPES = {"wgu": (NFF, 128, 2048), "wdn": (8, 128, NFF * 128), "win": (NWC, 128, 1024),
           "winv": (1, 128, 4096), "wout": (8, 128, 14 * 128)}


def host_params(inp, cfg, core):
    b = core // cfg.CPB
    j = core % cfg.CPB
    P = np.zeros((128, cfg.NPAR), np.float32)
    p = np.arange(128)
    for l in range(cfg.DEPTH):
        o = l * LP
        P[:, o:o + 72] = np.asarray(inp["b_mod"][l]).reshape(72, 128).T
        P[:, o + 72:o + 96] = np.asarray(inp["ln_g"][l]).reshape(24, 128).T
        P[:, o + 96:o + 120] = np.asarray(inp["ln_b"][l]).reshape(24, 128).T
        cw = np.asarray(inp["conv_w"][l])
        for cc in range(2):
            for w in range(3):
                P[:, o + 120 + cc * 3 + w] = cw[w, cc * 128:(cc + 1) * 128]
        P[:, o + 126] = np.asarray(inp["subln_w"][l])[p % 64]
        qn = np.asarray(inp["qn_w"][l])
        kn = np.asarray(inp["kn_w"][l])
        pp = _rot_perm(128, 64) % 64
        P[:, o + 127] = qn[p % 64]
        P[:, o + 128] = kn[p % 64]
        P[:, o + 129] = qn[pp]
        P[:, o + 130] = kn[pp]
        P[:, o + 131:o + 135] = np.asarray(inp["sink"][l])[None, :]
        for i, nm in enumerate(["lam_q1", "lam_k1", "lam_q2", "lam_k2"]):
            P[:, o + 135 + 32 * i:o + 135 + 32 * (i + 1)] = np.asarray(inp[nm][l])[None, :]
    o = cfg.DEPTH * LP
    cb = np.asarray(inp["c"][b]).reshape(8, 128).T
    cc = np.asarray(inp["c_ctx"]).reshape(8, 128).T
    P[:, o:o + 16:2] = cb
    P[:, o + 1:o + 16:2] = cc
    o += 16
    if j > 0:
        P[:, o + j - 1] = 1.0
    if j < cfg.CPB - 1:
        P[:, o + cfg.CPB + j + 1] = 1.0
    return P


def host_consts(cfg, core):
    j = core % cfg.CPB
    out = {}
    pos = j * cfg.S + np.arange(cfg.S)
    row = (pos // GRID_W).astype(np.float32)
    col = (pos % GRID_W).astype(np.float32)

    def tab(dim):
        nf = dim // 4
        inv = (ROPE_BASE ** (-np.arange(nf, dtype=np.float32) / nf)).astype(np.float32)
        ar = row[:, None] * inv
        ac = col[:, None] * inv
        ang = np.concatenate([ar, ar, ac, ac], axis=-1)
        sign = np.concatenate([-np.ones(nf), np.ones(nf), -np.ones(nf), np.ones(nf)]).astype(np.float32)
        c = np.cos(ang).astype(np.float32).T
        s = (np.sin(ang).astype(np.float32) * sign).T
        rep = 128 // dim
        return np.tile(c, (rep, 1)), np.tile(s, (rep, 1))
    ca, sa = tab(32)
    ch, sh = tab(64)
    t = np.stack([ca, sa, ch, sh], axis=1)
    t = t.reshape(128, 4, cfg.NT, 512).transpose(2, 0, 1, 3)
    out["rope"] = np.ascontiguousarray(t).astype(np.float32)
    m = np.zeros((128, 6, 4, 128), np.float32)
    jj = np.arange(128)[:, None]
    ii = np.arange(128)[None, :]
    for mm in range(6):
        for qb in range(4):
            d = (mm - 1) - qb
            if d == -1:
                m[:, mm, qb] = (jj >= ii)
            elif d == 0:
                m[:, mm, qb] = 1.0
            elif d == 1:
                m[:, mm, qb] = (jj <= ii)
    out["wmask"] = m.reshape(128, 6 * 512).astype(ml_dtypes.bfloat16)
    return out


N16 = 54 * 1024
N32 = 19 * 1024


class Arena:
    def __init__(self, a16, a32):
        self.a16, self.a32 = a16, a32
        self.o16 = self.o32 = 0

    def bf(self, n):
        ap = self.a16[:, self.o16:self.o16 + n]
        self.o16 += n
        assert self.o16 <= N16, ("bf16 arena overflow", self.o16)
        return ap

    def f32(self, n):
        ap = self.a32[:, self.o32:self.o32 + n]
        self.o32 += n
        assert self.o32 <= N32, ("fp32 arena overflow", self.o32)
        return ap

    def reset(self):
        self.o16 = self.o32 = 0


def v3(ap, inner):
    return ap.rearrange("p (a b) -> p a b", b=inner)


def bank(ps, i, w=512):
    return ps[:, i * 512:i * 512 + w]


RAWW = lambda l: [f"wgu{l}0", f"wdn{l}0", f"win{l}", f"winv{l}", f"wout{l}", f"wgu{l}1", f"wdn{l}1"]


def emit_p0(P, cfg, T, al, ps):
    al.reset()
    inb = [al.f32(4096) for _ in range(2)]
    outb = [al.bf(4096) for _ in range(2)]
    cin = al.f32(24)
    sc = al.f32(24)
    mo = al.f32(cfg.DEPTH * cfg.MPC * 3)
    P.dma(cin, T["call"])
    k = 0
    for l in range(cfg.DEPTH):
        for name in RAWW(l):
            src, dst = T[name + "s"], T[name + "bs"]
            Fd = src.shape[1]
            for c0 in range(0, Fd, 4096):
                n = min(4096, Fd - c0)
                ib, ob = inb[k % 2][:, :n], outb[k % 2][:, :n]
                P.dma(ib, src[:, c0:c0 + n])
                P.copy(ob, ib, eng=("dve" if k % 2 == 0 else "act"))
                P.dma(dst[:, c0:c0 + n], ob, q="pool")
                k += 1
    P.activation(sc, cin, AF.Silu)
    for l in range(cfg.DEPTH):
        psm = bank(ps, l % 2, cfg.MPC * 3).rearrange("p (n w) -> p n w", w=3)
        for g0 in range(0, cfg.MPC, 4):
            ng = min(4, cfg.MPC - g0)
            ib = inb[k % 2]
            k += 1
            P.dma(v3(ib[:, :ng * 1024], 1024), T[f"wmod{l}s"][g0:g0 + ng].rearrange("g p f -> p g f"))
            for g in range(ng):
                for kc in range(8):
                    P.matmul(psm[:, g0 + g, :], ib[:, g * 1024 + kc * 128:g * 1024 + (kc + 1) * 128],
                             sc[:, 3 * kc:3 * kc + 3], start=(kc == 0), stop=(kc == 7))
        P.copy(mo[:, l * cfg.MPC * 3:(l + 1) * cfg.MPC * 3], bank(ps, l % 2, cfg.MPC * 3))
    P.dma(T["modp"], mo, q="pool")


def emit_cst(P, cfg, par, cst, modraw):
    tmp = modraw[:, cfg.DEPTH * 144:cfg.DEPTH * 144 + 32]
    s12 = modraw[:, cfg.DEPTH * 144 + 32:cfg.DEPTH * 144 + 36]
    for l in range(cfg.DEPTH):
        o = l * LP
        oc = l * LC
        for w in range(2):
            mv = cst[:, oc + w * 72:oc + (w + 1) * 72]
            P.tt(mv, modraw[:, l * 144 + w * 72:l * 144 + (w + 1) * 72], par[:, o:o + 72], ALU.add)
            for n in (1, 4, 7):
                P.ts(mv[:, n * 8:n * 8 + 8], mv[:, n * 8:n * 8 + 8], 1.0, None, ALU.add)
            for n in (2, 8):
                P.ts(mv[:, n * 8:n * 8 + 8], mv[:, n * 8:n * 8 + 8], 0.5, None, ALU.mult)
        lam_init = 0.8 - 0.6 * math.exp(-0.3 * l)
        for i in range(2):
            P.tt(tmp, par[:, o + 135 + 64 * i:o + 167 + 64 * i], par[:, o + 167 + 64 * i:o + 199 + 64 * i], ALU.mult)
            P.reduce_sum(s12[:, i:i + 1], tmp)
        P.activation(s12[:, 2:4], s12[:, 0:2], AF.Exp)
        P.tt(cst[:, oc + 144:oc + 145], s12[:, 2:3], s12[:, 3:4], ALU.subtract)
        P.ts(cst[:, oc + 144:oc + 145], cst[:, oc + 144:oc + 145], lam_init, None, ALU.add)
        P.ts(cst[:, oc + 145:oc + 146], cst[:, oc + 144:oc + 145], -1.0, None, ALU.mult)
        P.activation(cst[:, oc + 146:oc + 150], par[:, o + 131:o + 135], AF.Exp)
        P.ts(cst[:, oc + 150:oc + 151], par[:, o + 126:o + 127], 1.0 - lam_init, None, ALU.mult)
        P.memset(cst[:, oc + 151:oc + 152], 0.0)


class FFNBufs:
    def __init__(self, al, TW=512, hT=None, aT=None, wgu=None):
        self.hT = hT if hT is not None else al.bf(8 * TW)
        self.aT = aT if aT is not None else al.bf(NFF * TW)
        wg = wgu if wgu is not None else al.bf(2 * 4096)
        self.wgu = [wg[:, 0:4096], wg[:, 4096:8192]]
        self.wdn_all = al.bf(2 * NFF * 128)
        self.wdn = [self.wdn_all[:, 0:NFF * 128], self.wdn_all[:, NFF * 128:2 * NFF * 128]]
        self.sg = [al.bf(TW) for _ in range(2)]
        self.t = [al.f32(TW) for _ in range(2)]
        self.z = al.f32(8 * TW)
        self.lt = [al.f32(TW) for _ in range(8)]
        self.u = [al.f32(TW) for _ in range(2)]


def emit_rsqrt(P, out, in_):
    P.ts(out, in_, EPS, None, ALU.add)
    P.activation(out, out, AF.Ln)
    P.activation(out, out, AF.Exp, scale=-0.5)


def emit_ln(P, ps, fb, z, TW, lng, lnb, out, ones, bk=(6, 7)):
    acc1, acc2, mean, msq, var, rstd, nmr, sq = [x[:, :TW] for x in fb.lt]
    zc = lambda c: z[:, c * TW:(c + 1) * TW]
    P.tt(acc1, zc(0), zc(1), ALU.add)
    for c in range(2, 8):
        P.tt(acc1, acc1, zc(c), ALU.add)
    for c in range(8):
        if c == 0:
            P.activation(acc2, zc(0), AF.Square)
        else:
            P.activation(sq, zc(c), AF.Square)
            P.tt(acc2, acc2, sq, ALU.add)
    b1, b2 = bank(ps, bk[0], TW), bank(ps, bk[1], TW)
    P.matmul(b1, ones, acc1)
    P.matmul(b2, ones, acc2)
    P.ts(mean, b1, 1.0 / D, None, ALU.mult)
    P.tt(msq, mean, mean, ALU.mult)
    P.stt(var, b2, 1.0 / D, msq, ALU.mult, ALU.subtract)
    emit_rsqrt(P, rstd, var)
    P.stt(nmr, mean, -1.0, rstd, ALU.mult, ALU.mult)
    for c in range(8):
        u = fb.u[c % 2][:, :TW]
        P.tt(u, zc(c), rstd, ALU.mult)
        P.tt(u, u, nmr, ALU.add)
        P.activation(out[:, c * TW:(c + 1) * TW], u, AF.Identity, scale=lng[:, c:c + 1], bias=lnb[:, c:c + 1])


def emit_ffn(P, ps, fb, s, TW, mods, wgu, wdn, lng, lnb, out, ones):
    alpha = (2.0 * 2) ** 0.25
    sc_ = lambda c: s[:, c * TW:(c + 1) * TW]
    hT = lambda c: fb.hT[:, c * TW:(c + 1) * TW]
    for c in range(8):
        P.ts(hT(c), sc_(c), mods[:, 8 + c:9 + c], mods[:, c:c + 1], ALU.mult, ALU.add)
    for jp in range(NFF // 2):
        wb = fb.wgu[jp % 2]
        P.dma(v3(wb, 2048), wgu[2 * jp:2 * jp + 2].rearrange("g p f -> p g f"))
        for jj in range(2):
            j = 2 * jp + jj
            bg, bu = bank(ps, 2 * (j % 2), TW), bank(ps, 2 * (j % 2) + 1, TW)
            for kc in range(8):
                P.matmul(bg, wb[:, jj * 2048 + kc * 256:jj * 2048 + kc * 256 + 128], hT(kc), start=(kc == 0), stop=(kc == 7))
            for kc in range(8):
                P.matmul(bu, wb[:, jj * 2048 + kc * 256 + 128:jj * 2048 + kc * 256 + 256], hT(kc), start=(kc == 0), stop=(kc == 7))
            sg = fb.sg[j % 2][:, :TW]
            P.activation(sg, bg, AF.Silu)
            P.tt(fb.aT[:, j * TW:(j + 1) * TW], sg, bu, ALU.mult)
    for dc in range(8):
        wb = fb.wdn[dc % 2]
        P.dma(wb, wdn[dc])
        by = bank(ps, 4 + dc % 2, TW)
        for j in range(NFF):
            P.matmul(by, wb[:, j * 128:(j + 1) * 128], fb.aT[:, j * TW:(j + 1) * TW], start=(j == 0), stop=(j == NFF - 1))
        t = fb.t[dc % 2][:, :TW]
        P.activation(t, by, AF.Identity, scale=mods[:, 16 + dc:17 + dc])
        P.stt(fb.z[:, dc * TW:(dc + 1) * TW], sc_(dc), alpha, t, ALU.mult, ALU.add)
    emit_ln(P, ps, fb, fb.z, TW, lng, lnb, out, ones)


def emit_p1(P, cfg, T, l, al, ps):
    al.reset()
    S = cfg.S
    par = al.f32(cfg.NPAR)
    cst = al.f32(cfg.NCST)
    modraw = al.f32(cfg.DEPTH * 144 + 36)
    ones = al.f32(128)
    blk = al.f32(128)
    fb = FFNBufs(al)
    sbuf = [al.f32(8 * 512)]
    s1 = fb.z
    rope = al.f32(4 * 512)
    tmp = fb.lt[0:4]
    win = fb.wgu
    winv = fb.wdn_all[:, 0:4096]
    qt = al.bf(6 * 512)
    kt = al.bf(4 * 512)
    bt = al.bf(4 * 512)
    vt = [al.bf(4 * 520) for _ in range(2)]
    P.dma(par, T["params"])
    P.dma(modraw[:, :cfg.DEPTH * 144], T["modraw"])
    emit_cst(P, cfg, par, cst, modraw)
    P.memset(ones, 1.0)
    P.memset(blk, 0.0)
    P.memset(blk[0:64, 0:64], 1.0)
    P.memset(blk[64:128, 64:128], 1.0)
    for b in vt:
        P.memset(b, 1.0)
    o = l * LP
    oc = l * LC
    src_s = T["xT"] if l == 0 else T[f"sout{l - 1}"]
    src_c = T["ctxT"] if l == 0 else T[f"cout{l - 1}"]
    tiles = ["ctx"] + list(range(cfg.NT))
    for it, ti in enumerate(tiles):
        isctx = ti == "ctx"
        TW = 256 if isctx else 512
        w = 1 if isctx else 0
        mods = cst[:, oc + w * 72:oc + (w + 1) * 72]
        s = sbuf[0][:, :8 * TW]
        P.dma(s, src_c if isctx else src_s[ti])
        s1t = s1[:, :8 * TW]
        emit_ffn(P, ps, fb, s, TW, mods[:, 0:24], T[f"wgu{l}0b"], T[f"wdn{l}0b"],
                 par[:, o + 72:o + 80], par[:, o + 96:o + 104], s1t, ones)
        P.dma(T[f"c1_{l}"] if isctx else T[f"s1_{l}"][ti], s1t, q="pool")
        hT = lambda c: fb.hT[:, c * TW:(c + 1) * TW]
        for c in range(8):
            P.ts(hT(c), s1t[:, c * TW:(c + 1) * TW], mods[:, 32 + c:33 + c], mods[:, 24 + c:25 + c], ALU.mult, ALU.add)
        if not isctx:
            P.dma(rope, T["rope"][ti])
        P.dma(winv, T[f"winv{l}b"][0])
        wq = []
        def wchunk(ch):
            g = ch // 4
            if ch % 4 == 0:
                n = min(4, NWC - ch)
                P.dma(v3(win[g % 2][:, :n * 1024], 1024), T[f"win{l}b"][ch:ch + n].rearrange("g p f -> p g f"))
            return win[g % 2][:, (ch % 4) * 1024:(ch % 4 + 1) * 1024]
        bkc = [0]
        def proj(ch):
            wb = wchunk(ch)
            b = bank(ps, bkc[0] % 4, TW)
            bkc[0] += 1
            for kc in range(8):
                P.matmul(b, wb[:, kc * 128:(kc + 1) * 128], hT(kc), start=(kc == 0), stop=(kc == 7))
            return b
        dsts = {"aq0": qt[:, 0:TW], "aq1": qt[:, TW:2 * TW], "cq0": qt[:, 2 * TW:3 * TW], "cq1": qt[:, 3 * TW:4 * TW],
                "dq0": qt[:, 4 * TW:5 * TW], "dq1": qt[:, 5 * TW:6 * TW],
                "ak0": kt[:, 0:TW], "ak1": kt[:, TW:2 * TW], "ck": kt[:, 2 * TW:3 * TW], "dk": kt[:, 3 * TW:4 * TW]}
        for ri, (nm, _, hd) in enumerate(ROPED):
            bz = proj(2 * ri)
            dst = dsts[nm]
            isd = nm.startswith("d")
            t0, t1, t2, t3 = [x[:, :TW] for x in tmp]
            if isd:
                gcol = par[:, o + 127:o + 128] if nm[1] == "q" else par[:, o + 128:o + 129]
                gpcol = par[:, o + 129:o + 130] if nm[1] == "q" else par[:, o + 130:o + 131]
                P.activation(t3, bz, AF.Square)
                bs = bank(ps, 6, TW)
                P.matmul(bs, blk, t3)
                P.ts(t3, bs, 1.0 / 64, None, ALU.mult)
                emit_rsqrt(P, t3, t3)
            if isctx:
                if isd:
                    P.stt(dst, bz, gcol, t3, ALU.mult, ALU.mult)
                else:
                    P.copy(dst, bz, eng="act")
                continue
            br = proj(2 * ri + 1)
            cosT = rope[:, (0 if hd == 32 else 2) * 512:(0 if hd == 32 else 2) * 512 + 512]
            sinT = rope[:, (1 if hd == 32 else 3) * 512:(1 if hd == 32 else 3) * 512 + 512]
            if isd:
                P.stt(t0, bz, gcol, cosT, ALU.mult, ALU.mult)
                P.stt(t1, br, gpcol, sinT, ALU.mult, ALU.mult)
                P.tt(t0, t0, t1, ALU.add)
                P.tt(dst, t0, t3, ALU.mult)
            else:
                P.tt(t0, bz, cosT, ALU.mult)
                P.tt(t1, br, sinT, ALU.mult)
                P.tt(dst, t0, t1, ALU.add)
        for cc in range(2):
            b = proj(20 + cc)
            P.copy(bt[:, (2 + cc) * TW:(3 + cc) * TW], b, eng="act")
        for cc in range(2):
            b = proj(22 + cc)
            P.copy(tmp[cc][:, :TW], b, eng="act")
        for cc in range(2):
            b = proj(24 + cc)
            P.tt(bt[:, cc * TW:(cc + 1) * TW], tmp[cc][:, :TW], b, ALU.mult)
        nsub = TW // 128
        vb = vt[it % 2]
        for sub in range(nsub):
            b = bank(ps, bkc[0] % 4, 512)
            bkc[0] += 1
            for kc in range(8):
                P.matmul(b, fb.hT[:, kc * TW + sub * 128:kc * TW + (sub + 1) * 128], winv[:, kc * 512:(kc + 1) * 512],
                         start=(kc == 0), stop=(kc == 7))
            P.copy(v3(vb[:, sub * 520:(sub + 1) * 520], 65)[:, :, 0:64], v3(b, 64), eng="act")
        if isctx:
            pay, po, SS, c0, kb0 = T[f"payc{l}"], pay_off(LCTX), LCTX, 0, 0
            P.dma(T[f"qc{l}"], qt[:, :6 * TW], q="pool")
        else:
            pay, po, SS, c0, kb0 = T[f"pay{l}"], pay_off(S), S, ti * 512, ti * 4
            P.dma(T[f"q{l}"][ti], qt, q="pool")
        P.dma(v3(pay[:, po["kA"]:po["kA"] + 2 * SS], SS)[:, :, c0:c0 + TW], v3(kt[:, 0:2 * TW], TW), q="pool")
        for nm, ki in (("kC2", 2), ("kD2", 3)):
            for kvs in range(2):
                for half in range(2):
                    P.dma(pay[64 * half:64 * half + 64, po[nm] + kvs * SS + c0:po[nm] + kvs * SS + c0 + TW],
                          kt[64 * kvs:64 * kvs + 64, ki * TW:(ki + 1) * TW], q="pool")
        P.dma(v3(pay[:, po["hB"]:po["hB"] + 4 * SS], SS)[:, :, c0:c0 + TW], v3(bt[:, 0:4 * TW], TW), q="pool")
        vb3 = v3(vb[:, :nsub * 520], 520)
        P.dma(v3(pay[:, po["vA"] + kb0 * 260:po["vA"] + (kb0 + nsub) * 260], 260), vb3[:, :, 0:260], q="pool")
        P.dma(v3(pay[:, po["vC"] + kb0 * 130:po["vC"] + (kb0 + nsub) * 130], 130), vb3[:, :, 260:390], q="pool")
        P.dma(v3(pay[:, po["vD"] + kb0 * 130:po["vD"] + (kb0 + nsub) * 130], 130), vb3[:, :, 390:520], q="pool")


def tensor_specs(cfg):
    sp = {}
    NT, S = cfg.NT, cfg.S
    sp["xT"] = ([NT, 128, 4096], F32)
    sp["ctxT"] = ([128, 2048], F32)
    sp["params"] = ([128, cfg.NPAR], F32)
    sp["rope"] = ([NT, 128, 2048], F32)
    sp["wmask"] = ([128, 3072], BF16)
    sp["modraw"] = ([128, cfg.DEPTH * 144], F32)
    sp["call"] = ([128, 24], F32)
    sp["modp"] = ([128, cfg.DEPTH * cfg.MPC * 3], F32)
    for l in range(cfg.DEPTH):
        for i in range(2):
            sp[f"wgu{l}{i}"] = (list(WSHAPES["wgu"]), F32)
            sp[f"wdn{l}{i}"] = (list(WSHAPES["wdn"]), F32)
        sp[f"win{l}"] = (list(WSHAPES["win"]), F32)
        sp[f"winv{l}"] = (list(WSHAPES["winv"]), F32)
        sp[f"wout{l}"] = (list(WSHAPES["wout"]), F32)
        sp[f"wmod{l}s"] = ([cfg.MPC, 128, 1024], F32)
        for nm in RAWW(l):
            ne = int(np.prod(WSHAPES[nm[:-2] if nm[-2:].isdigit() else nm[:-1]])) // (cfg.NCORES * 128)
            sp[nm + "s"] = ([128, ne], F32)
            sp[nm + "bs"] = ([128, ne], BF16)
        for nm in [f"wgu{l}0", f"wgu{l}1", f"wdn{l}0", f"wdn{l}1", f"win{l}", f"winv{l}", f"wout{l}"]:
            sp[nm + "b"] = (sp[nm][0], BF16)
        sp[f"s1_{l}"] = ([NT, 128, 4096], F32)
        sp[f"c1_{l}"] = ([128, 2048], F32)
        sp[f"q{l}"] = ([NT, 128, 3072], BF16)
        sp[f"qc{l}"] = ([128, 1536], BF16)
        sp[f"pay{l}"] = ([128, pay_off(S)["PW"]], BF16)
        sp[f"payc{l}"] = ([128, pay_off(LCTX)["PW"]], BF16)
        sp[f"gath{l}"] = ([cfg.CPB * 128, pay_off(S)["PW"]], BF16)
        sp[f"sout{l}"] = ([NT, 128, 4096], F32)
        sp[f"cout{l}"] = ([128, 2048], F32)
    return sp


def build_launch(cfg, phases, ext_in, ext_out, internal=()):
    nc = bass.Bass("TRN2", target_bir_lowering=False)
    sp = tensor_specs(cfg)
    T = {}
    for nm in ext_in:
        T[nm] = nc.dram_tensor(nm, sp[nm][0], sp[nm][1], kind="ExternalInput").ap()
    for nm in ext_out:
        T[nm] = nc.dram_tensor(nm, sp[nm][0], sp[nm][1], kind="ExternalOutput").ap()
    for nm in internal:
        T[nm] = nc.dram_tensor(nm, sp[nm][0], sp[nm][1]).ap()
    with contextlib.ExitStack() as es:
        a16 = es.enter_context(nc.sbuf_tensor("a16", [128, N16], BF16))
        a32 = es.enter_context(nc.sbuf_tensor("a32", [128, N32], F32))
        ps = es.enter_context(nc.psum_tensor("ps", [128, 4096], F32))
        al = Arena(a16, a32)
        P = Prog()
        for ph in phases:
            ph(P, cfg, T, al, ps)
        P.emit(nc, es)
    return nc


def emit_attn(P, ps, N, qs, tps, blocks, scale, PT):
    blocks = list(blocks) if not hasattr(blocks, "__next__") else blocks
    prev = None
    i = 0

    def pv(pt, vs, first, last):
        for t in range(4):
            P.matmul(ps[0:65, (4 + t) * 512:(4 + t) * 512 + N], vs[t], pt[:, t * N:(t + 1) * N], start=first, stop=last)

    for ks, vs, mask in blocks:
        pt = PT[i % 2][:, :4 * N]
        for t in range(4):
            P.matmul(ps[:, t * 512:t * 512 + N], ks[t], qs[t], tp=tps[t])
        for half in range(2):
            if N == 512:
                P.activation(pt[:, half * 1024:(half + 1) * 1024], ps[:, half * 1024:(half + 1) * 1024], AF.Exp, scale=scale)
            else:
                P.activation(v3(pt[:, half * 2 * N:(half + 1) * 2 * N], N), v3(ps[:, half * 1024:(half + 1) * 1024], 512)[:, :, 0:N],
                             AF.Exp, scale=scale)
        if mask is not None:
            for t in range(4):
                P.tt(pt[:, t * N:(t + 1) * N], pt[:, t * N:(t + 1) * N], mask, ALU.mult)
        if prev is not None:
            pv(prev[0], prev[1], prev[2] == 0, False)
        prev = (pt, vs, i)
        i += 1
    pv(prev[0], prev[1], prev[2] == 0, True)


def emit_p2(P, cfg, T, l, al, ps):
    al.reset()
    S, CPB, NKB = cfg.S, cfg.CPB, cfg.S // 128
    need_ctx = l < cfg.DEPTH - 1
    po, pc = pay_off(S), pay_off(LCTX)
    o, oc = l * LP, l * LC
    alpha = (2.0 * 2) ** 0.25
    par = al.f32(cfg.NPAR)
    cst = al.f32(cfg.NCST)
    modraw = al.f32(cfg.DEPTH * 144 + 36)
    ones = al.f32(128)
    sel = al.f32(64)
    s1t = al.f32(8 * 512)
    R1 = al.bf(11312)
    R2 = al.bf(8192)
    R3 = al.bf(4096)
    fb = FFNBufs(al, hT=R3, aT=R1[:, :NFF * 512], wgu=R2)
    kvb = [R1[:, i * 1544:(i + 1) * 1544] for i in range(3)]
    PT = [R1[:, 4632 + i * 2048:4632 + (i + 1) * 2048] for i in range(2)]
    yT = R2[:, :14 * 512]
    qt = R3[:, :6 * 512]
    wmask = al.bf(3072)
    kCx = [al.bf(S + 256) for _ in range(2)]
    vCx = al.bf((NKB + 2) * 130)
    kCc = al.bf(512)
    vCc = al.bf(260)
    wout = [al.bf(14 * 128) for _ in range(2)]
    hx = al.bf(2 * 514)
    gbt = al.bf(2 * 512)
    cand = al.bf(CPB * 130)
    candh = al.bf(4 * CPB)
    hal = al.bf(4)
    lt = fb.lt
    pay, payc, gath = T[f"pay{l}"], T[f"payc{l}"], T[f"gath{l}"]
    gath3 = gath.rearrange("(r p) c -> p r c", p=128)
    wsel = lambda side, r: par[:, cfg.DEPTH * LP + 16 + side * CPB + r:cfg.DEPTH * LP + 17 + side * CPB + r]

    P.dma(par, T["params"])
    P.dma(modraw[:, :cfg.DEPTH * 144], T["modraw"])
    emit_cst(P, cfg, par, cst, modraw)
    P.dma(wmask, T["wmask"])
    P.memset(ones, 1.0)
    P.memset(sel, 0.0)
    P.memset(sel[64:65, :], 1.0)

    def combine(dst, srcs, side):
        P.ts(dst, srcs[0], wsel(side, 0), None, ALU.mult)
        for r in range(1, CPB):
            P.stt(dst, srcs[r], wsel(side, r), dst, ALU.mult, ALU.add)

    for kvs in range(2):
        P.dma(kCx[kvs][:, 128:128 + S], pay[:, po["kC2"] + kvs * S:po["kC2"] + (kvs + 1) * S])
        for side in range(2):
            c_src = po["kC2"] + kvs * S + (S - 128 if side == 0 else 0)
            c_dst = 0 if side == 0 else 128 + S
            P.dma(v3(cand[:, :CPB * 128], 128), gath3[:, :, c_src:c_src + 128])
            combine(kCx[kvs][:, c_dst:c_dst + 128], [cand[:, r * 128:(r + 1) * 128] for r in range(CPB)], side)
    P.dma(vCx[:, 130:130 * (NKB + 1)], pay[:, po["vC"]:po["vC"] + NKB * 130])
    for side in range(2):
        c_src = po["vC"] + ((NKB - 1) * 130 if side == 0 else 0)
        c_dst = 0 if side == 0 else (NKB + 1) * 130
        P.dma(v3(cand[:, :CPB * 130], 130), gath3[:, :, c_src:c_src + 130])
        combine(vCx[:, c_dst:c_dst + 130], [cand[:, r * 130:(r + 1) * 130] for r in range(CPB)], side)
    for r in range(CPB):
        for cc in range(2):
            for side in range(2):
                c_src = po["hB"] + cc * S + (S - 1 if side == 0 else 0)
                k = r * 4 + cc * 2 + side
                P.dma(candh[:, k:k + 1], gath[r * 128:(r + 1) * 128, c_src:c_src + 1], slow=True)
    for cc in range(2):
        for side in range(2):
            combine(hal[:, cc * 2 + side:cc * 2 + side + 1],
                    [candh[:, r * 4 + cc * 2 + side:r * 4 + cc * 2 + side + 1] for r in range(CPB)], side)
    P.dma(v3(kCc, 256), v3(payc[:, pc["kC2"]:pc["kC2"] + 512], 256))
    P.dma(vCc, payc[:, pc["vC"]:pc["vC"] + 260])

    nlam = cst[:, oc + 145:oc + 146]
    ysub = cst[:, oc + 150:oc + 151]

    ft = [al.f32(512) for _ in range(6)]
    tmps = lt + fb.u + fb.t + ft

    def fin_std(N, t, ydst, esink=None):
        o_s = tmps[2 * t][0:65, :N]
        r = tmps[2 * t + 1][0:64, :N]
        P.copy(o_s, ps[0:65, (4 + t) * 512:(4 + t) * 512 + N], eng="act")
        zb = ps[0:64, t * 512:t * 512 + N]
        P.matmul(zb, sel[0:65, 0:64], o_s)
        if esink is not None:
            P.ts(r, zb, esink, None, ALU.add)
            P.reciprocal(r, r)
        else:
            P.reciprocal(r, zb)
        P.tt(ydst, o_s[0:64, :], r, ALU.mult)

    def fin_A(N, ydsts):
        T_ = [[x[:, :N] for x in tmps[7 * hl:7 * hl + 7]] for hl in range(2)]
        bks = [(0, 1, 2), (3, 4, 5)]
        for hl in range(2):
            P.copy(T_[hl][0][0:65, :], ps[0:65, (4 + 2 * hl) * 512:(4 + 2 * hl) * 512 + N], eng="act")
            P.copy(T_[hl][1][0:65, :], ps[0:65, (5 + 2 * hl) * 512:(5 + 2 * hl) * 512 + N], eng="act")
        zz = []
        for hl in range(2):
            z1, z2, ssb = [ps[0:64, b * 512:b * 512 + N] for b in bks[hl]]
            P.matmul(z1, sel[0:65, 0:64], T_[hl][0][0:65, :])
            P.matmul(z2, sel[0:65, 0:64], T_[hl][1][0:65, :])
            zz.append((z1, z2, ssb))
        for hl in range(2):
            o1, o2, r1, r2, t1, t2, rs = [x[0:64, :] for x in T_[hl]]
            z1, z2, ssb = zz[hl]
            P.reciprocal(r1, z1)
            P.reciprocal(r2, z2)
            P.tt(t1, o1, r1, ALU.mult)
            P.tt(t2, o2, r2, ALU.mult)
            P.stt(t1, t2, nlam[0:64, :], t1, ALU.mult, ALU.add)
            P.tt(t2, t1, t1, ALU.mult)
            P.matmul(ssb, ones[0:64, 0:64], t2)
        for hl in range(2):
            o1, o2, r1, r2, t1, t2, rs = [x[0:64, :] for x in T_[hl]]
            P.ts(rs, zz[hl][2], 1.0 / 64, None, ALU.mult)
            emit_rsqrt(P, rs, rs)
            P.stt(ydsts[hl], t1, ysub[0:64, :], rs, ALU.mult, ALU.mult)

    tiles = (["ctx"] if need_ctx else []) + list(range(cfg.NT))
    for it, ti in enumerate(tiles):
        isctx = ti == "ctx"
        N = 256 if isctx else 512
        w = 1 if isctx else 0
        mods = cst[:, oc + w * 72:oc + (w + 1) * 72]
        P.dma(qt[:, :6 * N], T[f"qc{l}"] if isctx else T[f"q{l}"][ti])
        P.dma(s1t[:, :8 * N], T[f"c1_{l}"] if isctx else T[f"s1_{l}"][ti])
        yc = lambda c, rows=128: yT[0:rows, c * N:(c + 1) * N]
        scs = [(payc, pc, LCTX, 0, 256)]
        if not isctx:
            for r in range(CPB):
                for sc in range(S // 512):
                    scs.append((gath[r * 128:(r + 1) * 128, :], po, S, sc * 512, 512))
        cnt = [0]

        def gen(kind, hp):
            def load(j):
                src, od, SS, k0, nk = scs[j]
                buf = kvb[cnt[0] % 3]
                cnt[0] += 1
                nsub = nk // 128
                kb0 = k0 // 128
                if kind == "A":
                    P.dma(buf[:, 0:nk], src[:, od["kA"] + hp * SS + k0:od["kA"] + hp * SS + k0 + nk])
                    P.dma(v3(buf[:, 1024:1024 + nsub * 130], 130),
                          v3(src[:, od["vA"] + kb0 * 260:od["vA"] + (kb0 + nsub) * 260], 260)[:, :, hp * 130:(hp + 1) * 130])
                else:
                    P.dma(v3(buf[:, 0:1024], 512)[:, :, 0:nk], v3(src[:, od["kD2"]:od["kD2"] + 2 * SS], SS)[:, :, k0:k0 + nk])
                    P.dma(buf[:, 1024:1024 + nsub * 130], src[:, od["vD"] + kb0 * 130:od["vD"] + (kb0 + nsub) * 130])
                return buf
            nxt = load(0)
            for j in range(len(scs)):
                buf = nxt
                if j + 1 < len(scs):
                    nxt = load(j + 1)
                for sub in range(scs[j][4] // 128):
                    if kind == "A":
                        ks = [buf[32 * t:32 * t + 32, sub * 128:(sub + 1) * 128] for t in range(4)]
                        vs = [buf[:, 1024 + sub * 130 + (t // 2) * 65:1024 + sub * 130 + (t // 2) * 65 + 65] for t in range(4)]
                    else:
                        ks = [buf[64 * (t % 2):64 * (t % 2) + 64, (t // 2) * 512 + sub * 128:(t // 2) * 512 + (sub + 1) * 128] for t in range(4)]
                        vs = [buf[:, 1024 + sub * 130 + (t // 2) * 65:1024 + sub * 130 + (t // 2) * 65 + 65] for t in range(4)]
                    yield ks, vs, None

        for hp in range(2):
            qs = [qt[32 * t:32 * t + 32, hp * N:(hp + 1) * N] for t in range(4)]
            emit_attn(P, ps, N, qs, [(32 * t, 0) for t in range(4)], gen("A", hp), 32 ** -0.5, PT)
            fin_A(N, [yc(2 * hp, 64), yc(2 * hp + 1, 64)])
        qs = [qt[64 * (t % 2):64 * (t % 2) + 64, (4 + t // 2) * N:(5 + t // 2) * N] for t in range(4)]
        tpd = [(64 * (t % 2), 0) for t in range(4)]
        emit_attn(P, ps, N, qs, tpd, gen("D", 0), 64 ** -0.5, PT)
        for t in range(4):
            fin_std(N, t, yc(10 + t, 64))
        qs = [qt[64 * (t % 2):64 * (t % 2) + 64, (2 + t // 2) * N:(3 + t // 2) * N] for t in range(4)]
        blocks = []
        for j in range(2):
            ks = [kCc[64 * (t % 2):64 * (t % 2) + 64, (t // 2) * 256 + j * 128:(t // 2) * 256 + (j + 1) * 128] for t in range(4)]
            vs = [vCc[:, j * 130 + (t // 2) * 65:j * 130 + (t // 2) * 65 + 65] for t in range(4)]
            blocks.append((ks, vs, None))
        if not isctx:
            for m in range(6):
                e = 4 * ti + m
                ks = [kCx[t // 2][64 * (t % 2):64 * (t % 2) + 64, e * 128:(e + 1) * 128] for t in range(4)]
                vs = [vCx[:, e * 130 + (t // 2) * 65:e * 130 + (t // 2) * 65 + 65] for t in range(4)]
                blocks.append((ks, vs, wmask[:, m * 512:(m + 1) * 512]))
        emit_attn(P, ps, N, qs, tpd, blocks, 64 ** -0.5, PT)
        for t in range(4):
            fin_std(N, t, yc(6 + t, 64), esink=cst[0:64, oc + 146 + t:oc + 147 + t])
        if isctx:
            P.memset(hx, 0.0)
            P.dma(v3(hx, 514)[:, :, 1:N + 1], v3(payc[:, pc["hB"]:pc["hB"] + 2 * LCTX], LCTX))
            P.dma(v3(gbt[:, :2 * N], N), v3(payc[:, pc["gB"]:pc["gB"] + 2 * LCTX], LCTX))
        else:
            c0 = ti * 512
            lo = max(c0 - 1, 0)
            hi = min(c0 + N + 1, S)
            P.dma(v3(hx, 514)[:, :, lo - (c0 - 1):hi - (c0 - 1)], v3(pay[:, po["hB"]:po["hB"] + 2 * S], S)[:, :, lo:hi])
            for cc in range(2):
                if c0 == 0:
                    P.copy(hx[:, cc * 514:cc * 514 + 1], hal[:, cc * 2:cc * 2 + 1])
                if c0 + N == S:
                    P.copy(hx[:, cc * 514 + N + 1:cc * 514 + N + 2], hal[:, cc * 2 + 1:cc * 2 + 2])
            P.dma(v3(gbt, 512), v3(pay[:, po["gB"]:po["gB"] + 2 * S], S)[:, :, c0:c0 + N])
        for cc in range(2):
            acc = lt[cc][:, :N]
            cw = lambda wi: par[:, o + 120 + cc * 3 + wi:o + 121 + cc * 3 + wi]
            P.ts(acc, hx[:, cc * 514:cc * 514 + N], cw(0), None, ALU.mult)
            P.stt(acc, hx[:, cc * 514 + 1:cc * 514 + N + 1], cw(1), acc, ALU.mult, ALU.add)
            P.stt(acc, hx[:, cc * 514 + 2:cc * 514 + N + 2], cw(2), acc, ALU.mult, ALU.add)
            P.tt(yc(4 + cc), acc, gbt[:, cc * N:(cc + 1) * N], ALU.mult)
        for dc in range(8):
            wb = wout[dc % 2]
            P.dma(wb, T[f"wout{l}b"][dc])
            by = bank(ps, 4 + dc % 2, N)
            for c in range(14):
                rows = 128 if c in (4, 5) else 64
                P.matmul(by, wb[0:rows, c * 128:(c + 1) * 128], yc(c, rows), start=(c == 0), stop=(c == 13))
            t = fb.t[dc % 2][:, :N]
            P.activation(t, by, AF.Identity, scale=mods[:, 40 + dc:41 + dc])
            P.stt(fb.z[:, dc * N:(dc + 1) * N], s1t[:, dc * N:(dc + 1) * N], alpha, t, ALU.mult, ALU.add)
        z = fb.z[:, :8 * N]
        emit_ln(P, ps, fb, z, N, par[:, o + 80:o + 88], par[:, o + 104:o + 112], z, ones)
        emit_ffn(P, ps, fb, z, N, mods[:, 48:72], T[f"wgu{l}1b"], T[f"wdn{l}1b"],
                 par[:, o + 88:o + 96], par[:, o + 112:o + 120], z, ones)
        P.dma(T[f"cout{l}"] if isctx else T[f"sout{l}"][ti], z, q="pool")


def host_inputs(cfg, inp):
    W = {}
    for l in range(cfg.DEPTH):
        W.update(host_weights(inp, l))
    maps = []
    x = np.asarray(inp["x"], np.float32)
    ctx = np.asarray(inp["ctx"], np.float32)
    for core in range(cfg.NCORES):
        b, j = core // cfg.CPB, core % cfg.CPB
        m = dict(W)
        m["params"] = host_params(inp, cfg, core)
        m.update(host_consts(cfg, core))
        xs = x[b, j * cfg.S:(j + 1) * cfg.S]
        m["xT"] = np.ascontiguousarray(xs.reshape(cfg.NT, 512, 8, 128).transpose(0, 3, 2, 1)).reshape(cfg.NT, 128, 4096)
        m["ctxT"] = np.ascontiguousarray(ctx[b].reshape(256, 8, 128).transpose(2, 1, 0)).reshape(128, 2048)
        maps.append(m)
    return maps


def assemble(cfg, outs):
    y = np.zeros((cfg.NB, cfg.SEQ, D), np.float32)
    for core in range(cfg.NCORES):
        b, j = core // cfg.CPB, core % cfg.CPB
        t = outs[core].reshape(cfg.NT, 128, 8, 512).transpose(0, 3, 2, 1).reshape(cfg.S, D)
        y[b, j * cfg.S:(j + 1) * cfg.S] = t
    return y


def p1_phase(l):
    return lambda P, c, T, al, ps: emit_p1(P, c, T, l, al, ps)


def p2_phase(l):
    return lambda P, c, T, al, ps: emit_p2(P, c, T, l, al, ps)


def run_unfused(cfg, inp, keep=None):
    maps = host_inputs(cfg, inp)
    cores = list(range(cfg.NCORES))
    NC = cfg.NCORES
    store = [dict(m) for m in maps]
    LD = cfg.DEPTH

    def launch(phases, ext_in, ext_out):
        nc = build_launch(cfg, phases, ext_in, ext_out)
        res = run_bass_kernel_spmd(nc, [{k: store[c][k] for k in ext_in} for c in cores], core_ids=cores)
        for c in cores:
            for k in ext_out:
                store[c][k] = res.results[c][k]

    raw = [n for l in range(LD) for n in RAWW(l)]
    call = np.zeros((128, 8, 3), np.float32)
    for b in range(2):
        call[:, :, b] = np.asarray(inp["c"][b]).reshape(8, 128).T
    call[:, :, 2] = np.asarray(inp["c_ctx"]).reshape(8, 128).T
    call = call.reshape(128, 24)
    for c in cores:
        for n in raw:
            flat = store[c][n].reshape(NC, 128, -1)
            store[c][n + "s"] = np.ascontiguousarray(flat[c])
        for l in range(LD):
            store[c][f"wmod{l}s"] = np.ascontiguousarray(store[c][f"wmod{l}"][c * cfg.MPC:(c + 1) * cfg.MPC])
        store[c]["call"] = call
    launch([emit_p0], ["call"] + [n + "s" for n in raw] + [f"wmod{l}s" for l in range(LD)],
           [n + "bs" for n in raw] + ["modp"])
    sp = tensor_specs(cfg)
    for n in raw:
        full = np.stack([store[c][n + "bs"] for c in cores], axis=0).reshape(sp[n + "b"][0])
        for c in cores:
            store[c][n + "b"] = full
    modp = np.stack([store[c]["modp"].reshape(128, LD, cfg.MPC, 3) for c in cores], axis=2)
    modp = modp.reshape(128, LD, 72, 3)
    for c in cores:
        b = c // cfg.CPB
        mr = np.stack([modp[..., b], modp[..., 2]], axis=2)
        store[c]["modraw"] = np.ascontiguousarray(mr).reshape(128, LD * 144)
        for n in raw:
            store[c].pop(n, None)
            store[c].pop(n + "s", None)
    for l in range(LD):
        s_in = "xT" if l == 0 else f"sout{l - 1}"
        c_in = "ctxT" if l == 0 else f"cout{l - 1}"
        launch([p1_phase(l)], [s_in, c_in, "params", "modraw", "rope", f"wgu{l}0b", f"wdn{l}0b", f"win{l}b", f"winv{l}b"],
               [f"s1_{l}", f"c1_{l}", f"q{l}", f"qc{l}", f"pay{l}", f"payc{l}"])
        for b in range(cfg.NB):
            g = np.concatenate([store[b * cfg.CPB + j][f"pay{l}"] for j in range(cfg.CPB)], axis=0)
            for j in range(cfg.CPB):
                store[b * cfg.CPB + j][f"gath{l}"] = g
        outs = [f"sout{l}"] + ([f"cout{l}"] if l < LD - 1 else [])
        launch([p2_phase(l)], [f"s1_{l}", f"c1_{l}", f"q{l}", f"qc{l}", f"pay{l}", f"payc{l}", f"gath{l}", "params", "modraw",
                               "wmask", f"wout{l}b", f"wgu{l}1b", f"wdn{l}1b"], outs)
    if keep is not None:
        keep.extend(store)
    return assemble(cfg, [store[c][f"sout{LD - 1}"] for c in cores])


def kernel(**inputs):
    cfg = Cfg()
    return run_unfused(cfg, inputs)
```

```python
import contextlib
import math
import numpy as np
import ml_dtypes
import concourse.bass as bass
import concourse.mybir as mybir
from concourse.bass_utils import run_bass_kernel_spmd

F32 = mybir.dt.float32
BF16 = mybir.dt.bfloat16
AF = mybir.ActivationFunctionType
ALU = mybir.AluOpType

D = 1024
DFF = 2816
NFF = 22
LCTX = 256
EPS = 1e-6
GRID_W = 64
ROPE_BASE = 10000.0
NWC = 26
LP = 263
LC = 152
SAME_ENGINE_SYNC = True
DEBUG_WHERE = None


class Op:
    __slots__ = ("eng", "fn", "deps", "sig", "isdma", "sem", "val", "id", "where")


def _region(x):
    if isinstance(x, tuple):
        return ("K", x), (0, 1, 0, 1)
    t = x.tensor
    pairs = list(x.ap)
    tn = type(t).__name__
    if tn.startswith("DRam"):
        lo = x.offset
        hi = lo + sum((c - 1) * abs(s) for s, c in pairs) + 1
        return ("D", t.name), (0, 1, lo, hi)
    shape = list(t.shape)
    row = 1
    for s in shape[1:]:
        row *= s
    p0 = x.offset // row
    f0 = x.offset % row
    npart = pairs[0][1]
    f1 = f0 + sum((c - 1) * abs(s) for s, c in pairs[1:]) + 1
    return ("S", t.name), (p0, p0 + npart, f0, f1)


class Prog:
    NDMASEM = 10

    def __init__(self):
        self.ops = []
        self.hist = {}

    def add(self, eng, fn, reads=(), writes=(), isdma=False):
        op = Op()
        op.eng = eng
        op.fn = fn
        op.isdma = isdma
        op.sig = False
        op.id = len(self.ops)
        if DEBUG_WHERE:
            import sys as _s
            f = _s._getframe(2)
            op.where = (f.f_code.co_name, f.f_lineno)
        deps = set()
        for x in reads:
            self._access(op, x, False, deps)
        for x in writes:
            self._access(op, x, True, deps)
        op.deps = deps
        self.ops.append(op)
        return op

    def _access(self, op, x, isw, deps):
        name, reg = _region(x)
        lst = self.hist.get(name)
        if lst is None:
            lst = []
            self.hist[name] = lst
        p0, p1, f0, f1 = reg
        new = []
        for e in lst:
            (q0, q1, g0, g1), eid, ew, eeng, edma = e
            ov = q0 < p1 and p0 < q1 and g0 < f1 and f0 < g1
            if ov and (isw or ew) and eid != op.id:
                deps.add(eid)
            if ov and isw and p0 <= q0 and q1 <= p1 and f0 <= g0 and g1 <= f1:
                continue
            if (not isw) and (not ew) and (edma == op.isdma) and eeng == op.eng \
                    and (q0, q1, g0, g1) == reg and (not edma or name[0] == "D"):
                continue
            new.append(e)
        new.append((reg, op.id, isw, op.eng, op.isdma))
        self.hist[name] = new

    def emit(self, nc, es):
        ops = self.ops
        engs = ["pe", "act", "dve", "pool", "sp"]
        for op in ops:
            for d in op.deps:
                p = ops[d]
                if p.eng == "pe" and op.eng == "pe" and not op.isdma and not p.isdma:
                    continue
                if (not SAME_ENGINE_SYNC) and p.eng == op.eng and not p.isdma and not op.isdma:
                    continue
                p.sig = True
        csem = {e: es.enter_context(nc.semaphore("c_" + e)) for e in ["pe", "act", "dve", "pool"]}
        dsem = {q: [es.enter_context(nc.semaphore(f"d_{q}{i}")) for i in range(self.NDMASEM)]
                for q in ["sp", "pool", "act"]}
        cnt = {e: 0 for e in csem}
        dcnt = {q: 0 for q in dsem}
        for op in ops:
            if op.isdma:
                n = dcnt[op.eng]
                dcnt[op.eng] += 1
                k = n % self.NDMASEM
                op.sem = dsem[op.eng][k]
                op.val = 16 * (n // self.NDMASEM + 1)
                op.sig = True
                if n >= self.NDMASEM:
                    op.deps.add(("prev", op.sem, op.val - 16))
            elif op.sig:
                cnt[op.eng] += 1
                op.sem = csem[op.eng]
                op.val = cnt[op.eng]
        per = {e: [] for e in engs}
        for op in ops:
            per[op.eng].append(op)
        finals = {q: [] for q in dsem}
        for q in dsem:
            n = dcnt[q]
            for k in range(min(n, self.NDMASEM)):
                last = ((n - 1 - k) // self.NDMASEM) * self.NDMASEM + k
                finals[q].append((dsem[q][k], 16 * (last // self.NDMASEM + 1)))

        def run(engname, e):
            known = {}
            for op in per[engname]:
                waits = {}
                for d in op.deps:
                    if isinstance(d, tuple):
                        sem, val = d[1], d[2]
                    else:
                        p = ops[d]
                        if not p.isdma and not op.isdma and p.eng == op.eng:
                            if p.eng == "pe" or not SAME_ENGINE_SYNC:
                                continue
                        sem, val = p.sem, p.val
                    key = id(sem)
                    if waits.get(key, (None, 0))[1] < val:
                        waits[key] = (sem, val)
                for key, (sem, val) in waits.items():
                    if known.get(key, 0) >= val:
                        continue
                    known[key] = val
                    e.wait_ge(sem, val)
                ins = op.fn(e)
                if DEBUG_WHERE and ins.ins.name in DEBUG_WHERE:
                    print("DBG", ins.ins.name, op.where)
                if op.sig:
                    ins.then_inc(op.sem, 16 if op.isdma else 1)
            for q in dsem:
                if q == engname or (engname == "sp" and q == "act"):
                    for sem, val in finals[q]:
                        e.wait_ge(sem, val)

        block = es.enter_context(nc.Block())
        block.sync(lambda e: run("sp", e))
        block.tensor(lambda e: run("pe", e))
        block.scalar(lambda e: run("act", e))
        block.vector(lambda e: run("dve", e))
        block.gpsimd(lambda e: run("pool", e))

    def matmul(self, out, lhsT, rhs, start=True, stop=True, tp=None):
        kw = dict(start=start, stop=stop)
        if tp is not None:
            kw["tile_position"] = tp
        return self.add("pe", lambda e: e.matmul(out, lhsT=lhsT, rhs=rhs, **kw), [lhsT, rhs], [out])

    def activation(self, out, in_, func, scale=1.0, bias=None, eng="act"):
        rd = [in_]
        kw = {}
        if not isinstance(scale, (int, float)):
            rd.append(scale)
        if bias is not None:
            kw["bias"] = bias
            if not isinstance(bias, (int, float)):
                rd.append(bias)
        return self.add(eng, lambda e: e.activation(out=out, in_=in_, func=func, scale=scale, **kw), rd, [out])

    def tt(self, out, a, b, op, eng="dve"):
        return self.add(eng, lambda e: e.tensor_tensor(out=out, in0=a, in1=b, op=op), [a, b], [out])

    def ts(self, out, a, s1, s2, op0, op1=None, eng="dve"):
        rd = [a] + [s for s in (s1, s2) if s is not None and not isinstance(s, (int, float))]
        if op1 is None:
            return self.add(eng, lambda e: e.tensor_scalar(out=out, in0=a, scalar1=s1, scalar2=None, op0=op0), rd, [out])
        return self.add(eng, lambda e: e.tensor_scalar(out=out, in0=a, scalar1=s1, scalar2=s2, op0=op0, op1=op1), rd, [out])

    def stt(self, out, in0, scalar, in1, op0, op1, eng="dve"):
        rd = [in0, in1] + ([] if isinstance(scalar, (int, float)) else [scalar])
        return self.add(eng, lambda e: e.scalar_tensor_tensor(out=out, in0=in0, scalar=scalar, in1=in1, op0=op0, op1=op1), rd, [out])

    def copy(self, out, in_, eng="dve"):
        if eng == "act":
            return self.add("act", lambda e: e.copy(out=out, in_=in_), [in_], [out])
        return self.add(eng, lambda e: e.tensor_copy(out=out, in_=in_), [in_], [out])

    def memset(self, ap, val, eng="dve"):
        return self.add(eng, lambda e: e.memset(ap, val), [], [ap])

    def reduce_sum(self, out, in_, eng="dve"):
        return self.add(eng, lambda e: e.reduce_sum(out=out, in_=in_, axis=mybir.AxisListType.X), [in_], [out])

    def reciprocal(self, out, in_, eng="dve"):
        return self.add(eng, lambda e: e.reciprocal(out=out, in_=in_), [in_], [out])

    def dma(self, out, in_, q="sp", rd=None, wr=None, slow=False):
        rd = [in_] if rd is None else rd
        wr = [out] if wr is None else wr
        if slow:
            return self.add(q, lambda e: e.dma_start(out=out, in_=in_, allow_slow_non_contiguous=True), rd, wr, isdma=True)
        return self.add(q, lambda e: e.dma_start(out=out, in_=in_), rd, wr, isdma=True)


class Cfg:
    def __init__(self, S_LOC=4096, CPB=4, NB=2, DEPTH=2):
        self.S = S_LOC
        self.CPB = CPB
        self.NB = NB
        self.DEPTH = DEPTH
        self.NT = S_LOC // 512
        self.NCORES = CPB * NB
        self.SEQ = S_LOC * CPB
        self.NPAR = DEPTH * LP + 16 + 2 * CPB
        self.NCST = DEPTH * LC
        self.MPC = 72 // self.NCORES if 72 % self.NCORES == 0 else None


def pay_off(S):
    nkb = S // 128
    o = {}
    o["kA"] = 0
    o["kC2"] = 2 * S
    o["kD2"] = 4 * S
    o["hB"] = 6 * S
    o["gB"] = 8 * S
    o["vA"] = 10 * S
    o["vC"] = o["vA"] + nkb * 260
    o["vD"] = o["vC"] + nkb * 130
    o["PW"] = o["vD"] + nkb * 130
    return o


def _rot_perm(n, hd):
    q = hd // 4
    idx = np.arange(n)
    h = idx // hd
    r = idx % hd
    quarter = r // q
    partner = np.where(quarter % 2 == 0, r + q, r - q)
    return h * hd + partner


ROPED = [("aq0", 0, 32), ("aq1", 128, 32), ("ak0", 256, 32), ("ak1", 384, 32), ("cq0", 1536, 64), ("cq1", 1664, 64),
         ("ck", 1792, 64), ("dq0", 2048, 64), ("dq1", 2176, 64), ("dk", 2304, 64)]


def win_columns():
    cols = []
    for _, a, hd in ROPED:
        cols.append(np.arange(a, a + 128))
        cols.append(a + _rot_perm(128, hd))
    cols.append(np.arange(768, 1536))
    cols = np.concatenate(cols)
    assert cols.size == NWC * 128
    vcols = np.concatenate([np.arange(512, 768), np.arange(1920, 2048), np.arange(2432, 2560)])
    return cols, vcols


def host_weights(inp, l):
    out = {}
    for i, (gu, dn) in enumerate([("w_gu1", "w_dn1"), ("w_gu2", "w_dn2")]):
        w = np.asarray(inp[gu][l])
        g = w[:, :DFF].reshape(8, 128, NFF, 128)
        u = w[:, DFF:].reshape(8, 128, NFF, 128)
        gu_ = np.stack([g, u], axis=3)
        out[f"wgu{l}{i}"] = np.ascontiguousarray(gu_.transpose(2, 1, 0, 3, 4)).reshape(NFF, 128, 8 * 256)
        w = np.asarray(inp[dn][l])
        w = w.reshape(NFF, 128, 8, 128)
        out[f"wdn{l}{i}"] = np.ascontiguousarray(w.transpose(2, 1, 0, 3)).reshape(8, 128, NFF * 128)
    cols, vcols = win_columns()
    w = np.asarray(inp["w_in"][l])
    wf = w[:, cols].reshape(8, 128, NWC, 128)
    out[f"win{l}"] = np.ascontiguousarray(wf.transpose(2, 1, 0, 3)).reshape(NWC, 128, 8 * 128)
    wv = w[:, vcols].reshape(8, 128, 512)
    out[f"winv{l}"] = np.ascontiguousarray(wv.transpose(1, 0, 2)).reshape(1, 128, 8 * 512)
    w = np.asarray(inp["w_out"][l])
    wo = w.reshape(8, 128, 8, 128)
    out[f"wout{l}"] = np.ascontiguousarray(wo.transpose(2, 1, 0, 3)).reshape(8, 128, 8 * 128)
    w = np.asarray(inp["w_mod"][l])
    wm = w.reshape(8, 128, 72, 128)
    out[f"wmod{l}"] = np.ascontiguousarray(wm.transpose(2, 1, 0, 3)).reshape(72, 128, 8 * 128)
    return out


WSHAPES = {"wgu": (NFF, 128, 2048), "wdn": (8, 128, NFF * 128), "win": (NWC, 128, 1024),
           "winv": (1, 128, 4096), "wout": (8, 128, 8 * 128)}


def host_params(inp, cfg, core):
    b = core // cfg.CPB
    j = core % cfg.CPB
    P = np.zeros((128, cfg.NPAR), np.float32)
    p = np.arange(128)
    for l in range(cfg.DEPTH):
        o = l * LP
        P[:, o:o + 72] = np.asarray(inp["b_mod"][l]).reshape(72, 128).T
        P[:, o + 72:o + 96] = np.asarray(inp["ln_g"][l]).reshape(24, 128).T
        P[:, o + 96:o + 120] = np.asarray(inp["ln_b"][l]).reshape(24, 128).T
        cw = np.asarray(inp["conv_w"][l])
        for cc in range(2):
            for w in range(3):
                P[:, o + 120 + cc * 3 + w] = cw[w, cc * 128:(cc + 1) * 128]
        P[:, o + 126] = np.asarray(inp["subln_w"][l])[p % 64]
        qn = np.asarray(inp["qn_w"][l])
        kn = np.asarray(inp["kn_w"][l])
        pp = _rot_perm(128, 64) % 64
        P[:, o + 127] = qn[p % 64]
        P[:, o + 128] = kn[p % 64]
        P[:, o + 129] = qn[pp]
        P[:, o + 130] = kn[pp]
        P[:, o + 131:o + 135] = np.asarray(inp["sink"][l])[None, :]
        for i, nm in enumerate(["lam_q1", "lam_k1", "lam_q2", "lam_k2"]):
            P[:, o + 135 + 32 * i:o + 135 + 32 * (i + 1)] = np.asarray(inp[nm][l])[None, :]
    o = cfg.DEPTH * LP
    cb = np.asarray(inp["c"][b]).reshape(8, 128).T
    cc = np.asarray(inp["c_ctx"]).reshape(8, 128).T
    P[:, o:o + 16:2] = cb
    P[:, o + 1:o + 16:2] = cc
    o += 16
    if j > 0:
        P[:, o + j - 1] = 1.0
    if j < cfg.CPB - 1:
        P[:, o + cfg.CPB + j + 1] = 1.0
    return P


def host_consts(cfg, core):
    j = core % cfg.CPB
    out = {}
    pos = j * cfg.S + np.arange(cfg.S)
    row = (pos // GRID_W).astype(np.float32)
    col = (pos % GRID_W).astype(np.float32)

    def tab(dim):
        nf = dim // 4
        inv = (ROPE_BASE ** (-np.arange(nf, dtype=np.float32) / nf)).astype(np.float32)
        ar = row[:, None] * inv
        ac = col[:, None] * inv
        ang = np.concatenate([ar, ar, ac, ac], axis=-1)
        sign = np.concatenate([-np.ones(nf), np.ones(nf), -np.ones(nf), np.ones(nf)]).astype(np.float32)
        c = np.cos(ang).astype(np.float32).T
        s = (np.sin(ang).astype(np.float32) * sign).T
        rep = 128 // dim
        return np.tile(c, (rep, 1)), np.tile(s, (rep, 1))
    ca, sa = tab(32)
    ch, sh = tab(64)
    t = np.stack([ca, sa, ch, sh], axis=1)
    t = t.reshape(128, 4, cfg.NT, 512).transpose(2, 0, 1, 3)
    out["rope"] = np.ascontiguousarray(t).astype(np.float32)
    m = np.zeros((128, 6, 4, 128), np.float32)
    jj = np.arange(128)[:, None]
    ii = np.arange(128)[None, :]
    for mm in range(6):
        for qb in range(4):
            d = (mm - 1) - qb
            if d == -1:
                m[:, mm, qb] = (jj >= ii)
            elif d == 0:
                m[:, mm, qb] = 1.0
            elif d == 1:
                m[:, mm, qb] = (jj <= ii)
    out["wmask"] = m.reshape(128, 6 * 512).astype(ml_dtypes.bfloat16)
    return out


N16 = 54 * 1024
N32 = 19 * 1024


class Arena:
    def __init__(self, a16, a32):
        self.a16, self.a32 = a16, a32
        self.o16 = self.o32 = 0

    def bf(self, n):
        ap = self.a16[:, self.o16:self.o16 + n]
        self.o16 += n
        assert self.o16 <= N16, ("bf16 arena overflow", self.o16)
        return ap

    def f32(self, n):
        ap = self.a32[:, self.o32:self.o32 + n]
        self.o32 += n
        assert self.o32 <= N32, ("fp32 arena overflow", self.o32)
        return ap

    def reset(self):
        self.o16 = self.o32 = 0


def v3(ap, inner):
    return ap.rearrange("p (a b) -> p a b", b=inner)


def bank(ps, i, w=512):
    return ps[:, i * 512:i * 512 + w]


RAWW = lambda l: [f"wgu{l}0", f"wdn{l}0", f"win{l}", f"winv{l}", f"wout{l}", f"wgu{l}1", f"wdn{l}1"]


def emit_p0(P, cfg, T, al, ps):
    al.reset()
    inb = [al.f32(4096) for _ in range(2)]
    outb = [al.bf(4096) for _ in range(2)]
    cin = al.f32(24)
    sc = al.f32(24)
    mo = al.f32(cfg.DEPTH * cfg.MPC * 3)
    P.dma(cin, T["call"])
    k = 0
    for l in range(cfg.DEPTH):
        for name in RAWW(l):
            src, dst = T[name + "s"], T[name + "bs"]
            Fd = src.shape[1]
            for c0 in range(0, Fd, 4096):
                n = min(4096, Fd - c0)
                ib, ob = inb[k % 2][:, :n], outb[k % 2][:, :n]
                P.dma(ib, src[:, c0:c0 + n])
                P.copy(ob, ib, eng=("dve" if k % 2 == 0 else "act"))
                P.dma(dst[:, c0:c0 + n], ob, q="pool")
                k += 1
    P.activation(sc, cin, AF.Silu)
    for l in range(cfg.DEPTH):
        psm = bank(ps, l % 2, cfg.MPC * 3).rearrange("p (n w) -> p n w", w=3)
        for g0 in range(0, cfg.MPC, 4):
            ng = min(4, cfg.MPC - g0)
            ib = inb[k % 2]
            k += 1
            P.dma(v3(ib[:, :ng * 1024], 1024), T[f"wmod{l}s"][g0:g0 + ng].rearrange("g p f -> p g f"))
            for g in range(ng):
                for kc in range(8):
                    P.matmul(psm[:, g0 + g, :], ib[:, g * 1024 + kc * 128:g * 1024 + (kc + 1) * 128],
                             sc[:, 3 * kc:3 * kc + 3], start=(kc == 0), stop=(kc == 7))
        P.copy(mo[:, l * cfg.MPC * 3:(l + 1) * cfg.MPC * 3], bank(ps, l % 2, cfg.MPC * 3))
    P.dma(T["modp"], mo, q="pool")


def emit_cst(P, cfg, par, cst, modraw):
    tmp = modraw[:, cfg.DEPTH * 144:cfg.DEPTH * 144 + 32]
    s12 = modraw[:, cfg.DEPTH * 144 + 32:cfg.DEPTH * 144 + 36]
    for l in range(cfg.DEPTH):
        o = l * LP
        oc = l * LC
        for w in range(2):
            mv = cst[:, oc + w * 72:oc + (w + 1) * 72]
            P.tt(mv, modraw[:, l * 144 + w * 72:l * 144 + (w + 1) * 72], par[:, o:o + 72], ALU.add)
            for n in (1, 4, 7):
                P.ts(mv[:, n * 8:n * 8 + 8], mv[:, n * 8:n * 8 + 8], 1.0, None, ALU.add)
            for n in (2, 8):
                P.ts(mv[:, n * 8:n * 8 + 8], mv[:, n * 8:n * 8 + 8], 0.5, None, ALU.mult)
        lam_init = 0.8 - 0.6 * math.exp(-0.3 * l)
        for i in range(2):
            P.tt(tmp, par[:, o + 135 + 64 * i:o + 167 + 64 * i], par[:, o + 167 + 64 * i:o + 199 + 64 * i], ALU.mult)
            P.reduce_sum(s12[:, i:i + 1], tmp)
        P.activation(s12[:, 2:4], s12[:, 0:2], AF.Exp)
        P.tt(cst[:, oc + 144:oc + 145], s12[:, 2:3], s12[:, 3:4], ALU.subtract)
        P.ts(cst[:, oc + 144:oc + 145], cst[:, oc + 144:oc + 145], lam_init, None, ALU.add)
        P.ts(cst[:, oc + 145:oc + 146], cst[:, oc + 144:oc + 145], -1.0, None, ALU.mult)
        P.activation(cst[:, oc + 146:oc + 150], par[:, o + 131:o + 135], AF.Exp)
        P.ts(cst[:, oc + 150:oc + 151], par[:, o + 126:o + 127], 1.0 - lam_init, None, ALU.mult)
        P.memset(cst[:, oc + 151:oc + 152], 0.0)


class FFNBufs:
    def __init__(self, al, TW=512, hT=None, aT=None, wgu=None):
        self.hT = hT if hT is not None else al.bf(8 * TW)
        self.aT = aT if aT is not None else al.bf(NFF * TW)
        wg = wgu if wgu is not None else al.bf(2 * 4096)
        self.wgu = [wg[:, 0:4096], wg[:, 4096:8192]]
        self.wdn_all = al.bf(2 * NFF * 128)
        self.wdn = [self.wdn_all[:, 0:NFF * 128], self.wdn_all[:, NFF * 128:2 * NFF * 128]]
        self.sg = [al.bf(TW) for _ in range(2)]
        self.t = [al.f32(TW) for _ in range(2)]
        self.z = al.f32(8 * TW)
        self.lt_all = al.f32(8 * TW)
        self.lt = [self.lt_all[:, i * TW:(i + 1) * TW] for i in range(8)]
        self.u = [al.f32(TW) for _ in range(2)]


def emit_rsqrt(P, out, in_):
    P.ts(out, in_, EPS, None, ALU.add)
    P.activation(out, out, AF.Ln)
    P.activation(out, out, AF.Exp, scale=-0.5)


def emit_ln(P, ps, fb, z, TW, lng, lnb, out, ones, bk=(6, 7)):
    acc1, acc2, mean, msq, var, rstd, nmr, sq = [x[:, :TW] for x in fb.lt]
    zc = lambda c: z[:, c * TW:(c + 1) * TW]
    P.tt(acc1, zc(0), zc(1), ALU.add)
    for c in range(2, 8):
        P.tt(acc1, acc1, zc(c), ALU.add)
    for c in range(8):
        if c == 0:
            P.activation(acc2, zc(0), AF.Square)
        else:
            P.activation(sq, zc(c), AF.Square)
            P.tt(acc2, acc2, sq, ALU.add)
    b1, b2 = bank(ps, bk[0], TW), bank(ps, bk[1], TW)
    P.matmul(b1, ones, acc1)
    P.matmul(b2, ones, acc2)
    P.ts(mean, b1, 1.0 / D, None, ALU.mult)
    P.tt(msq, mean, mean, ALU.mult)
    P.stt(var, b2, 1.0 / D, msq, ALU.mult, ALU.subtract)
    emit_rsqrt(P, rstd, var)
    P.stt(nmr, mean, -1.0, rstd, ALU.mult, ALU.mult)
    for c in range(8):
        u = fb.u[c % 2][:, :TW]
        P.tt(u, zc(c), rstd, ALU.mult)
        P.tt(u, u, nmr, ALU.add)
        P.activation(out[:, c * TW:(c + 1) * TW], u, AF.Identity, scale=lng[:, c:c + 1], bias=lnb[:, c:c + 1])


def emit_ffn(P, ps, fb, s, TW, mods, wgu, wdn, lng, lnb, out, ones):
    alpha = (2.0 * 2) ** 0.25
    sc_ = lambda c: s[:, c * TW:(c + 1) * TW]
    hT = lambda c: fb.hT[:, c * TW:(c + 1) * TW]
    for c in range(8):
        P.ts(hT(c), sc_(c), mods[:, 8 + c:9 + c], mods[:, c:c + 1], ALU.mult, ALU.add)
    for jp in range(NFF // 2):
        wb = fb.wgu[jp % 2]
        P.dma(v3(wb, 2048), wgu[2 * jp:2 * jp + 2].rearrange("g p f -> p g f"))
        for jj in range(2):
            j = 2 * jp + jj
            bg, bu = bank(ps, 2 * (j % 2), TW), bank(ps, 2 * (j % 2) + 1, TW)
            for kc in range(8):
                P.matmul(bg, wb[:, jj * 2048 + kc * 256:jj * 2048 + kc * 256 + 128], hT(kc), start=(kc == 0), stop=(kc == 7))
            for kc in range(8):
                P.matmul(bu, wb[:, jj * 2048 + kc * 256 + 128:jj * 2048 + kc * 256 + 256], hT(kc), start=(kc == 0), stop=(kc == 7))
            sg = fb.sg[j % 2][:, :TW]
            P.activation(sg, bg, AF.Silu)
            P.tt(fb.aT[:, j * TW:(j + 1) * TW], sg, bu, ALU.mult)
    for dc in range(8):
        wb = fb.wdn[dc % 2]
        P.dma(wb, wdn[dc])
        by = bank(ps, 4 + dc % 2, TW)
        for j in range(NFF):
            P.matmul(by, wb[:, j * 128:(j + 1) * 128], fb.aT[:, j * TW:(j + 1) * TW], start=(j == 0), stop=(j == NFF - 1))
        t = fb.t[dc % 2][:, :TW]
        P.activation(t, by, AF.Identity, scale=mods[:, 16 + dc:17 + dc])
        P.stt(fb.z[:, dc * TW:(dc + 1) * TW], sc_(dc), alpha, t, ALU.mult, ALU.add)
    emit_ln(P, ps, fb, fb.z, TW, lng, lnb, out, ones)


def emit_p1(P, cfg, T, l, al, ps):
    al.reset()
    S = cfg.S
    par = al.f32(cfg.NPAR)
    cst = al.f32(cfg.NCST)
    modraw = al.f32(cfg.DEPTH * 144 + 36)
    ones = al.f32(128)
    blk = al.f32(128)
    fb = FFNBufs(al)
    sbuf = [al.f32(8 * 512)]
    s1 = fb.z
    rope = al.f32(4 * 512)
    tmp = fb.lt[0:4]
    win = fb.wgu
    winv = fb.wdn_all[:, 0:4096]
    qt = al.bf(6 * 512)
    kt = al.bf(4 * 512)
    bt = al.bf(4 * 512)
    vt = [al.bf(4 * 520) for _ in range(2)]
    P.dma(par, T["params"])
    P.dma(modraw[:, :cfg.DEPTH * 144], T["modraw"])
    emit_cst(P, cfg, par, cst, modraw)
    P.memset(ones, 1.0)
    P.memset(blk, 0.0)
    P.memset(blk[0:64, 0:64], 1.0)
    P.memset(blk[64:128, 64:128], 1.0)
    for b in vt:
        P.memset(b, 1.0)
    o = l * LP
    oc = l * LC
    src_s = T["xT"] if l == 0 else T[f"sout{l - 1}"]
    src_c = T["ctxT"] if l == 0 else T[f"cout{l - 1}"]
    tiles = ["ctx"] + list(range(cfg.NT))
    for it, ti in enumerate(tiles):
        isctx = ti == "ctx"
        TW = 256 if isctx else 512
        w = 1 if isctx else 0
        mods = cst[:, oc + w * 72:oc + (w + 1) * 72]
        s = sbuf[0][:, :8 * TW]
        P.dma(s, src_c if isctx else src_s[ti])
        s1t = s1[:, :8 * TW]
        emit_ffn(P, ps, fb, s, TW, mods[:, 0:24], T[f"wgu{l}0b"], T[f"wdn{l}0b"],
                 par[:, o + 72:o + 80], par[:, o + 96:o + 104], s1t, ones)
        P.dma(T[f"c1_{l}"] if isctx else T[f"s1_{l}"][ti], s1t, q="pool")
        hT = lambda c: fb.hT[:, c * TW:(c + 1) * TW]
        for c in range(8):
            P.ts(hT(c), s1t[:, c * TW:(c + 1) * TW], mods[:, 32 + c:33 + c], mods[:, 24 + c:25 + c], ALU.mult, ALU.add)
        if not isctx:
            P.dma(rope, T["rope"][ti])
        P.dma(winv, T[f"winv{l}b"][0])
        wq = []
        def wchunk(ch):
            g = ch // 4
            if ch % 4 == 0:
                n = min(4, NWC - ch)
                P.dma(v3(win[g % 2][:, :n * 1024], 1024), T[f"win{l}b"][ch:ch + n].rearrange("g p f -> p g f"))
            return win[g % 2][:, (ch % 4) * 1024:(ch % 4 + 1) * 1024]
        bkc = [0]
        def proj(ch):
            wb = wchunk(ch)
            b = bank(ps, bkc[0] % 4, TW)
            bkc[0] += 1
            for kc in range(8):
                P.matmul(b, wb[:, kc * 128:(kc + 1) * 128], hT(kc), start=(kc == 0), stop=(kc == 7))
            return b
        dsts = {"aq0": qt[:, 0:TW], "aq1": qt[:, TW:2 * TW], "cq0": qt[:, 2 * TW:3 * TW], "cq1": qt[:, 3 * TW:4 * TW],
                "dq0": qt[:, 4 * TW:5 * TW], "dq1": qt[:, 5 * TW:6 * TW],
                "ak0": kt[:, 0:TW], "ak1": kt[:, TW:2 * TW], "ck": kt[:, 2 * TW:3 * TW], "dk": kt[:, 3 * TW:4 * TW]}
        for ri, (nm, _, hd) in enumerate(ROPED):
            bz = proj(2 * ri)
            dst = dsts[nm]
            isd = nm.startswith("d")
            t0, t1, t2, t3 = [x[:, :TW] for x in tmp]
            if isd:
                gcol = par[:, o + 127:o + 128] if nm[1] == "q" else par[:, o + 128:o + 129]
                gpcol = par[:, o + 129:o + 130] if nm[1] == "q" else par[:, o + 130:o + 131]
                P.activation(t3, bz, AF.Square)
                bs = bank(ps, 6, TW)
                P.matmul(bs, blk, t3)
                P.ts(t3, bs, 1.0 / 64, None, ALU.mult)
                emit_rsqrt(P, t3, t3)
            if isctx:
                if isd:
                    P.stt(dst, bz, gcol, t3, ALU.mult, ALU.mult)
                else:
                    P.copy(dst, bz, eng="act")
                continue
            br = proj(2 * ri + 1)
            cosT = rope[:, (0 if hd == 32 else 2) * 512:(0 if hd == 32 else 2) * 512 + 512]
            sinT = rope[:, (1 if hd == 32 else 3) * 512:(1 if hd == 32 else 3) * 512 + 512]
            if isd:
                P.stt(t0, bz, gcol, cosT, ALU.mult, ALU.mult)
                P.stt(t1, br, gpcol, sinT, ALU.mult, ALU.mult)
                P.tt(t0, t0, t1, ALU.add)
                P.tt(dst, t0, t3, ALU.mult)
            else:
                P.tt(t0, bz, cosT, ALU.mult)
                P.tt(t1, br, sinT, ALU.mult)
                P.tt(dst, t0, t1, ALU.add)
        for cc in range(2):
            b = proj(20 + cc)
            P.copy(bt[:, (2 + cc) * TW:(3 + cc) * TW], b, eng="act")
        for cc in range(2):
            b = proj(22 + cc)
            P.copy(tmp[cc][:, :TW], b, eng="act")
        for cc in range(2):
            b = proj(24 + cc)
            P.tt(bt[:, cc * TW:(cc + 1) * TW], tmp[cc][:, :TW], b, ALU.mult)
        nsub = TW // 128
        vb = vt[it % 2]
        for sub in range(nsub):
            b = bank(ps, bkc[0] % 4, 512)
            bkc[0] += 1
            for kc in range(8):
                P.matmul(b, fb.hT[:, kc * TW + sub * 128:kc * TW + (sub + 1) * 128], winv[:, kc * 512:(kc + 1) * 512],
                         start=(kc == 0), stop=(kc == 7))
            P.copy(v3(vb[:, sub * 520:(sub + 1) * 520], 65)[:, :, 0:64], v3(b, 64), eng="act")
        if isctx:
            pay, po, SS, c0, kb0 = T[f"payc{l}"], pay_off(LCTX), LCTX, 0, 0
            P.dma(T[f"qc{l}"], qt[:, :6 * TW], q="pool")
        else:
            pay, po, SS, c0, kb0 = T[f"pay{l}"], pay_off(S), S, ti * 512, ti * 4
            P.dma(T[f"q{l}"][ti], qt, q="pool")
        P.dma(v3(pay[:, po["kA"]:po["kA"] + 2 * SS], SS)[:, :, c0:c0 + TW], v3(kt[:, 0:2 * TW], TW), q="pool")
        for nm, ki in (("kC2", 2), ("kD2", 3)):
            for kvs in range(2):
                for half in range(2):
                    P.dma(pay[64 * half:64 * half + 64, po[nm] + kvs * SS + c0:po[nm] + kvs * SS + c0 + TW],
                          kt[64 * kvs:64 * kvs + 64, ki * TW:(ki + 1) * TW], q="pool")
        P.dma(v3(pay[:, po["hB"]:po["hB"] + 4 * SS], SS)[:, :, c0:c0 + TW], v3(bt[:, 0:4 * TW], TW), q="pool")
        vb3 = v3(vb[:, :nsub * 520], 520)
        P.dma(v3(pay[:, po["vA"] + kb0 * 260:po["vA"] + (kb0 + nsub) * 260], 260), vb3[:, :, 0:260], q="pool")
        P.dma(v3(pay[:, po["vC"] + kb0 * 130:po["vC"] + (kb0 + nsub) * 130], 130), vb3[:, :, 260:390], q="pool")
        P.dma(v3(pay[:, po["vD"] + kb0 * 130:po["vD"] + (kb0 + nsub) * 130], 130), vb3[:, :, 390:520], q="pool")


def tensor_specs(cfg):
    sp = {}
    NT, S = cfg.NT, cfg.S
    sp["xT"] = ([NT, 128, 4096], F32)
    sp["ctxT"] = ([128, 2048], F32)
    sp["params"] = ([128, cfg.NPAR], F32)
    sp["rope"] = ([NT, 128, 2048], F32)
    sp["wmask"] = ([128, 3072], BF16)
    sp["modraw"] = ([128, cfg.DEPTH * 144], F32)
    sp["call"] = ([128, 24], F32)
    sp["modp"] = ([128, cfg.DEPTH * cfg.MPC * 3], F32)
    for l in range(cfg.DEPTH):
        for i in range(2):
            sp[f"wgu{l}{i}"] = (list(WSHAPES["wgu"]), F32)
            sp[f"wdn{l}{i}"] = (list(WSHAPES["wdn"]), F32)
        sp[f"win{l}"] = (list(WSHAPES["win"]), F32)
        sp[f"winv{l}"] = (list(WSHAPES["winv"]), F32)
        sp[f"wout{l}"] = (list(WSHAPES["wout"]), F32)
        sp[f"wmod{l}s"] = ([cfg.MPC, 128, 1024], F32)
        for nm in RAWW(l):
            ne = int(np.prod(WSHAPES[nm[:-2] if nm[-2:].isdigit() else nm[:-1]])) // (cfg.NCORES * 128)
            sp[nm + "s"] = ([128, ne], F32)
            sp[nm + "bs"] = ([128, ne], BF16)
        for nm in [f"wgu{l}0", f"wgu{l}1", f"wdn{l}0", f"wdn{l}1", f"win{l}", f"winv{l}", f"wout{l}"]:
            sp[nm + "b"] = (sp[nm][0], BF16)
        sp[f"s1_{l}"] = ([NT, 128, 4096], F32)
        sp[f"c1_{l}"] = ([128, 2048], F32)
        sp[f"q{l}"] = ([NT, 128, 3072], BF16)
        sp[f"qc{l}"] = ([128, 1536], BF16)
        sp[f"pay{l}"] = ([128, pay_off(S)["PW"]], BF16)
        sp[f"payc{l}"] = ([128, pay_off(LCTX)["PW"]], BF16)
        sp[f"gath{l}"] = ([cfg.CPB * 128, pay_off(S)["PW"]], BF16)
        sp[f"sout{l}"] = ([NT, 128, 4096], F32)
        sp[f"cout{l}"] = ([128, 2048], F32)
    return sp


def build_launch(cfg, phases, ext_in, ext_out, internal=()):
    nc = bass.Bass("TRN2", target_bir_lowering=False)
    sp = tensor_specs(cfg)
    T = {}
    for nm in ext_in:
        T[nm] = nc.dram_tensor(nm, sp[nm][0], sp[nm][1], kind="ExternalInput").ap()
    for nm in ext_out:
        T[nm] = nc.dram_tensor(nm, sp[nm][0], sp[nm][1], kind="ExternalOutput").ap()
    for nm in internal:
        T[nm] = nc.dram_tensor(nm, sp[nm][0], sp[nm][1]).ap()
    with contextlib.ExitStack() as es:
        a16 = es.enter_context(nc.sbuf_tensor("a16", [128, N16], BF16))
        a32 = es.enter_context(nc.sbuf_tensor("a32", [128, N32], F32))
        ps = es.enter_context(nc.psum_tensor("ps", [128, 4096], F32))
        al = Arena(a16, a32)
        P = Prog()
        for ph in phases:
            ph(P, cfg, T, al, ps)
        P.emit(nc, es)
    return nc


def emit_attn(P, ps, N, qs, tps, blocks, scale, PT):
    blocks = list(blocks) if not hasattr(blocks, "__next__") else blocks
    prev = None
    i = 0

    def pv(pt, vs, first, last):
        for t in range(4):
            P.matmul(ps[0:65, (4 + t) * 512:(4 + t) * 512 + N], vs[t], pt[:, t * N:(t + 1) * N], start=first, stop=last)

    for ks, vs, mask in blocks:
        pt = PT[i % 2][:, :4 * N]
        for t in range(4):
            P.matmul(ps[:, t * 512:t * 512 + N], ks[t], qs[t], tp=tps[t])
        for half in range(2):
            if N == 512:
                P.activation(pt[:, half * 1024:(half + 1) * 1024], ps[:, half * 1024:(half + 1) * 1024], AF.Exp, scale=scale)
            else:
                P.activation(v3(pt[:, half * 2 * N:(half + 1) * 2 * N], N), v3(ps[:, half * 1024:(half + 1) * 1024], 512)[:, :, 0:N],
                             AF.Exp, scale=scale)
        if mask is not None:
            for t in range(4):
                P.tt(pt[:, t * N:(t + 1) * N], pt[:, t * N:(t + 1) * N], mask, ALU.mult)
        if prev is not None:
            pv(prev[0], prev[1], prev[2] == 0, False)
        prev = (pt, vs, i)
        i += 1
    pv(prev[0], prev[1], prev[2] == 0, True)


def emit_p2(P, cfg, T, l, al, ps):
    al.reset()
    S, CPB, NKB = cfg.S, cfg.CPB, cfg.S // 128
    need_ctx = l < cfg.DEPTH - 1
    po, pc = pay_off(S), pay_off(LCTX)
    o, oc = l * LP, l * LC
    alpha = (2.0 * 2) ** 0.25
    par = al.f32(cfg.NPAR)
    cst = al.f32(cfg.NCST)
    modraw = al.f32(cfg.DEPTH * 144 + 36)
    ones = al.f32(128)
    sel = al.f32(64)
    s1t = al.f32(8 * 512)
    R1 = al.bf(11312)
    R2 = al.bf(8192)
    R3 = al.bf(4096)
    fb = FFNBufs(al, hT=R3, aT=R1[:, :NFF * 512], wgu=R2)
    kvb = [R1[:, i * 1544:(i + 1) * 1544] for i in range(3)]
    PT = [R1[:, 4632 + i * 2048:4632 + (i + 1) * 2048] for i in range(2)]
    yT = R2[:, :8 * 512]
    qt = R3[:, :6 * 512]
    wmask = al.bf(3072)
    kCx = [al.bf(S + 256) for _ in range(2)]
    vCx = al.bf((NKB + 2) * 130)
    kCc = al.bf(512)
    vCc = al.bf(260)
    wout = [al.bf(8 * 128) for _ in range(2)]
    hx = al.bf(2 * 514)
    gbt = al.bf(2 * 512)
    cand = al.bf(CPB * 130)
    candh = al.bf(4 * CPB)
    hal = al.bf(4)
    lt = fb.lt
    pay, payc, gath = T[f"pay{l}"], T[f"payc{l}"], T[f"gath{l}"]
    gath3 = gath.rearrange("(r p) c -> p r c", p=128)
    wsel = lambda side, r: par[:, cfg.DEPTH * LP + 16 + side * CPB + r:cfg.DEPTH * LP + 17 + side * CPB + r]

    P.dma(par, T["params"])
    P.dma(modraw[:, :cfg.DEPTH * 144], T["modraw"])
    emit_cst(P, cfg, par, cst, modraw)
    P.dma(wmask, T["wmask"])
    P.memset(ones, 1.0)
    P.memset(sel, 0.0)
    P.memset(sel[64:65, :], 1.0)

    def combine(dst, srcs, side):
        P.ts(dst, srcs[0], wsel(side, 0), None, ALU.mult)
        for r in range(1, CPB):
            P.stt(dst, srcs[r], wsel(side, r), dst, ALU.mult, ALU.add)

    for kvs in range(2):
        P.dma(kCx[kvs][:, 128:128 + S], pay[:, po["kC2"] + kvs * S:po["kC2"] + (kvs + 1) * S])
        for side in range(2):
            c_src = po["kC2"] + kvs * S + (S - 128 if side == 0 else 0)
            c_dst = 0 if side == 0 else 128 + S
            P.dma(v3(cand[:, :CPB * 128], 128), gath3[:, :, c_src:c_src + 128])
            combine(kCx[kvs][:, c_dst:c_dst + 128], [cand[:, r * 128:(r + 1) * 128] for r in range(CPB)], side)
    P.dma(vCx[:, 130:130 * (NKB + 1)], pay[:, po["vC"]:po["vC"] + NKB * 130])
    for side in range(2):
        c_src = po["vC"] + ((NKB - 1) * 130 if side == 0 else 0)
        c_dst = 0 if side == 0 else (NKB + 1) * 130
        P.dma(v3(cand[:, :CPB * 130], 130), gath3[:, :, c_src:c_src + 130])
        combine(vCx[:, c_dst:c_dst + 130], [cand[:, r * 130:(r + 1) * 130] for r in range(CPB)], side)
    for r in range(CPB):
        for cc in range(2):
            for side in range(2):
                c_src = po["hB"] + cc * S + (S - 1 if side == 0 else 0)
                k = r * 4 + cc * 2 + side
                P.dma(candh[:, k:k + 1], gath[r * 128:(r + 1) * 128, c_src:c_src + 1], slow=True)
    for cc in range(2):
        for side in range(2):
            combine(hal[:, cc * 2 + side:cc * 2 + side + 1],
                    [candh[:, r * 4 + cc * 2 + side:r * 4 + cc * 2 + side + 1] for r in range(CPB)], side)
    P.dma(v3(kCc, 256), v3(payc[:, pc["kC2"]:pc["kC2"] + 512], 256))
    P.dma(vCc, payc[:, pc["vC"]:pc["vC"] + 260])

    nlam = cst[:, oc + 145:oc + 146]
    ysub = cst[:, oc + 150:oc + 151]

    ft = [al.f32(512) for _ in range(6)]
    tmps = lt + fb.u + fb.t + ft

    def finish(N, kind, ydsts, esinks=None):
        zr = v3(fb.lt_all[64:65, 0:2048], 512)[:, :, 0:N]
        zsrc = v3(ps[64:65, 2048:4096], 512)[:, :, 0:N]
        if kind == "C":
            for t in range(4):
                P.ts(zr[:, t, :], zsrc[:, t, :], esinks[t], None, ALU.add)
            P.activation(zr, zr, AF.Ln)
        else:
            P.activation(zr, zsrc, AF.Ln)
        P.activation(zr, zr, AF.Exp, scale=-1.0)
        o = [tmps[t][0:64, :N] for t in range(4)]
        zb = [ps[0:64, t * 512:t * 512 + N] for t in range(4)]
        for t in range(4):
            P.copy(o[t], ps[0:64, (4 + t) * 512:(4 + t) * 512 + N], eng="act")
        for t in range(4):
            P.matmul(zb[t], sel[64:65, 0:64], zr[:, t, :])
        if kind != "A":
            for t in range(4):
                P.tt(ydsts[t], o[t], zb[t], ALU.mult)
            return
        for hl in range(2):
            t1, t2 = tmps[8 + 3 * hl][0:64, :N], tmps[9 + 3 * hl][0:64, :N]
            P.tt(t1, o[2 * hl], zb[2 * hl], ALU.mult)
            P.tt(t2, o[2 * hl + 1], zb[2 * hl + 1], ALU.mult)
            P.stt(t1, t2, nlam[0:64, :], t1, ALU.mult, ALU.add)
            P.tt(t2, t1, t1, ALU.mult)
            P.matmul(ps[0:64, (4 + hl) * 512:(4 + hl) * 512 + N], ones[0:64, 0:64], t2)
        for hl in range(2):
            t1, rs = tmps[8 + 3 * hl][0:64, :N], tmps[10 + 3 * hl][0:64, :N]
            P.ts(rs, ps[0:64, (4 + hl) * 512:(4 + hl) * 512 + N], 1.0 / 64, None, ALU.mult)
            emit_rsqrt(P, rs, rs)
            P.stt(ydsts[hl], t1, ysub[0:64, :], rs, ALU.mult, ALU.mult)

    tiles = (["ctx"] if need_ctx else []) + list(range(cfg.NT))
    for it, ti in enumerate(tiles):
        isctx = ti == "ctx"
        N = 256 if isctx else 512
        w = 1 if isctx else 0
        mods = cst[:, oc + w * 72:oc + (w + 1) * 72]
        P.dma(qt[:, :6 * N], T[f"qc{l}"] if isctx else T[f"q{l}"][ti])
        P.dma(s1t[:, :8 * N], T[f"c1_{l}"] if isctx else T[f"s1_{l}"][ti])
        yc = lambda c: yT[:, c * N:(c + 1) * N]
        yh = lambda base, h: yT[64 * (h % 2):64 * (h % 2) + 64, (base + h // 2) * N:(base + h // 2 + 1) * N]
        scs = [(payc, pc, LCTX, 0, 256)]
        if not isctx:
            for r in range(CPB):
                for sc in range(S // 512):
                    scs.append((gath[r * 128:(r + 1) * 128, :], po, S, sc * 512, 512))
        cnt = [0]

        def gen(kind, hp):
            def load(j):
                src, od, SS, k0, nk = scs[j]
                buf = kvb[cnt[0] % 3]
                cnt[0] += 1
                nsub = nk // 128
                kb0 = k0 // 128
                if kind == "A":
                    P.dma(buf[:, 0:nk], src[:, od["kA"] + hp * SS + k0:od["kA"] + hp * SS + k0 + nk])
                    P.dma(v3(buf[:, 1024:1024 + nsub * 130], 130),
                          v3(src[:, od["vA"] + kb0 * 260:od["vA"] + (kb0 + nsub) * 260], 260)[:, :, hp * 130:(hp + 1) * 130])
                else:
                    P.dma(v3(buf[:, 0:1024], 512)[:, :, 0:nk], v3(src[:, od["kD2"]:od["kD2"] + 2 * SS], SS)[:, :, k0:k0 + nk])
                    P.dma(buf[:, 1024:1024 + nsub * 130], src[:, od["vD"] + kb0 * 130:od["vD"] + (kb0 + nsub) * 130])
                return buf
            nxt = load(0)
            for j in range(len(scs)):
                buf = nxt
                if j + 1 < len(scs):
                    nxt = load(j + 1)
                for sub in range(scs[j][4] // 128):
                    if kind == "A":
                        ks = [buf[32 * t:32 * t + 32, sub * 128:(sub + 1) * 128] for t in range(4)]
                        vs = [buf[:, 1024 + sub * 130 + (t // 2) * 65:1024 + sub * 130 + (t // 2) * 65 + 65] for t in range(4)]
                    else:
                        ks = [buf[64 * (t % 2):64 * (t % 2) + 64, (t // 2) * 512 + sub * 128:(t // 2) * 512 + (sub + 1) * 128] for t in range(4)]
                        vs = [buf[:, 1024 + sub * 130 + (t // 2) * 65:1024 + sub * 130 + (t // 2) * 65 + 65] for t in range(4)]
                    yield ks, vs, None

        for hp in range(2):
            qs = [qt[32 * t:32 * t + 32, hp * N:(hp + 1) * N] for t in range(4)]
            emit_attn(P, ps, N, qs, [(32 * t, 0) for t in range(4)], gen("A", hp), 32 ** -0.5, PT)
            finish(N, "A", [yh(0, 2 * hp), yh(0, 2 * hp + 1)])
        qs = [qt[64 * (t % 2):64 * (t % 2) + 64, (4 + t // 2) * N:(5 + t // 2) * N] for t in range(4)]
        tpd = [(64 * (t % 2), 0) for t in range(4)]
        emit_attn(P, ps, N, qs, tpd, gen("D", 0), 64 ** -0.5, PT)
        finish(N, "D", [yh(6, t) for t in range(4)])
        qs = [qt[64 * (t % 2):64 * (t % 2) + 64, (2 + t // 2) * N:(3 + t // 2) * N] for t in range(4)]
        blocks = []
        for j in range(2):
            ks = [kCc[64 * (t % 2):64 * (t % 2) + 64, (t // 2) * 256 + j * 128:(t // 2) * 256 + (j + 1) * 128] for t in range(4)]
            vs = [vCc[:, j * 130 + (t // 2) * 65:j * 130 + (t // 2) * 65 + 65] for t in range(4)]
            blocks.append((ks, vs, None))
        if not isctx:
            for m in range(6):
                e = 4 * ti + m
                ks = [kCx[t // 2][64 * (t % 2):64 * (t % 2) + 64, e * 128:(e + 1) * 128] for t in range(4)]
                vs = [vCx[:, e * 130 + (t // 2) * 65:e * 130 + (t // 2) * 65 + 65] for t in range(4)]
                blocks.append((ks, vs, wmask[:, m * 512:(m + 1) * 512]))
        emit_attn(P, ps, N, qs, tpd, blocks, 64 ** -0.5, PT)
        finish(N, "C", [yh(4, t) for t in range(4)], esinks=[cst[64:65, oc + 146 + t:oc + 147 + t] for t in range(4)])
        if isctx:
            P.memset(hx, 0.0)
            P.dma(v3(hx, 514)[:, :, 1:N + 1], v3(payc[:, pc["hB"]:pc["hB"] + 2 * LCTX], LCTX))
            P.dma(v3(gbt[:, :2 * N], N), v3(payc[:, pc["gB"]:pc["gB"] + 2 * LCTX], LCTX))
        else:
            c0 = ti * 512
            lo = max(c0 - 1, 0)
            hi = min(c0 + N + 1, S)
            P.dma(v3(hx, 514)[:, :, lo - (c0 - 1):hi - (c0 - 1)], v3(pay[:, po["hB"]:po["hB"] + 2 * S], S)[:, :, lo:hi])
            for cc in range(2):
                if c0 == 0:
                    P.copy(hx[:, cc * 514:cc * 514 + 1], hal[:, cc * 2:cc * 2 + 1])
                if c0 + N == S:
                    P.copy(hx[:, cc * 514 + N + 1:cc * 514 + N + 2], hal[:, cc * 2 + 1:cc * 2 + 2])
            P.dma(v3(gbt, 512), v3(pay[:, po["gB"]:po["gB"] + 2 * S], S)[:, :, c0:c0 + N])
        for cc in range(2):
            acc = lt[cc][:, :N]
            cw = lambda wi: par[:, o + 120 + cc * 3 + wi:o + 121 + cc * 3 + wi]
            P.ts(acc, hx[:, cc * 514:cc * 514 + N], cw(0), None, ALU.mult)
            P.stt(acc, hx[:, cc * 514 + 1:cc * 514 + N + 1], cw(1), acc, ALU.mult, ALU.add)
            P.stt(acc, hx[:, cc * 514 + 2:cc * 514 + N + 2], cw(2), acc, ALU.mult, ALU.add)
            P.tt(yc(2 + cc), acc, gbt[:, cc * N:(cc + 1) * N], ALU.mult)
        for dc in range(8):
            wb = wout[dc % 2]
            P.dma(wb, T[f"wout{l}b"][dc])
            by = bank(ps, 4 + dc % 2, N)
            for c in range(8):
                P.matmul(by, wb[:, c * 128:(c + 1) * 128], yc(c), start=(c == 0), stop=(c == 7))
            t = fb.t[dc % 2][:, :N]
            P.activation(t, by, AF.Identity, scale=mods[:, 40 + dc:41 + dc])
            P.stt(fb.z[:, dc * N:(dc + 1) * N], s1t[:, dc * N:(dc + 1) * N], alpha, t, ALU.mult, ALU.add)
        z = fb.z[:, :8 * N]
        emit_ln(P, ps, fb, z, N, par[:, o + 80:o + 88], par[:, o + 104:o + 112], z, ones)
        emit_ffn(P, ps, fb, z, N, mods[:, 48:72], T[f"wgu{l}1b"], T[f"wdn{l}1b"],
                 par[:, o + 88:o + 96], par[:, o + 112:o + 120], z, ones)
        P.dma(T[f"cout{l}"] if isctx else T[f"sout{l}"][ti], z, q="pool")


def host_inputs(cfg, inp):
    W = {}
    for l in range(cfg.DEPTH):
        W.update(host_weights(inp, l))
    maps = []
    x = np.asarray(inp["x"], np.float32)
    ctx = np.asarray(inp["ctx"], np.float32)
    for core in range(cfg.NCORES):
        b, j = core // cfg.CPB, core % cfg.CPB
        m = dict(W)
        m["params"] = host_params(inp, cfg, core)
        m.update(host_consts(cfg, core))
        xs = x[b, j * cfg.S:(j + 1) * cfg.S]
        m["xT"] = np.ascontiguousarray(xs.reshape(cfg.NT, 512, 8, 128).transpose(0, 3, 2, 1)).reshape(cfg.NT, 128, 4096)
        m["ctxT"] = np.ascontiguousarray(ctx[b].reshape(256, 8, 128).transpose(2, 1, 0)).reshape(128, 2048)
        maps.append(m)
    return maps


def assemble(cfg, outs):
    y = np.zeros((cfg.NB, cfg.SEQ, D), np.float32)
    for core in range(cfg.NCORES):
        b, j = core // cfg.CPB, core % cfg.CPB
        t = outs[core].reshape(cfg.NT, 128, 8, 512).transpose(0, 3, 2, 1).reshape(cfg.S, D)
        y[b, j * cfg.S:(j + 1) * cfg.S] = t
    return y


def p1_phase(l):
    return lambda P, c, T, al, ps: emit_p1(P, c, T, l, al, ps)


def p2_phase(l):
    return lambda P, c, T, al, ps: emit_p2(P, c, T, l, al, ps)


def run_unfused(cfg, inp, keep=None):
    maps = host_inputs(cfg, inp)
    cores = list(range(cfg.NCORES))
    NC = cfg.NCORES
    store = [dict(m) for m in maps]
    LD = cfg.DEPTH

    def launch(phases, ext_in, ext_out):
        nc = build_launch(cfg, phases, ext_in, ext_out)
        res = run_bass_kernel_spmd(nc, [{k: store[c][k] for k in ext_in} for c in cores], core_ids=cores)
        for c in cores:
            for k in ext_out:
                store[c][k] = res.results[c][k]

    raw = [n for l in range(LD) for n in RAWW(l)]
    call = np.zeros((128, 8, 3), np.float32)
    for b in range(2):
        call[:, :, b] = np.asarray(inp["c"][b]).reshape(8, 128).T
    call[:, :, 2] = np.asarray(inp["c_ctx"]).reshape(8, 128).T
    call = call.reshape(128, 24)
    for c in cores:
        for n in raw:
            flat = store[c][n].reshape(NC, 128, -1)
            store[c][n + "s"] = np.ascontiguousarray(flat[c])
        for l in range(LD):
            store[c][f"wmod{l}s"] = np.ascontiguousarray(store[c][f"wmod{l}"][c * cfg.MPC:(c + 1) * cfg.MPC])
        store[c]["call"] = call
    launch([emit_p0], ["call"] + [n + "s" for n in raw] + [f"wmod{l}s" for l in range(LD)],
           [n + "bs" for n in raw] + ["modp"])
    sp = tensor_specs(cfg)
    for n in raw:
        full = np.stack([store[c][n + "bs"] for c in cores], axis=0).reshape(sp[n + "b"][0])
        for c in cores:
            store[c][n + "b"] = full
    modp = np.stack([store[c]["modp"].reshape(128, LD, cfg.MPC, 3) for c in cores], axis=2)
    modp = modp.reshape(128, LD, 72, 3)
    for c in cores:
        b = c // cfg.CPB
        mr = np.stack([modp[..., b], modp[..., 2]], axis=2)
        store[c]["modraw"] = np.ascontiguousarray(mr).reshape(128, LD * 144)
        for n in raw:
            store[c].pop(n, None)
            store[c].pop(n + "s", None)
    for l in range(LD):
        s_in = "xT" if l == 0 else f"sout{l - 1}"
        c_in = "ctxT" if l == 0 else f"cout{l - 1}"
        launch([p1_phase(l)], [s_in, c_in, "params", "modraw", "rope", f"wgu{l}0b", f"wdn{l}0b", f"win{l}b", f"winv{l}b"],
               [f"s1_{l}", f"c1_{l}", f"q{l}", f"qc{l}", f"pay{l}", f"payc{l}"])
        for b in range(cfg.NB):
            g = np.concatenate([store[b * cfg.CPB + j][f"pay{l}"] for j in range(cfg.CPB)], axis=0)
            for j in range(cfg.CPB):
                store[b * cfg.CPB + j][f"gath{l}"] = g
        outs = [f"sout{l}"] + ([f"cout{l}"] if l < LD - 1 else [])
        launch([p2_phase(l)], [f"s1_{l}", f"c1_{l}", f"q{l}", f"qc{l}", f"pay{l}", f"payc{l}", f"gath{l}", "params", "modraw",
                               "wmask", f"wout{l}b", f"wgu{l}1b", f"wdn{l}1b"], outs)
    if keep is not None:
        keep.extend(store)
    return assemble(cfg, [store[c][f"sout{LD - 1}"] for c in cores])


def kernel(**inputs):
    cfg = Cfg()
    return run_unfused(cfg, inputs)
```

```python
import contextlib
import math
import numpy as np
import ml_dtypes
import concourse.bass as bass
import concourse.mybir as mybir
from concourse.bass_utils import run_bass_kernel_spmd

F32 = mybir.dt.float32
BF16 = mybir.dt.bfloat16
AF = mybir.ActivationFunctionType
ALU = mybir.AluOpType

D = 1024
DFF = 2816
NFF = 22
LCTX = 256
EPS = 1e-6
GRID_W = 64
ROPE_BASE = 10000.0
NWC = 26
LP = 263
LC = 152
SAME_ENGINE_SYNC = True
DEBUG_WHERE = None


class Op:
    __slots__ = ("eng", "fn", "deps", "sig", "isdma", "sem", "val", "id", "where")


def _region(x):
    if isinstance(x, tuple):
        return ("K", x), (0, 1, 0, 1)
    t = x.tensor
    pairs = list(x.ap)
    tn = type(t).__name__
    if tn.startswith("DRam"):
        lo = x.offset
        hi = lo + sum((c - 1) * abs(s) for s, c in pairs) + 1
        return ("D", t.name), (0, 1, lo, hi)
    shape = list(t.shape)
    row = 1
    for s in shape[1:]:
        row *= s
    p0 = x.offset // row
    f0 = x.offset % row
    npart = pairs[0][1]
    f1 = f0 + sum((c - 1) * abs(s) for s, c in pairs[1:]) + 1
    return ("S", t.name), (p0, p0 + npart, f0, f1)


class Prog:
    NDMASEM = 10

    def __init__(self):
        self.ops = []
        self.hist = {}

    def add(self, eng, fn, reads=(), writes=(), isdma=False):
        op = Op()
        op.eng = eng
        op.fn = fn
        op.isdma = isdma
        op.sig = False
        op.id = len(self.ops)
        if DEBUG_WHERE:
            import sys as _s
            f = _s._getframe(2)
            op.where = (f.f_code.co_name, f.f_lineno)
        deps = set()
        for x in reads:
            self._access(op, x, False, deps)
        for x in writes:
            self._access(op, x, True, deps)
        op.deps = deps
        self.ops.append(op)
        return op

    def _access(self, op, x, isw, deps):
        name, reg = _region(x)
        lst = self.hist.get(name)
        if lst is None:
            lst = []
            self.hist[name] = lst
        p0, p1, f0, f1 = reg
        new = []
        for e in lst:
            (q0, q1, g0, g1), eid, ew, eeng, edma = e
            ov = q0 < p1 and p0 < q1 and g0 < f1 and f0 < g1
            if ov and (isw or ew) and eid != op.id:
                deps.add(eid)
            if ov and isw and p0 <= q0 and q1 <= p1 and f0 <= g0 and g1 <= f1:
                continue
            if (not isw) and (not ew) and (edma == op.isdma) and eeng == op.eng \
                    and (q0, q1, g0, g1) == reg and (not edma or name[0] == "D"):
                continue
            new.append(e)
        new.append((reg, op.id, isw, op.eng, op.isdma))
        self.hist[name] = new

    def emit(self, nc, es):
        ops = self.ops
        engs = ["pe", "act", "dve", "pool", "sp"]
        for op in ops:
            for d in op.deps:
                p = ops[d]
                if p.eng == "pe" and op.eng == "pe" and not op.isdma and not p.isdma:
                    continue
                if (not SAME_ENGINE_SYNC) and p.eng == op.eng and not p.isdma and not op.isdma:
                    continue
                p.sig = True
        csem = {e: es.enter_context(nc.semaphore("c_" + e)) for e in ["pe", "act", "dve", "pool"]}
        dsem = {q: [es.enter_context(nc.semaphore(f"d_{q}{i}")) for i in range(self.NDMASEM)]
                for q in ["sp", "pool", "act"]}
        cnt = {e: 0 for e in csem}
        dcnt = {q: 0 for q in dsem}
        for op in ops:
            if op.isdma:
                n = dcnt[op.eng]
                dcnt[op.eng] += 1
                k = n % self.NDMASEM
                op.sem = dsem[op.eng][k]
                op.val = 16 * (n // self.NDMASEM + 1)
                op.sig = True
                if n >= self.NDMASEM:
                    op.deps.add(("prev", op.sem, op.val - 16))
            elif op.sig:
                cnt[op.eng] += 1
                op.sem = csem[op.eng]
                op.val = cnt[op.eng]
        per = {e: [] for e in engs}
        for op in ops:
            per[op.eng].append(op)
        finals = {q: [] for q in dsem}
        for q in dsem:
            n = dcnt[q]
            for k in range(min(n, self.NDMASEM)):
                last = ((n - 1 - k) // self.NDMASEM) * self.NDMASEM + k
                finals[q].append((dsem[q][k], 16 * (last // self.NDMASEM + 1)))

        def run(engname, e):
            known = {}
            for op in per[engname]:
                waits = {}
                for d in op.deps:
                    if isinstance(d, tuple):
                        sem, val = d[1], d[2]
                    else:
                        p = ops[d]
                        if not p.isdma and not op.isdma and p.eng == op.eng:
                            if p.eng == "pe" or not SAME_ENGINE_SYNC:
                                continue
                        sem, val = p.sem, p.val
                    key = id(sem)
                    if waits.get(key, (None, 0))[1] < val:
                        waits[key] = (sem, val)
                for key, (sem, val) in waits.items():
                    if known.get(key, 0) >= val:
                        continue
                    known[key] = val
                    e.wait_ge(sem, val)
                ins = op.fn(e)
                if DEBUG_WHERE and ins.ins.name in DEBUG_WHERE:
                    print("DBG", ins.ins.name, op.where)
                if op.sig:
                    ins.then_inc(op.sem, 16 if op.isdma else 1)
            for q in dsem:
                if q == engname or (engname == "sp" and q == "act"):
                    for sem, val in finals[q]:
                        e.wait_ge(sem, val)

        block = es.enter_context(nc.Block())
        block.sync(lambda e: run("sp", e))
        block.tensor(lambda e: run("pe", e))
        block.scalar(lambda e: run("act", e))
        block.vector(lambda e: run("dve", e))
        block.gpsimd(lambda e: run("pool", e))

    def matmul(self, out, lhsT, rhs, start=True, stop=True, tp=None):
        kw = dict(start=start, stop=stop)
        if tp is not None:
            kw["tile_position"] = tp
        return self.add("pe", lambda e: e.matmul(out, lhsT=lhsT, rhs=rhs, **kw), [lhsT, rhs], [out])

    def activation(self, out, in_, func, scale=1.0, bias=None, eng="act"):
        rd = [in_]
        kw = {}
        if not isinstance(scale, (int, float)):
            rd.append(scale)
        if bias is not None:
            kw["bias"] = bias
            if not isinstance(bias, (int, float)):
                rd.append(bias)
        return self.add(eng, lambda e: e.activation(out=out, in_=in_, func=func, scale=scale, **kw), rd, [out])

    def tt(self, out, a, b, op, eng="dve"):
        return self.add(eng, lambda e: e.tensor_tensor(out=out, in0=a, in1=b, op=op), [a, b], [out])

    def ts(self, out, a, s1, s2, op0, op1=None, eng="dve"):
        rd = [a] + [s for s in (s1, s2) if s is not None and not isinstance(s, (int, float))]
        if op1 is None:
            return self.add(eng, lambda e: e.tensor_scalar(out=out, in0=a, scalar1=s1, scalar2=None, op0=op0), rd, [out])
        return self.add(eng, lambda e: e.tensor_scalar(out=out, in0=a, scalar1=s1, scalar2=s2, op0=op0, op1=op1), rd, [out])

    def stt(self, out, in0, scalar, in1, op0, op1, eng="dve"):
        rd = [in0, in1] + ([] if isinstance(scalar, (int, float)) else [scalar])
        return self.add(eng, lambda e: e.scalar_tensor_tensor(out=out, in0=in0, scalar=scalar, in1=in1, op0=op0, op1=op1), rd, [out])

    def copy(self, out, in_, eng="dve"):
        if eng == "act":
            return self.add("act", lambda e: e.copy(out=out, in_=in_), [in_], [out])
        return self.add(eng, lambda e: e.tensor_copy(out=out, in_=in_), [in_], [out])

    def memset(self, ap, val, eng="dve"):
        return self.add(eng, lambda e: e.memset(ap, val), [], [ap])

    def reduce_sum(self, out, in_, eng="dve"):
        return self.add(eng, lambda e: e.reduce_sum(out=out, in_=in_, axis=mybir.AxisListType.X), [in_], [out])

    def reciprocal(self, out, in_, eng="dve"):
        return self.add(eng, lambda e: e.reciprocal(out=out, in_=in_), [in_], [out])

    def dma(self, out, in_, q="sp", rd=None, wr=None, slow=False):
        rd = [in_] if rd is None else rd
        wr = [out] if wr is None else wr
        if slow:
            return self.add(q, lambda e: e.dma_start(out=out, in_=in_, allow_slow_non_contiguous=True), rd, wr, isdma=True)
        return self.add(q, lambda e: e.dma_start(out=out, in_=in_), rd, wr, isdma=True)


class Cfg:
    def __init__(self, S_LOC=4096, CPB=4, NB=2, DEPTH=2):
        self.S = S_LOC
        self.CPB = CPB
        self.NB = NB
        self.DEPTH = DEPTH
        self.NT = S_LOC // 512
        self.NCORES = CPB * NB
        self.SEQ = S_LOC * CPB
        self.NPAR = DEPTH * LP + 16 + 2 * CPB
        self.NCST = DEPTH * LC
        self.MPC = 72 // self.NCORES if 72 % self.NCORES == 0 else None


def pay_off(S):
    nkb = S // 128
    o = {}
    o["kA"] = 0
    o["kC2"] = 2 * S
    o["kD2"] = 4 * S
    o["hB"] = 6 * S
    o["gB"] = 8 * S
    o["vA"] = 10 * S
    o["vC"] = o["vA"] + nkb * 260
    o["vD"] = o["vC"] + nkb * 130
    o["PW"] = o["vD"] + nkb * 130
    return o


def _rot_perm(n, hd):
    q = hd // 4
    idx = np.arange(n)
    h = idx // hd
    r = idx % hd
    quarter = r // q
    partner = np.where(quarter % 2 == 0, r + q, r - q)
    return h * hd + partner


ROPED = [("aq0", 0, 32), ("aq1", 128, 32), ("ak0", 256, 32), ("ak1", 384, 32), ("cq0", 1536, 64), ("cq1", 1664, 64),
         ("ck", 1792, 64), ("dq0", 2048, 64), ("dq1", 2176, 64), ("dk", 2304, 64)]


def win_columns():
    cols = []
    for _, a, hd in ROPED:
        cols.append(np.arange(a, a + 128))
        cols.append(a + _rot_perm(128, hd))
    cols.append(np.arange(768, 1536))
    cols = np.concatenate(cols)
    assert cols.size == NWC * 128
    vcols = np.concatenate([np.arange(512, 768), np.arange(1920, 2048), np.arange(2432, 2560)])
    return cols, vcols


def host_weights(inp, l):
    out = {}
    for i, (gu, dn) in enumerate([("w_gu1", "w_dn1"), ("w_gu2", "w_dn2")]):
        w = np.asarray(inp[gu][l])
        g = w[:, :DFF].reshape(8, 128, NFF, 128)
        u = w[:, DFF:].reshape(8, 128, NFF, 128)
        gu_ = np.stack([g, u], axis=3)
        out[f"wgu{l}{i}"] = np.ascontiguousarray(gu_.transpose(2, 1, 0, 3, 4)).reshape(NFF, 128, 8 * 256)
        w = np.asarray(inp[dn][l])
        w = w.reshape(NFF, 128, 8, 128)
        out[f"wdn{l}{i}"] = np.ascontiguousarray(w.transpose(2, 1, 0, 3)).reshape(8, 128, NFF * 128)
    cols, vcols = win_columns()
    w = np.asarray(inp["w_in"][l])
    wf = w[:, cols].reshape(8, 128, NWC, 128)
    out[f"win{l}"] = np.ascontiguousarray(wf.transpose(2, 1, 0, 3)).reshape(NWC, 128, 8 * 128)
    wv = w[:, vcols].reshape(8, 128, 512)
    out[f"winv{l}"] = np.ascontiguousarray(wv.transpose(1, 0, 2)).reshape(1, 128, 8 * 512)
    w = np.asarray(inp["w_out"][l])
    wo = w.reshape(8, 128, 8, 128)
    out[f"wout{l}"] = np.ascontiguousarray(wo.transpose(2, 1, 0, 3)).reshape(8, 128, 8 * 128)
    w = np.asarray(inp["w_mod"][l])
    wm = w.reshape(8, 128, 72, 128)
    out[f"wmod{l}"] = np.ascontiguousarray(wm.transpose(2, 1, 0, 3)).reshape(72, 128, 8 * 128)
    return out


WSHAPES = {"wgu": (NFF, 128, 2048), "wdn": (8, 128, NFF * 128), "win": (NWC, 128, 1024),
           "winv": (1, 128, 4096), "wout": (8, 128, 8 * 128)}


def host_params(inp, cfg, core):
    b = core // cfg.CPB
    j = core % cfg.CPB
    P = np.zeros((128, cfg.NPAR), np.float32)
    p = np.arange(128)
    for l in range(cfg.DEPTH):
        o = l * LP
        P[:, o:o + 72] = np.asarray(inp["b_mod"][l]).reshape(72, 128).T
        P[:, o + 72:o + 96] = np.asarray(inp["ln_g"][l]).reshape(24, 128).T
        P[:, o + 96:o + 120] = np.asarray(inp["ln_b"][l]).reshape(24, 128).T
        cw = np.asarray(inp["conv_w"][l])
        for cc in range(2):
            for w in range(3):
                P[:, o + 120 + cc * 3 + w] = cw[w, cc * 128:(cc + 1) * 128]
        P[:, o + 126] = np.asarray(inp["subln_w"][l])[p % 64]
        qn = np.asarray(inp["qn_w"][l])
        kn = np.asarray(inp["kn_w"][l])
        pp = _rot_perm(128, 64) % 64
        P[:, o + 127] = qn[p % 64]
        P[:, o + 128] = kn[p % 64]
        P[:, o + 129] = qn[pp]
        P[:, o + 130] = kn[pp]
        P[:, o + 131:o + 135] = np.asarray(inp["sink"][l])[None, :]
        for i, nm in enumerate(["lam_q1", "lam_k1", "lam_q2", "lam_k2"]):
            P[:, o + 135 + 32 * i:o + 135 + 32 * (i + 1)] = np.asarray(inp[nm][l])[None, :]
    o = cfg.DEPTH * LP
    cb = np.asarray(inp["c"][b]).reshape(8, 128).T
    cc = np.asarray(inp["c_ctx"]).reshape(8, 128).T
    P[:, o:o + 16:2] = cb
    P[:, o + 1:o + 16:2] = cc
    o += 16
    if j > 0:
        P[:, o + j - 1] = 1.0
    if j < cfg.CPB - 1:
        P[:, o + cfg.CPB + j + 1] = 1.0
    return P


def host_consts(cfg, core):
    j = core % cfg.CPB
    out = {}
    pos = j * cfg.S + np.arange(cfg.S)
    row = (pos // GRID_W).astype(np.float32)
    col = (pos % GRID_W).astype(np.float32)

    def tab(dim):
        nf = dim // 4
        inv = (ROPE_BASE ** (-np.arange(nf, dtype=np.float32) / nf)).astype(np.float32)
        ar = row[:, None] * inv
        ac = col[:, None] * inv
        ang = np.concatenate([ar, ar, ac, ac], axis=-1)
        sign = np.concatenate([-np.ones(nf), np.ones(nf), -np.ones(nf), np.ones(nf)]).astype(np.float32)
        c = np.cos(ang).astype(np.float32).T
        s = (np.sin(ang).astype(np.float32) * sign).T
        rep = 128 // dim
        return np.tile(c, (rep, 1)), np.tile(s, (rep, 1))
    ca, sa = tab(32)
    ch, sh = tab(64)
    t = np.stack([ca, sa, ch, sh], axis=1)
    t = t.reshape(128, 4, cfg.NT, 512).transpose(2, 0, 1, 3)
    out["rope"] = np.ascontiguousarray(t).astype(np.float32).reshape(cfg.NT, 128, 2048)
    m = np.zeros((128, 6, 4, 128), np.float32)
    jj = np.arange(128)[:, None]
    ii = np.arange(128)[None, :]
    for mm in range(6):
        for qb in range(4):
            d = (mm - 1) - qb
            if d == -1:
                m[:, mm, qb] = (jj >= ii)
            elif d == 0:
                m[:, mm, qb] = 1.0
            elif d == 1:
                m[:, mm, qb] = (jj <= ii)
    out["wmask"] = m.reshape(128, 6 * 512).astype(ml_dtypes.bfloat16)
    return out


N16 = 54 * 1024
N32 = 19 * 1024


class Arena:
    def __init__(self, a16, a32):
        self.a16, self.a32 = a16, a32
        self.o16 = self.o32 = 0

    def bf(self, n):
        ap = self.a16[:, self.o16:self.o16 + n]
        self.o16 += n
        assert self.o16 <= N16, ("bf16 arena overflow", self.o16)
        return ap

    def f32(self, n):
        ap = self.a32[:, self.o32:self.o32 + n]
        self.o32 += n
        assert self.o32 <= N32, ("fp32 arena overflow", self.o32)
        return ap

    def reset(self):
        self.o16 = self.o32 = 0


def v3(ap, inner):
    return ap.rearrange("p (a b) -> p a b", b=inner)


def bank(ps, i, w=512):
    return ps[:, i * 512:i * 512 + w]


RAWW = lambda l: [f"wgu{l}0", f"wdn{l}0", f"win{l}", f"winv{l}", f"wout{l}", f"wgu{l}1", f"wdn{l}1"]


def emit_p0(P, cfg, T, al, ps):
    al.reset()
    inb = [al.f32(4096) for _ in range(2)]
    outb = [al.bf(4096) for _ in range(2)]
    cin = al.f32(24)
    sc = al.f32(24)
    mo = al.f32(cfg.DEPTH * cfg.MPC * 3)
    P.dma(cin, T["call"])
    k = 0
    for l in range(cfg.DEPTH):
        for name in RAWW(l):
            src, dst = T[name + "s"], T[name + "bs"]
            Fd = src.shape[1]
            for c0 in range(0, Fd, 4096):
                n = min(4096, Fd - c0)
                ib, ob = inb[k % 2][:, :n], outb[k % 2][:, :n]
                P.dma(ib, src[:, c0:c0 + n])
                P.copy(ob, ib, eng=("dve" if k % 2 == 0 else "act"))
                P.dma(dst[:, c0:c0 + n], ob, q="pool")
                k += 1
    P.activation(sc, cin, AF.Silu)
    for l in range(cfg.DEPTH):
        psm = bank(ps, l % 2, cfg.MPC * 3).rearrange("p (n w) -> p n w", w=3)
        for g0 in range(0, cfg.MPC, 4):
            ng = min(4, cfg.MPC - g0)
            ib = inb[k % 2]
            k += 1
            P.dma(v3(ib[:, :ng * 1024], 1024), T[f"wmod{l}s"][g0:g0 + ng].rearrange("g p f -> p g f"))
            for g in range(ng):
                for kc in range(8):
                    P.matmul(psm[:, g0 + g, :], ib[:, g * 1024 + kc * 128:g * 1024 + (kc + 1) * 128],
                             sc[:, 3 * kc:3 * kc + 3], start=(kc == 0), stop=(kc == 7))
        P.copy(mo[:, l * cfg.MPC * 3:(l + 1) * cfg.MPC * 3], bank(ps, l % 2, cfg.MPC * 3))
    P.dma(T["modp"], mo, q="pool")


def emit_cst(P, cfg, par, cst, modraw):
    tmp = modraw[:, cfg.DEPTH * 144:cfg.DEPTH * 144 + 32]
    s12 = modraw[:, cfg.DEPTH * 144 + 32:cfg.DEPTH * 144 + 36]
    for l in range(cfg.DEPTH):
        o = l * LP
        oc = l * LC
        for w in range(2):
            mv = cst[:, oc + w * 72:oc + (w + 1) * 72]
            P.tt(mv, modraw[:, l * 144 + w * 72:l * 144 + (w + 1) * 72], par[:, o:o + 72], ALU.add)
            for n in (1, 4, 7):
                P.ts(mv[:, n * 8:n * 8 + 8], mv[:, n * 8:n * 8 + 8], 1.0, None, ALU.add)
            for n in (2, 8):
                P.ts(mv[:, n * 8:n * 8 + 8], mv[:, n * 8:n * 8 + 8], 0.5, None, ALU.mult)
        lam_init = 0.8 - 0.6 * math.exp(-0.3 * l)
        for i in range(2):
            P.tt(tmp, par[:, o + 135 + 64 * i:o + 167 + 64 * i], par[:, o + 167 + 64 * i:o + 199 + 64 * i], ALU.mult)
            P.reduce_sum(s12[:, i:i + 1], tmp)
        P.activation(s12[:, 2:4], s12[:, 0:2], AF.Exp)
        P.tt(cst[:, oc + 144:oc + 145], s12[:, 2:3], s12[:, 3:4], ALU.subtract)
        P.ts(cst[:, oc + 144:oc + 145], cst[:, oc + 144:oc + 145], lam_init, None, ALU.add)
        P.ts(cst[:, oc + 145:oc + 146], cst[:, oc + 144:oc + 145], -1.0, None, ALU.mult)
        P.activation(cst[:, oc + 146:oc + 150], par[:, o + 131:o + 135], AF.Exp)
        P.ts(cst[:, oc + 150:oc + 151], par[:, o + 126:o + 127], 1.0 - lam_init, None, ALU.mult)
        P.memset(cst[:, oc + 151:oc + 152], 0.0)


class FFNBufs:
    def __init__(self, al, TW=512, hT=None, aT=None, wgu=None):
        self.hT = hT if hT is not None else al.bf(8 * TW)
        self.aT = aT if aT is not None else al.bf(NFF * TW)
        wg = wgu if wgu is not None else al.bf(2 * 4096)
        self.wgu = [wg[:, 0:4096], wg[:, 4096:8192]]
        self.wdn_all = al.bf(2 * NFF * 128)
        self.wdn = [self.wdn_all[:, 0:NFF * 128], self.wdn_all[:, NFF * 128:2 * NFF * 128]]
        self.sg = [al.bf(TW) for _ in range(2)]
        self.t = [al.f32(TW) for _ in range(2)]
        self.z = al.f32(8 * TW)
        self.lt_all = al.f32(8 * TW)
        self.lt = [self.lt_all[:, i * TW:(i + 1) * TW] for i in range(8)]
        self.u = [al.f32(TW) for _ in range(2)]


def emit_rsqrt(P, out, in_):
    P.ts(out, in_, EPS, None, ALU.add)
    P.activation(out, out, AF.Ln)
    P.activation(out, out, AF.Exp, scale=-0.5)


def emit_ln(P, ps, fb, z, TW, lng, lnb, out, ones, bk=(6, 7)):
    acc1, acc2, mean, msq, var, rstd, nmr, sq = [x[:, :TW] for x in fb.lt]
    zc = lambda c: z[:, c * TW:(c + 1) * TW]
    P.tt(acc1, zc(0), zc(1), ALU.add)
    for c in range(2, 8):
        P.tt(acc1, acc1, zc(c), ALU.add)
    sqs = [sq, fb.u[1][:, :TW]]
    for c in range(8):
        if c == 0:
            P.activation(acc2, zc(0), AF.Square)
        else:
            P.activation(sqs[c % 2], zc(c), AF.Square)
            P.tt(acc2, acc2, sqs[c % 2], ALU.add)
    b1, b2 = bank(ps, bk[0], TW), bank(ps, bk[1], TW)
    P.matmul(b1, ones, acc1)
    P.matmul(b2, ones, acc2)
    P.ts(mean, b1, 1.0 / D, None, ALU.mult)
    P.tt(msq, mean, mean, ALU.mult)
    P.stt(var, b2, 1.0 / D, msq, ALU.mult, ALU.subtract)
    emit_rsqrt(P, rstd, var)
    P.stt(nmr, mean, -1.0, rstd, ALU.mult, ALU.mult)
    for c in range(8):
        u = fb.u[c % 2][:, :TW]
        P.tt(u, zc(c), rstd, ALU.mult)
        P.tt(u, u, nmr, ALU.add)
        P.activation(out[:, c * TW:(c + 1) * TW], u, AF.Identity, scale=lng[:, c:c + 1], bias=lnb[:, c:c + 1])


def emit_ffn(P, ps, fb, s, TW, mods, wgu, wdn, lng, lnb, out, ones):
    alpha = (2.0 * 2) ** 0.25
    sc_ = lambda c: s[:, c * TW:(c + 1) * TW]
    hT = lambda c: fb.hT[:, c * TW:(c + 1) * TW]
    for c in range(8):
        P.ts(hT(c), sc_(c), mods[:, 8 + c:9 + c], mods[:, c:c + 1], ALU.mult, ALU.add)
    for jp in range(NFF // 2):
        wb = fb.wgu[jp % 2]
        P.dma(v3(wb, 2048), wgu[2 * jp:2 * jp + 2].rearrange("g p f -> p g f"))
        for jj in range(2):
            j = 2 * jp + jj
            bg, bu = bank(ps, 2 * (j % 2), TW), bank(ps, 2 * (j % 2) + 1, TW)
            for kc in range(8):
                P.matmul(bg, wb[:, jj * 2048 + kc * 256:jj * 2048 + kc * 256 + 128], hT(kc), start=(kc == 0), stop=(kc == 7))
            for kc in range(8):
                P.matmul(bu, wb[:, jj * 2048 + kc * 256 + 128:jj * 2048 + kc * 256 + 256], hT(kc), start=(kc == 0), stop=(kc == 7))
            sg = fb.sg[j % 2][:, :TW]
            P.activation(sg, bg, AF.Silu)
            P.tt(fb.aT[:, j * TW:(j + 1) * TW], sg, bu, ALU.mult)
    for dc in range(8):
        wb = fb.wdn[dc % 2]
        P.dma(wb, wdn[dc])
        by = bank(ps, 4 + dc % 2, TW)
        for j in range(NFF):
            P.matmul(by, wb[:, j * 128:(j + 1) * 128], fb.aT[:, j * TW:(j + 1) * TW], start=(j == 0), stop=(j == NFF - 1))
        t = fb.t[dc % 2][:, :TW]
        P.activation(t, by, AF.Identity, scale=mods[:, 16 + dc:17 + dc])
        P.stt(fb.z[:, dc * TW:(dc + 1) * TW], sc_(dc), alpha, t, ALU.mult, ALU.add)
    emit_ln(P, ps, fb, fb.z, TW, lng, lnb, out, ones)


def emit_p1(P, cfg, T, l, al, ps):
    al.reset()
    S = cfg.S
    par = al.f32(cfg.NPAR)
    cst = al.f32(cfg.NCST)
    modraw = al.f32(cfg.DEPTH * 144 + 36)
    ones = al.f32(128)
    blk = al.f32(128)
    fb = FFNBufs(al)
    sbuf = [al.f32(8 * 512)]
    s1 = fb.z
    rope = al.f32(4 * 512)
    tmp = fb.lt[0:4]
    win = fb.wgu
    winv = fb.wdn_all[:, 0:4096]
    qt = al.bf(6 * 512)
    kt = al.bf(4 * 512)
    bt = al.bf(4 * 512)
    vt = [al.bf(4 * 520) for _ in range(2)]
    P.dma(par, T["params"])
    P.dma(modraw[:, :cfg.DEPTH * 144], T["modraw"])
    emit_cst(P, cfg, par, cst, modraw)
    P.memset(ones, 1.0)
    P.memset(blk, 0.0)
    P.memset(blk[0:64, 0:64], 1.0)
    P.memset(blk[64:128, 64:128], 1.0)
    for b in vt:
        P.memset(b, 1.0)
    o = l * LP
    oc = l * LC
    src_s = T["xT"] if l == 0 else T[f"sout{l - 1}"]
    src_c = T["ctxT"] if l == 0 else T[f"cout{l - 1}"]
    tiles = ["ctx"] + list(range(cfg.NT))
    for it, ti in enumerate(tiles):
        isctx = ti == "ctx"
        TW = 256 if isctx else 512
        w = 1 if isctx else 0
        mods = cst[:, oc + w * 72:oc + (w + 1) * 72]
        s = sbuf[0][:, :8 * TW]
        P.dma(s, src_c if isctx else src_s[ti])
        s1t = s1[:, :8 * TW]
        emit_ffn(P, ps, fb, s, TW, mods[:, 0:24], T[f"wgu{l}0b"], T[f"wdn{l}0b"],
                 par[:, o + 72:o + 80], par[:, o + 96:o + 104], s1t, ones)
        P.dma(T[f"c1_{l}"] if isctx else T[f"s1_{l}"][ti], s1t, q="pool")
        hT = lambda c: fb.hT[:, c * TW:(c + 1) * TW]
        for c in range(8):
            P.ts(hT(c), s1t[:, c * TW:(c + 1) * TW], mods[:, 32 + c:33 + c], mods[:, 24 + c:25 + c], ALU.mult, ALU.add)
        if not isctx:
            P.dma(rope, T["rope"][ti])
        P.dma(winv, T[f"winv{l}b"][0])
        wq = []
        def wchunk(ch):
            g = ch // 4
            if ch % 4 == 0:
                n = min(4, NWC - ch)
                P.dma(v3(win[g % 2][:, :n * 1024], 1024), T[f"win{l}b"][ch:ch + n].rearrange("g p f -> p g f"))
            return win[g % 2][:, (ch % 4) * 1024:(ch % 4 + 1) * 1024]
        bkc = [0]
        def proj(ch):
            wb = wchunk(ch)
            b = bank(ps, bkc[0] % 4, TW)
            bkc[0] += 1
            for kc in range(8):
                P.matmul(b, wb[:, kc * 128:(kc + 1) * 128], hT(kc), start=(kc == 0), stop=(kc == 7))
            return b
        dsts = {"aq0": qt[:, 0:TW], "aq1": qt[:, TW:2 * TW], "cq0": qt[:, 2 * TW:3 * TW], "cq1": qt[:, 3 * TW:4 * TW],
                "dq0": qt[:, 4 * TW:5 * TW], "dq1": qt[:, 5 * TW:6 * TW],
                "ak0": kt[:, 0:TW], "ak1": kt[:, TW:2 * TW], "ck": kt[:, 2 * TW:3 * TW], "dk": kt[:, 3 * TW:4 * TW]}
        for ri, (nm, _, hd) in enumerate(ROPED):
            bz = proj(2 * ri)
            dst = dsts[nm]
            isd = nm.startswith("d")
            t0, t1, t2, t3 = [x[:, :TW] for x in tmp]
            if isd:
                gcol = par[:, o + 127:o + 128] if nm[1] == "q" else par[:, o + 128:o + 129]
                gpcol = par[:, o + 129:o + 130] if nm[1] == "q" else par[:, o + 130:o + 131]
                P.activation(t3, bz, AF.Square)
                bs = bank(ps, 6, TW)
                P.matmul(bs, blk, t3)
                P.ts(t3, bs, 1.0 / 64, None, ALU.mult)
                emit_rsqrt(P, t3, t3)
            if isctx:
                if isd:
                    P.stt(dst, bz, gcol, t3, ALU.mult, ALU.mult)
                else:
                    P.copy(dst, bz, eng="act")
                continue
            br = proj(2 * ri + 1)
            cosT = rope[:, (0 if hd == 32 else 2) * 512:(0 if hd == 32 else 2) * 512 + 512]
            sinT = rope[:, (1 if hd == 32 else 3) * 512:(1 if hd == 32 else 3) * 512 + 512]
            if isd:
                P.stt(t0, bz, gcol, cosT, ALU.mult, ALU.mult)
                P.stt(t1, br, gpcol, sinT, ALU.mult, ALU.mult)
                P.tt(t0, t0, t1, ALU.add)
                P.tt(dst, t0, t3, ALU.mult)
            else:
                P.tt(t0, bz, cosT, ALU.mult)
                P.tt(t1, br, sinT, ALU.mult)
                P.tt(dst, t0, t1, ALU.add)
        for cc in range(2):
            b = proj(20 + cc)
            P.copy(bt[:, (2 + cc) * TW:(3 + cc) * TW], b, eng="act")
        for cc in range(2):
            b = proj(22 + cc)
            P.copy(tmp[cc][:, :TW], b, eng="act")
        for cc in range(2):
            b = proj(24 + cc)
            P.tt(bt[:, cc * TW:(cc + 1) * TW], tmp[cc][:, :TW], b, ALU.mult)
        nsub = TW // 128
        vb = vt[it % 2]
        for sub in range(nsub):
            b = bank(ps, bkc[0] % 4, 512)
            bkc[0] += 1
            for kc in range(8):
                P.matmul(b, fb.hT[:, kc * TW + sub * 128:kc * TW + (sub + 1) * 128], winv[:, kc * 512:(kc + 1) * 512],
                         start=(kc == 0), stop=(kc == 7))
            P.copy(v3(vb[:, sub * 520:(sub + 1) * 520], 65)[:, :, 0:64], v3(b, 64), eng="act")
        if isctx:
            pay, po, SS, c0, kb0 = T[f"payc{l}"], pay_off(LCTX), LCTX, 0, 0
            P.dma(T[f"qc{l}"], qt[:, :6 * TW], q="pool")
        else:
            pay, po, SS, c0, kb0 = T[f"pay{l}"], pay_off(S), S, ti * 512, ti * 4
            P.dma(T[f"q{l}"][ti], qt, q="pool")
        P.dma(v3(pay[:, po["kA"]:po["kA"] + 2 * SS], SS)[:, :, c0:c0 + TW], v3(kt[:, 0:2 * TW], TW), q="pool")
        for nm, ki in (("kC2", 2), ("kD2", 3)):
            for kvs in range(2):
                for half in range(2):
                    P.dma(pay[64 * half:64 * half + 64, po[nm] + kvs * SS + c0:po[nm] + kvs * SS + c0 + TW],
                          kt[64 * kvs:64 * kvs + 64, ki * TW:(ki + 1) * TW], q="pool")
        P.dma(v3(pay[:, po["hB"]:po["hB"] + 4 * SS], SS)[:, :, c0:c0 + TW], v3(bt[:, 0:4 * TW], TW), q="pool")
        vb3 = v3(vb[:, :nsub * 520], 520)
        P.dma(v3(pay[:, po["vA"] + kb0 * 260:po["vA"] + (kb0 + nsub) * 260], 260), vb3[:, :, 0:260], q="pool")
        P.dma(v3(pay[:, po["vC"] + kb0 * 130:po["vC"] + (kb0 + nsub) * 130], 130), vb3[:, :, 260:390], q="pool")
        P.dma(v3(pay[:, po["vD"] + kb0 * 130:po["vD"] + (kb0 + nsub) * 130], 130), vb3[:, :, 390:520], q="pool")


def tensor_specs(cfg):
    sp = {}
    NT, S = cfg.NT, cfg.S
    sp["xT"] = ([NT, 128, 4096], F32)
    sp["ctxT"] = ([128, 2048], F32)
    sp["params"] = ([128, cfg.NPAR], F32)
    sp["rope"] = ([NT, 128, 2048], F32)
    sp["wmask"] = ([128, 3072], BF16)
    sp["modraw"] = ([128, cfg.DEPTH * 144], F32)
    sp["call"] = ([128, 24], F32)
    sp["modp"] = ([128, cfg.DEPTH * cfg.MPC * 3], F32)
    for l in range(cfg.DEPTH):
        for i in range(2):
            sp[f"wgu{l}{i}"] = (list(WSHAPES["wgu"]), F32)
            sp[f"wdn{l}{i}"] = (list(WSHAPES["wdn"]), F32)
        sp[f"win{l}"] = (list(WSHAPES["win"]), F32)
        sp[f"winv{l}"] = (list(WSHAPES["winv"]), F32)
        sp[f"wout{l}"] = (list(WSHAPES["wout"]), F32)
        sp[f"wmod{l}s"] = ([cfg.MPC, 128, 1024], F32)
        for nm in RAWW(l):
            ne = int(np.prod(WSHAPES[nm[:-2] if nm[-2:].isdigit() else nm[:-1]])) // (cfg.NCORES * 128)
            sp[nm + "s"] = ([128, ne], F32)
            sp[nm + "bs"] = ([128, ne], BF16)
        for nm in [f"wgu{l}0", f"wgu{l}1", f"wdn{l}0", f"wdn{l}1", f"win{l}", f"winv{l}", f"wout{l}"]:
            sp[nm + "b"] = (sp[nm][0], BF16)
        sp[f"s1_{l}"] = ([NT, 128, 4096], F32)
        sp[f"c1_{l}"] = ([128, 2048], F32)
        sp[f"q{l}"] = ([NT, 128, 3072], BF16)
        sp[f"qc{l}"] = ([128, 1536], BF16)
        sp[f"pay{l}"] = ([128, pay_off(S)["PW"]], BF16)
        sp[f"payc{l}"] = ([128, pay_off(LCTX)["PW"]], BF16)
        sp[f"gath{l}"] = ([cfg.CPB * 128, pay_off(S)["PW"]], BF16)
        sp[f"sout{l}"] = ([NT, 128, 4096], F32)
        sp[f"cout{l}"] = ([128, 2048], F32)
    return sp


def build_launch(cfg, phases, ext_in, ext_out, internal=()):
    nc = bass.Bass("TRN2", target_bir_lowering=False)
    sp = tensor_specs(cfg)
    T = {}
    for nm in ext_in:
        T[nm] = nc.dram_tensor(nm, sp[nm][0], sp[nm][1], kind="ExternalInput").ap()
    for nm in ext_out:
        T[nm] = nc.dram_tensor(nm, sp[nm][0], sp[nm][1], kind="ExternalOutput").ap()
    for nm in internal:
        T[nm] = nc.dram_tensor(nm, sp[nm][0], sp[nm][1]).ap()
    with contextlib.ExitStack() as es:
        a16 = es.enter_context(nc.sbuf_tensor("a16", [128, N16], BF16))
        a32 = es.enter_context(nc.sbuf_tensor("a32", [128, N32], F32))
        ps = es.enter_context(nc.psum_tensor("ps", [128, 4096], F32))
        al = Arena(a16, a32)
        P = Prog()
        for ph in phases:
            ph(P, cfg, T, al, ps)
        P.emit(nc, es)
    return nc


def emit_attn(P, ps, N, qs, tps, blocks, scale, PT):
    blocks = list(blocks) if not hasattr(blocks, "__next__") else blocks
    prev = None
    i = 0

    def pv(pt, vs, first, last):
        for t in range(4):
            P.matmul(ps[0:65, (4 + t) * 512:(4 + t) * 512 + N], vs[t], pt[:, t * N:(t + 1) * N], start=first, stop=last)

    for ks, vs, mask in blocks:
        pt = PT[i % 2][:, :4 * N]
        for t in range(4):
            P.matmul(ps[:, t * 512:t * 512 + N], ks[t], qs[t], tp=tps[t])
        for half in range(2):
            if N == 512:
                P.activation(pt[:, half * 1024:(half + 1) * 1024], ps[:, half * 1024:(half + 1) * 1024], AF.Exp, scale=scale)
            else:
                P.activation(v3(pt[:, half * 2 * N:(half + 1) * 2 * N], N), v3(ps[:, half * 1024:(half + 1) * 1024], 512)[:, :, 0:N],
                             AF.Exp, scale=scale)
        if mask is not None:
            for t in range(4):
                P.tt(pt[:, t * N:(t + 1) * N], pt[:, t * N:(t + 1) * N], mask, ALU.mult)
        if prev is not None:
            pv(prev[0], prev[1], prev[2] == 0, False)
        prev = (pt, vs, i)
        i += 1
    pv(prev[0], prev[1], prev[2] == 0, True)


def emit_p2(P, cfg, T, l, al, ps):
    al.reset()
    S, CPB, NKB = cfg.S, cfg.CPB, cfg.S // 128
    need_ctx = l < cfg.DEPTH - 1
    po, pc = pay_off(S), pay_off(LCTX)
    o, oc = l * LP, l * LC
    alpha = (2.0 * 2) ** 0.25
    par = al.f32(cfg.NPAR)
    cst = al.f32(cfg.NCST)
    modraw = al.f32(cfg.DEPTH * 144 + 36)
    ones = al.f32(128)
    sel = al.f32(64)
    s1t = al.f32(8 * 512)
    R1 = al.bf(11312)
    R2 = al.bf(8192)
    R3 = al.bf(4096)
    fb = FFNBufs(al, hT=R3, aT=R1[:, :NFF * 512], wgu=R2)
    kvb = [R1[:, i * 1544:(i + 1) * 1544] for i in range(3)]
    PT = [R1[:, 4632 + i * 2048:4632 + (i + 1) * 2048] for i in range(2)]
    yT = R2[:, :8 * 512]
    qt = R3[:, :6 * 512]
    wmask = al.bf(3072)
    kCx = [al.bf(S + 256) for _ in range(2)]
    vCx = al.bf((NKB + 2) * 130)
    kCc = al.bf(512)
    vCc = al.bf(260)
    wout = [al.bf(8 * 128) for _ in range(2)]
    hx = al.bf(2 * 514)
    gbt = al.bf(2 * 512)
    cand = al.bf(CPB * 130)
    candh = al.bf(4 * CPB)
    hal = al.bf(4)
    lt = fb.lt
    pay, payc, gath = T[f"pay{l}"], T[f"payc{l}"], T[f"gath{l}"]
    gath3 = gath.rearrange("(r p) c -> p r c", p=128)
    wsel = lambda side, r: par[:, cfg.DEPTH * LP + 16 + side * CPB + r:cfg.DEPTH * LP + 17 + side * CPB + r]

    P.dma(par, T["params"])
    P.dma(modraw[:, :cfg.DEPTH * 144], T["modraw"])
    emit_cst(P, cfg, par, cst, modraw)
    P.dma(wmask, T["wmask"])
    P.memset(ones, 1.0)
    P.memset(sel, 0.0)
    P.memset(sel[64:65, :], 1.0)

    def combine(dst, srcs, side):
        P.ts(dst, srcs[0], wsel(side, 0), None, ALU.mult)
        for r in range(1, CPB):
            P.stt(dst, srcs[r], wsel(side, r), dst, ALU.mult, ALU.add)

    for kvs in range(2):
        P.dma(kCx[kvs][:, 128:128 + S], pay[:, po["kC2"] + kvs * S:po["kC2"] + (kvs + 1) * S])
        for side in range(2):
            c_src = po["kC2"] + kvs * S + (S - 128 if side == 0 else 0)
            c_dst = 0 if side == 0 else 128 + S
            P.dma(v3(cand[:, :CPB * 128], 128), gath3[:, :, c_src:c_src + 128])
            combine(kCx[kvs][:, c_dst:c_dst + 128], [cand[:, r * 128:(r + 1) * 128] for r in range(CPB)], side)
    P.dma(vCx[:, 130:130 * (NKB + 1)], pay[:, po["vC"]:po["vC"] + NKB * 130])
    for side in range(2):
        c_src = po["vC"] + ((NKB - 1) * 130 if side == 0 else 0)
        c_dst = 0 if side == 0 else (NKB + 1) * 130
        P.dma(v3(cand[:, :CPB * 130], 130), gath3[:, :, c_src:c_src + 130])
        combine(vCx[:, c_dst:c_dst + 130], [cand[:, r * 130:(r + 1) * 130] for r in range(CPB)], side)
    for r in range(CPB):
        for cc in range(2):
            for side in range(2):
                c_src = po["hB"] + cc * S + (S - 1 if side == 0 else 0)
                k = r * 4 + cc * 2 + side
                P.dma(candh[:, k:k + 1], gath[r * 128:(r + 1) * 128, c_src:c_src + 1], slow=True)
    for cc in range(2):
        for side in range(2):
            combine(hal[:, cc * 2 + side:cc * 2 + side + 1],
                    [candh[:, r * 4 + cc * 2 + side:r * 4 + cc * 2 + side + 1] for r in range(CPB)], side)
    P.dma(v3(kCc, 256), v3(payc[:, pc["kC2"]:pc["kC2"] + 512], 256))
    P.dma(vCc, payc[:, pc["vC"]:pc["vC"] + 260])

    nlam = cst[:, oc + 145:oc + 146]
    ysub = cst[:, oc + 150:oc + 151]

    ft = [al.f32(512) for _ in range(6)]
    tmps = lt + fb.u + fb.t + ft

    def finish(N, kind, ydsts, esinks=None):
        zr = v3(fb.lt_all[64:65, 0:2048], 512)[:, :, 0:N]
        zsrc = v3(ps[64:65, 2048:4096], 512)[:, :, 0:N]
        if kind == "C":
            for t in range(4):
                P.ts(zr[:, t, :], zsrc[:, t, :], esinks[t], None, ALU.add)
            P.activation(zr, zr, AF.Ln)
        else:
            P.activation(zr, zsrc, AF.Ln)
        P.activation(zr, zr, AF.Exp, scale=-1.0)
        o = [tmps[t][0:64, :N] for t in range(4)]
        zb = [ps[0:64, t * 512:t * 512 + N] for t in range(4)]
        for t in range(4):
            P.copy(o[t], ps[0:64, (4 + t) * 512:(4 + t) * 512 + N], eng="act")
        for t in range(4):
            P.matmul(zb[t], sel[64:65, 0:64], zr[:, t, :])
        if kind != "A":
            for t in range(4):
                P.tt(ydsts[t], o[t], zb[t], ALU.mult)
            return
        for hl in range(2):
            t1, t2 = tmps[8 + 3 * hl][0:64, :N], tmps[9 + 3 * hl][0:64, :N]
            P.tt(t1, o[2 * hl], zb[2 * hl], ALU.mult)
            P.tt(t2, o[2 * hl + 1], zb[2 * hl + 1], ALU.mult)
            P.stt(t1, t2, nlam[0:64, :], t1, ALU.mult, ALU.add)
            P.tt(t2, t1, t1, ALU.mult)
            P.matmul(ps[0:64, (4 + hl) * 512:(4 + hl) * 512 + N], ones[0:64, 0:64], t2)
        for hl in range(2):
            t1, rs = tmps[8 + 3 * hl][0:64, :N], tmps[10 + 3 * hl][0:64, :N]
            P.ts(rs, ps[0:64, (4 + hl) * 512:(4 + hl) * 512 + N], 1.0 / 64, None, ALU.mult)
            emit_rsqrt(P, rs, rs)
            P.stt(ydsts[hl], t1, ysub[0:64, :], rs, ALU.mult, ALU.mult)

    tiles = (["ctx"] if need_ctx else []) + list(range(cfg.NT))
    for it, ti in enumerate(tiles):
        isctx = ti == "ctx"
        N = 256 if isctx else 512
        w = 1 if isctx else 0
        mods = cst[:, oc + w * 72:oc + (w + 1) * 72]
        P.dma(qt[:, :6 * N], T[f"qc{l}"] if isctx else T[f"q{l}"][ti])
        P.dma(s1t[:, :8 * N], T[f"c1_{l}"] if isctx else T[f"s1_{l}"][ti])
        yc = lambda c: yT[:, c * N:(c + 1) * N]
        yh = lambda base, h: yT[64 * (h % 2):64 * (h % 2) + 64, (base + h // 2) * N:(base + h // 2 + 1) * N]
        scs = [(payc, pc, LCTX, 0, 256)]
        if not isctx:
            for r in range(CPB):
                for sc in range(S // 512):
                    scs.append((gath[r * 128:(r + 1) * 128, :], po, S, sc * 512, 512))
        cnt = [0]

        def gen(kind, hp):
            def load(j):
                src, od, SS, k0, nk = scs[j]
                buf = kvb[cnt[0] % 3]
                cnt[0] += 1
                nsub = nk // 128
                kb0 = k0 // 128
                if kind == "A":
                    P.dma(buf[:, 0:nk], src[:, od["kA"] + hp * SS + k0:od["kA"] + hp * SS + k0 + nk])
                    P.dma(v3(buf[:, 1024:1024 + nsub * 130], 130),
                          v3(src[:, od["vA"] + kb0 * 260:od["vA"] + (kb0 + nsub) * 260], 260)[:, :, hp * 130:(hp + 1) * 130])
                else:
                    P.dma(v3(buf[:, 0:1024], 512)[:, :, 0:nk], v3(src[:, od["kD2"]:od["kD2"] + 2 * SS], SS)[:, :, k0:k0 + nk])
                    P.dma(buf[:, 1024:1024 + nsub * 130], src[:, od["vD"] + kb0 * 130:od["vD"] + (kb0 + nsub) * 130])
                return buf
            nxt = load(0)
            for j in range(len(scs)):
                buf = nxt
                if j + 1 < len(scs):
                    nxt = load(j + 1)
                for sub in range(scs[j][4] // 128):
                    if kind == "A":
                        ks = [buf[32 * t:32 * t + 32, sub * 128:(sub + 1) * 128] for t in range(4)]
                        vs = [buf[:, 1024 + sub * 130 + (t // 2) * 65:1024 + sub * 130 + (t // 2) * 65 + 65] for t in range(4)]
                    else:
                        ks = [buf[64 * (t % 2):64 * (t % 2) + 64, (t // 2) * 512 + sub * 128:(t // 2) * 512 + (sub + 1) * 128] for t in range(4)]
                        vs = [buf[:, 1024 + sub * 130 + (t // 2) * 65:1024 + sub * 130 + (t // 2) * 65 + 65] for t in range(4)]
                    yield ks, vs, None

        for hp in range(2):
            qs = [qt[32 * t:32 * t + 32, hp * N:(hp + 1) * N] for t in range(4)]
            emit_attn(P, ps, N, qs, [(32 * t, 0) for t in range(4)], gen("A", hp), 32 ** -0.5, PT)
            finish(N, "A", [yh(0, 2 * hp), yh(0, 2 * hp + 1)])
        qs = [qt[64 * (t % 2):64 * (t % 2) + 64, (4 + t // 2) * N:(5 + t // 2) * N] for t in range(4)]
        tpd = [(64 * (t % 2), 0) for t in range(4)]
        emit_attn(P, ps, N, qs, tpd, gen("D", 0), 64 ** -0.5, PT)
        finish(N, "D", [yh(6, t) for t in range(4)])
        qs = [qt[64 * (t % 2):64 * (t % 2) + 64, (2 + t // 2) * N:(3 + t // 2) * N] for t in range(4)]
        blocks = []
        for j in range(2):
            ks = [kCc[64 * (t % 2):64 * (t % 2) + 64, (t // 2) * 256 + j * 128:(t // 2) * 256 + (j + 1) * 128] for t in range(4)]
            vs = [vCc[:, j * 130 + (t // 2) * 65:j * 130 + (t // 2) * 65 + 65] for t in range(4)]
            blocks.append((ks, vs, None))
        if not isctx:
            for m in range(6):
                e = 4 * ti + m
                ks = [kCx[t // 2][64 * (t % 2):64 * (t % 2) + 64, e * 128:(e + 1) * 128] for t in range(4)]
                vs = [vCx[:, e * 130 + (t // 2) * 65:e * 130 + (t // 2) * 65 + 65] for t in range(4)]
                blocks.append((ks, vs, wmask[:, m * 512:(m + 1) * 512]))
        emit_attn(P, ps, N, qs, tpd, blocks, 64 ** -0.5, PT)
        finish(N, "C", [yh(4, t) for t in range(4)], esinks=[cst[64:65, oc + 146 + t:oc + 147 + t] for t in range(4)])
        if isctx:
            P.memset(hx, 0.0)
            P.dma(v3(hx, 514)[:, :, 1:N + 1], v3(payc[:, pc["hB"]:pc["hB"] + 2 * LCTX], LCTX))
            P.dma(v3(gbt[:, :2 * N], N), v3(payc[:, pc["gB"]:pc["gB"] + 2 * LCTX], LCTX))
        else:
            c0 = ti * 512
            lo = max(c0 - 1, 0)
            hi = min(c0 + N + 1, S)
            P.dma(v3(hx, 514)[:, :, lo - (c0 - 1):hi - (c0 - 1)], v3(pay[:, po["hB"]:po["hB"] + 2 * S], S)[:, :, lo:hi])
            for cc in range(2):
                if c0 == 0:
                    P.copy(hx[:, cc * 514:cc * 514 + 1], hal[:, cc * 2:cc * 2 + 1])
                if c0 + N == S:
                    P.copy(hx[:, cc * 514 + N + 1:cc * 514 + N + 2], hal[:, cc * 2 + 1:cc * 2 + 2])
            P.dma(v3(gbt, 512), v3(pay[:, po["gB"]:po["gB"] + 2 * S], S)[:, :, c0:c0 + N])
        for cc in range(2):
            acc = lt[cc][:, :N]
            cw = lambda wi: par[:, o + 120 + cc * 3 + wi:o + 121 + cc * 3 + wi]
            P.ts(acc, hx[:, cc * 514:cc * 514 + N], cw(0), None, ALU.mult)
            P.stt(acc, hx[:, cc * 514 + 1:cc * 514 + N + 1], cw(1), acc, ALU.mult, ALU.add)
            P.stt(acc, hx[:, cc * 514 + 2:cc * 514 + N + 2], cw(2), acc, ALU.mult, ALU.add)
            P.tt(yc(2 + cc), acc, gbt[:, cc * N:(cc + 1) * N], ALU.mult)
        for dc in range(8):
            wb = wout[dc % 2]
            P.dma(wb, T[f"wout{l}b"][dc])
            by = bank(ps, 4 + dc % 2, N)
            for c in range(8):
                P.matmul(by, wb[:, c * 128:(c + 1) * 128], yc(c), start=(c == 0), stop=(c == 7))
            t = fb.t[dc % 2][:, :N]
            P.activation(t, by, AF.Identity, scale=mods[:, 40 + dc:41 + dc])
            P.stt(fb.z[:, dc * N:(dc + 1) * N], s1t[:, dc * N:(dc + 1) * N], alpha, t, ALU.mult, ALU.add)
        z = fb.z[:, :8 * N]
        emit_ln(P, ps, fb, z, N, par[:, o + 80:o + 88], par[:, o + 104:o + 112], z, ones)
        emit_ffn(P, ps, fb, z, N, mods[:, 48:72], T[f"wgu{l}1b"], T[f"wdn{l}1b"],
                 par[:, o + 88:o + 96], par[:, o + 112:o + 120], z, ones)
        P.dma(T[f"cout{l}"] if isctx else T[f"sout{l}"][ti], z, q="pool")


def host_inputs(cfg, inp):
    W = {}
    for l in range(cfg.DEPTH):
        W.update(host_weights(inp, l))
    maps = []
    x = np.asarray(inp["x"], np.float32)
    ctx = np.asarray(inp["ctx"], np.float32)
    for core in range(cfg.NCORES):
        b, j = core // cfg.CPB, core % cfg.CPB
        m = dict(W)
        m["params"] = host_params(inp, cfg, core)
        m.update(host_consts(cfg, core))
        xs = x[b, j * cfg.S:(j + 1) * cfg.S]
        m["xT"] = np.ascontiguousarray(xs.reshape(cfg.NT, 512, 8, 128).transpose(0, 3, 2, 1)).reshape(cfg.NT, 128, 4096)
        m["ctxT"] = np.ascontiguousarray(ctx[b].reshape(256, 8, 128).transpose(2, 1, 0)).reshape(128, 2048)
        maps.append(m)
    return maps


def assemble(cfg, outs):
    y = np.zeros((cfg.NB, cfg.SEQ, D), np.float32)
    for core in range(cfg.NCORES):
        b, j = core // cfg.CPB, core % cfg.CPB
        t = outs[core].reshape(cfg.NT, 128, 8, 512).transpose(0, 3, 2, 1).reshape(cfg.S, D)
        y[b, j * cfg.S:(j + 1) * cfg.S] = t
    return y


def p1_phase(l):
    return lambda P, c, T, al, ps: emit_p1(P, c, T, l, al, ps)


def p2_phase(l):
    return lambda P, c, T, al, ps: emit_p2(P, c, T, l, al, ps)


def run_unfused(cfg, inp, keep=None):
    maps = host_inputs(cfg, inp)
    cores = list(range(cfg.NCORES))
    NC = cfg.NCORES
    store = [dict(m) for m in maps]
    LD = cfg.DEPTH

    def launch(phases, ext_in, ext_out):
        nc = build_launch(cfg, phases, ext_in, ext_out)
        res = run_bass_kernel_spmd(nc, [{k: store[c][k] for k in ext_in} for c in cores], core_ids=cores)
        for c in cores:
            for k in ext_out:
                store[c][k] = res.results[c][k]

    def launch_m(phases, ext_in, ext_out, internal):
        nc = build_launch(cfg, phases, ext_in, ext_out, internal)
        res = run_bass_kernel_spmd(nc, [{k: store[c][k] for k in ext_in} for c in cores], core_ids=cores)
        for c in cores:
            for k in ext_out:
                store[c][k] = res.results[c][k]

    raw = [n for l in range(LD) for n in RAWW(l)]
    call = np.zeros((128, 8, 3), np.float32)
    for b in range(2):
        call[:, :, b] = np.asarray(inp["c"][b]).reshape(8, 128).T
    call[:, :, 2] = np.asarray(inp["c_ctx"]).reshape(8, 128).T
    call = call.reshape(128, 24)
    for c in cores:
        for n in raw:
            flat = store[c][n].reshape(NC, 128, -1)
            store[c][n + "s"] = np.ascontiguousarray(flat[c])
        for l in range(LD):
            store[c][f"wmod{l}s"] = np.ascontiguousarray(store[c][f"wmod{l}"][c * cfg.MPC:(c + 1) * cfg.MPC])
        store[c]["call"] = call
    launch([emit_p0], ["call"] + [n + "s" for n in raw] + [f"wmod{l}s" for l in range(LD)],
           [n + "bs" for n in raw] + ["modp"])
    sp = tensor_specs(cfg)
    for n in raw:
        full = np.stack([store[c][n + "bs"] for c in cores], axis=0).reshape(sp[n + "b"][0])
        for c in cores:
            store[c][n + "b"] = full
    modp = np.stack([store[c]["modp"].reshape(128, LD, cfg.MPC, 3) for c in cores], axis=2)
    modp = modp.reshape(128, LD, 72, 3)
    for c in cores:
        b = c // cfg.CPB
        mr = np.stack([modp[..., b], modp[..., 2]], axis=2)
        store[c]["modraw"] = np.ascontiguousarray(mr).reshape(128, LD * 144)
        for n in raw:
            store[c].pop(n, None)
            store[c].pop(n + "s", None)
    def p1_io(l):
        s_in = "xT" if l == 0 else f"sout{l - 1}"
        c_in = "ctxT" if l == 0 else f"cout{l - 1}"
        return ([s_in, c_in, "params", "modraw", "rope", f"wgu{l}0b", f"wdn{l}0b", f"win{l}b", f"winv{l}b"],
                [f"s1_{l}", f"c1_{l}", f"q{l}", f"qc{l}", f"pay{l}", f"payc{l}"])

    def p2_io(l):
        return ([f"s1_{l}", f"c1_{l}", f"q{l}", f"qc{l}", f"pay{l}", f"payc{l}", f"gath{l}", "params", "modraw",
                 "wmask", f"wout{l}b", f"wgu{l}1b", f"wdn{l}1b"], [f"sout{l}"] + ([f"cout{l}"] if l < LD - 1 else []))

    def exchange(l):
        for b in range(cfg.NB):
            g = np.concatenate([store[b * cfg.CPB + j][f"pay{l}"] for j in range(cfg.CPB)], axis=0)
            for j in range(cfg.CPB):
                store[b * cfg.CPB + j][f"gath{l}"] = g

    launch([p1_phase(0)], *p1_io(0))
    exchange(0)
    for l in range(LD):
        i2, o2 = p2_io(l)
        if l + 1 < LD:
            i1, o1 = p1_io(l + 1)
            ins = list(dict.fromkeys(i2 + [n for n in i1 if n not in o2]))
            launch_m([p2_phase(l), p1_phase(l + 1)], ins, o1, o2)
            exchange(l + 1)
        else:
            launch([p2_phase(l)], i2, o2)
    if keep is not None:
        keep.extend(store)
    return assemble(cfg, [store[c][f"sout{LD - 1}"] for c in cores])


def kernel(**inputs):
    cfg = Cfg()
    return run_unfused(cfg, inputs)
```
